# Optimizing a Trainium2 kernel written in Bass

```python
import math
import jax, jax.numpy as jnp
from jax import lax
import numpy as np

D_MODEL = 2048
BATCH = 4
SEQ = 2048
DEPTH = 2
DEC_BATCH = 128
DEC_SEQ = 8
PAST_LEN = 16384
PAGE_SIZE = 128

N_META = 16
N_EVEN = (DEPTH + 1) // 2
N_ODD = DEPTH // 2
EPS = 1e-6

S5_WIDTH = D_MODEL // 2
S5_GROUP = 16
S5_GROUPS = S5_WIDTH // S5_GROUP
S5_STATE = 64
HG_WIDTH = D_MODEL - S5_WIDTH
HG_HEADS = 8
HG_V = HG_WIDTH // HG_HEADS
HG_K = 128
HG_KEY = HG_HEADS * HG_K
HG_CHUNK = 16
EVEN_SPLITS = (S5_WIDTH, S5_WIDTH + HG_KEY, S5_WIDTH + 2 * HG_KEY, S5_WIDTH + 2 * HG_KEY + HG_WIDTH)
EVEN_IN = S5_WIDTH + 2 * HG_KEY + 2 * HG_WIDTH

RW_HEAD = 64
RW_HEADS = D_MODEL // RW_HEAD
W_LORA = max(32, int(round(1.8 * D_MODEL ** 0.5 / 32)) * 32)
A_LORA = max(32, int(round(1.8 * D_MODEL ** 0.5 / 32)) * 32)
G_LORA = max(32, int(round(0.6 * D_MODEL ** 0.8 / 32)) * 32)
RW_GN_EPS = 64e-5

D_FF = 11 * D_MODEL // 4
CONV_W = 3

kernel_name = 'hybrid_s5_hgrn2_rwkv7_convglu_step'


def _rmsnorm(x, w):
    xf = x.astype(jnp.float32)
    y = xf * lax.rsqrt(jnp.mean(xf * xf, axis=-1, keepdims=True) + EPS)
    return (y * w.astype(jnp.float32)).astype(x.dtype)


def _cplx_combine(e1, e2):
    a1r, a1i, b1r, b1i = e1
    a2r, a2i, b2r, b2i = e2
    return (a2r * a1r - a2i * a1i, a2r * a1i + a2i * a1r,
            a2r * b1r - a2i * b1i + b2r, a2r * b1i + a2i * b1r + b2i)


def _s5(u, h0_re, h0_im, lam_re, lam_im, log_step, b_re, b_im, c_re, c_im, d, w_glu):
    f32 = jnp.float32
    bsz, seq, _ = u.shape
    uf = u.astype(f32).reshape(bsz, seq, S5_GROUPS, S5_GROUP)
    lr = jnp.minimum(lam_re.astype(f32), -1e-4)
    li = lam_im.astype(f32)
    dt = jnp.exp(log_step.astype(f32))[:, None]
    mag = jnp.exp(lr * dt)
    ar, ai = mag * jnp.cos(li * dt), mag * jnp.sin(li * dt)
    den = lr * lr + li * li
    zr = ((ar - 1.0) * lr + ai * li) / den
    zi = (ai * lr - (ar - 1.0) * li) / den
    br, bi = b_re.astype(f32), b_im.astype(f32)
    bbr = zr[..., None] * br - zi[..., None] * bi
    bbi = zr[..., None] * bi + zi[..., None] * br
    bur = jnp.einsum('blgc,gpc->blgp', uf, bbr)
    bui = jnp.einsum('blgc,gpc->blgp', uf, bbi)
    h0r, h0i = h0_re.astype(f32), h0_im.astype(f32)
    bur = bur.at[:, 0].add(ar * h0r - ai * h0i)
    bui = bui.at[:, 0].add(ar * h0i + ai * h0r)
    shape = bur.shape
    _, _, xr, xi = lax.associative_scan(
        _cplx_combine, (jnp.broadcast_to(ar, shape), jnp.broadcast_to(ai, shape), bur, bui), axis=1)
    y = (jnp.einsum('blgp,gcp->blgc', xr, c_re.astype(f32))
         - jnp.einsum('blgp,gcp->blgc', xi, c_im.astype(f32))
         + d.astype(f32) * uf)
    y = jax.nn.gelu(y.reshape(bsz, seq, S5_WIDTH))
    y = y * jax.nn.sigmoid(y @ w_glu.astype(f32))
    return y, xr[:, -1], xi[:, -1]


def _hgrn2(q, f, i, g, s0, lb, norm_w, front):
    f32 = jnp.float32
    bsz, seq, _ = q.shape
    lbf = lb.astype(f32)
    fg = lbf + (1.0 - lbf) * jax.nn.sigmoid(f.astype(f32))
    qh = jax.nn.silu(q.astype(f32)).reshape(bsz, seq, HG_HEADS, HG_K)
    kh = (1.0 - fg).reshape(bsz, seq, HG_HEADS, HG_K)
    lfh = jnp.log(fg).reshape(bsz, seq, HG_HEADS, HG_K)
    vh = i.astype(f32).reshape(bsz, seq, HG_HEADS, HG_V)
    back = (-(front + seq)) % HG_CHUNK
    n_chunks = (front + seq + back) // HG_CHUNK

    def chunks(t):
        t = jnp.pad(t, ((0, 0), (front, back), (0, 0), (0, 0)))
        t = t.reshape(bsz, n_chunks, HG_CHUNK, HG_HEADS, t.shape[-1])
        return t.transpose(1, 0, 3, 2, 4)

    qc, kc, lfc, vc = chunks(qh), chunks(kh), chunks(lfh), chunks(vh)
    bcum = jnp.cumsum(lfc, axis=3)
    btot = bcum[:, :, :, -1:, :]
    q_in = qc * jnp.exp(bcum)
    k_in = kc * jnp.exp(-bcum)
    k_end = kc * jnp.exp(btot - bcum)
    decay = jnp.exp(btot[:, :, :, 0, :])
    mask = jnp.tril(jnp.ones((HG_CHUNK, HG_CHUNK), f32))

    def step(S, xs):
        qi, ki, ke, vv, dc = xs
        att = jnp.einsum('bhtk,bhsk->bhts', qi, ki) * mask
        o = jnp.einsum('bhtk,bhkv->bhtv', qi, S) + jnp.einsum('bhts,bhsv->bhtv', att, vv)
        S = S * dc[..., None] + jnp.einsum('bhsk,bhsv->bhkv', ke, vv)
        return S, o

    s_fin, o = lax.scan(step, s0.astype(f32), (q_in, k_in, k_end, vc, decay))
    o = o.transpose(1, 0, 3, 2, 4).reshape(bsz, n_chunks * HG_CHUNK, HG_HEADS, HG_V)[:, front:front + seq]
    o = o * lax.rsqrt(jnp.mean(o * o, axis=-1, keepdims=True) + EPS) * norm_w.astype(f32)
    o = o.reshape(bsz, seq, HG_WIDTH) * jax.nn.silu(g.astype(f32))
    return o, s_fin


def _rwkv7(xn, shift0, s0, mu, w0, w1, w2, a0, a1, a2, g1, g2, k_k, k_a, r_k, w_r, w_k, w_v, w_o, ln_w, ln_b):
    f32 = jnp.float32
    bsz, seq, _ = xn.shape
    prev = jnp.concatenate([shift0[:, None].astype(xn.dtype), xn[:, :-1]], axis=1)
    xx = prev - xn
    xr, xw, xk, xv, xa, xg = (xn + xx * mu[j] for j in range(6))

    def heads(t):
        return t.astype(f32).reshape(bsz, seq, RW_HEADS, RW_HEAD)

    r = heads(xr @ w_r)
    wlog = -jax.nn.softplus(-(w0.astype(f32) + (jnp.tanh(xw @ w1) @ w2).astype(f32))) - 0.5
    decay = heads(jnp.exp(-jnp.exp(wlog)))
    k = heads(xk @ w_k)
    v = heads(xv @ w_v)
    a = heads(jax.nn.sigmoid(a0.astype(f32) + ((xa @ a1) @ a2).astype(f32)))
    g = (jax.nn.sigmoid(xg @ g1) @ g2).astype(f32)
    kk = k * k_k.astype(f32).reshape(RW_HEADS, RW_HEAD)
    kk = kk / jnp.maximum(jnp.sqrt(jnp.sum(kk * kk, axis=-1, keepdims=True)), 1e-12)
    k = k * (1.0 + (a - 1.0) * k_a.astype(f32).reshape(RW_HEADS, RW_HEAD))

    def tm(t):
        return jnp.swapaxes(t, 0, 1)

    def step(S, xs):
        rt, wt, kt, vt, at, bt = xs
        sa = jnp.einsum('bhvk,bhk->bhv', S, at)
        S = S * wt[:, :, None, :] + sa[..., None] * bt[:, :, None, :] + vt[..., None] * kt[:, :, None, :]
        return S, jnp.einsum('bhvk,bhk->bhv', S, rt)

    s_fin, y = lax.scan(step, s0.astype(f32), (tm(r), tm(decay), tm(k), tm(v), tm(-kk), tm(kk * a)))
    y = tm(y)
    mean = jnp.mean(y, axis=-1, keepdims=True)
    var = jnp.mean(jnp.square(y - mean), axis=-1, keepdims=True)
    y = ((y - mean) * lax.rsqrt(var + RW_GN_EPS) * ln_w.astype(f32).reshape(RW_HEADS, RW_HEAD)
         + ln_b.astype(f32).reshape(RW_HEADS, RW_HEAD))
    y = y + jnp.sum(r * k * r_k.astype(f32), axis=-1, keepdims=True) * v
    out = (y.reshape(bsz, seq, D_MODEL) * g).astype(xn.dtype) @ w_o
    return out, s_fin, xn[:, -1]


def _conv_ffn(xn, conv0, w_in, conv_w, conv_b, w_down):
    seq = xn.shape[1]
    h = xn @ w_in
    a, v = h[..., :D_FF], h[..., D_FF:]
    full = jnp.concatenate([conv0.astype(a.dtype), a], axis=1)
    c = conv_b + conv_w[0] * full[:, 0:seq]
    for j in range(1, CONV_W):
        c = c + conv_w[j] * full[:, j:j + seq]
    out = (jax.nn.gelu(c) * v) @ w_down
    return out, full[:, seq:]


def setup_inputs(seed: int = 0) -> dict:
    keys = jax.random.split(jax.random.key(seed), 64)
    it = iter(range(64))

    def nrm(shape, scale=1.0):
        return jax.random.normal(keys[next(it)], shape, jnp.float32) * scale

    def uni(shape, lo, hi):
        return jax.random.uniform(keys[next(it)], shape, jnp.float32, lo, hi)

    D = D_MODEL
    ne, no = N_EVEN, N_ODD
    ramp = jnp.arange(D, dtype=jnp.float32) / (D - 1)
    return {
        'x_prompt': nrm((BATCH, SEQ, D)),
        'x_sample': nrm((DEC_BATCH, DEC_SEQ, D)),
        'state_s5_re': nrm((ne, DEC_BATCH, S5_GROUPS, S5_STATE), 0.5),
        'state_s5_im': nrm((ne, DEC_BATCH, S5_GROUPS, S5_STATE), 0.5),
        'state_hgrn': nrm((ne, DEC_BATCH, HG_HEADS, HG_K, HG_V), 0.5),
        'state_rwkv': nrm((no, DEC_BATCH, RW_HEADS, RW_HEAD, RW_HEAD), 0.2),
        'state_shift': nrm((no, DEC_BATCH, D)),
        'state_conv': nrm((DEPTH, DEC_BATCH, CONV_W - 1, D_FF)),
        'meta_tokens': nrm((N_META, D)),
        'ln_mix': 1.0 + nrm((DEPTH, D), 0.02),
        'ln_ffn': 1.0 + nrm((DEPTH, D), 0.02),
        'ln_final': 1.0 + nrm((D,), 0.02),
        'ev_w_in': nrm((ne, D, EVEN_IN), D ** -0.5),
        'ev_w_out': nrm((ne, D, D), D ** -0.5),
        's5_lam_re': -0.5 + nrm((ne, S5_GROUPS, S5_STATE), 0.01),
        's5_lam_im': jnp.pi * jnp.arange(S5_STATE, dtype=jnp.float32) + nrm((ne, S5_GROUPS, S5_STATE), 0.01),
        's5_log_step': uni((ne, S5_GROUPS), math.log(1e-3), math.log(1e-1)),
        's5_b_re': nrm((ne, S5_GROUPS, S5_STATE, S5_GROUP), (2 * S5_GROUP) ** -0.5),
        's5_b_im': nrm((ne, S5_GROUPS, S5_STATE, S5_GROUP), (2 * S5_GROUP) ** -0.5),
        's5_c_re': nrm((ne, S5_GROUPS, S5_GROUP, S5_STATE), S5_STATE ** -0.5),
        's5_c_im': nrm((ne, S5_GROUPS, S5_GROUP, S5_STATE), S5_STATE ** -0.5),
        's5_d': nrm((ne, S5_GROUPS, S5_GROUP)),
        's5_w_glu': nrm((ne, S5_WIDTH, S5_WIDTH), S5_WIDTH ** -0.5),
        'hg_lb': nrm((ne + 1, HG_KEY), 0.1),
        'hg_norm_w': 1.0 + nrm((ne, HG_V), 0.02),
        'rw_mu': uni((no, 6, D), 0.0, 1.0),
        'rw_w0': -5.5 + 5.0 * ramp ** 0.85 + nrm((no, D), 0.01),
        'rw_w1': nrm((no, D, W_LORA), D ** -0.5),
        'rw_w2': nrm((no, W_LORA, D), 0.1 * W_LORA ** -0.5),
        'rw_a0': nrm((no, D), 0.1),
        'rw_a1': nrm((no, D, A_LORA), D ** -0.5),
        'rw_a2': nrm((no, A_LORA, D), A_LORA ** -0.5),
        'rw_g1': nrm((no, D, G_LORA), D ** -0.5),
        'rw_g2': nrm((no, G_LORA, D), G_LORA ** -0.5),
        'rw_k_k': 0.85 + nrm((no, D), 0.02),
        'rw_k_a': 1.0 + nrm((no, D), 0.02),
        'rw_r_k': -0.04 + nrm((no, RW_HEADS, RW_HEAD), 0.1),
        'rw_w_r': nrm((no, D, D), D ** -0.5),
        'rw_w_k': nrm((no, D, D), D ** -0.5),
        'rw_w_v': nrm((no, D, D), D ** -0.5),
        'rw_w_o': nrm((no, D, D), D ** -0.5),
        'rw_ln_w': 1.0 + nrm((no, D), 0.02),
        'rw_ln_b': nrm((no, D), 0.02),
        'ffn_w_in': nrm((DEPTH, D, 2 * D_FF), D ** -0.5),
        'ffn_conv_w': nrm((DEPTH, CONV_W, D_FF), 0.5),
        'ffn_conv_b': nrm((DEPTH, D_FF), 0.02),
        'ffn_w_down': nrm((DEPTH, D_FF, D), D_FF ** -0.5),
    }


def reference(x_prompt, x_sample, state_s5_re, state_s5_im, state_hgrn, state_rwkv, state_shift, state_conv,
              meta_tokens, ln_mix, ln_ffn, ln_final, ev_w_in, ev_w_out,
              s5_lam_re, s5_lam_im, s5_log_step, s5_b_re, s5_b_im, s5_c_re, s5_c_im, s5_d, s5_w_glu,
              hg_lb, hg_norm_w,
              rw_mu, rw_w0, rw_w1, rw_w2, rw_a0, rw_a1, rw_a2, rw_g1, rw_g2, rw_k_k, rw_k_a, rw_r_k,
              rw_w_r, rw_w_k, rw_w_v, rw_w_o, rw_ln_w, rw_ln_b,
              ffn_w_in, ffn_conv_w, ffn_conv_b, ffn_w_down):
    lb_all = jnp.cumsum(jax.nn.softmax(hg_lb.astype(jnp.float32), axis=0), axis=0)

    def trunk(h, s5r, s5i, hg, rw, sh, cv, front):
        n_s5r, n_s5i, n_hg, n_rw, n_sh, n_cv = [], [], [], [], [], []
        for l in range(DEPTH):
            xn = _rmsnorm(h, ln_mix[l])
            if l % 2 == 0:
                e = l // 2
                z = xn @ ev_w_in[e]
                u, q, f, i, g = jnp.split(z, EVEN_SPLITS, axis=-1)
                ya, nr, ni = _s5(u, s5r[e], s5i[e], s5_lam_re[e], s5_lam_im[e], s5_log_step[e],
                                 s5_b_re[e], s5_b_im[e], s5_c_re[e], s5_c_im[e], s5_d[e], s5_w_glu[e])
                yb, nh = _hgrn2(q, f, i, g, hg[e], lb_all[e], hg_norm_w[e], front)
                mix = jnp.concatenate([ya, yb], axis=-1).astype(h.dtype) @ ev_w_out[e]
                n_s5r.append(nr)
                n_s5i.append(ni)
                n_hg.append(nh)
            else:
                o = l // 2
                mix, ns, nshift = _rwkv7(xn, sh[o], rw[o], rw_mu[o], rw_w0[o], rw_w1[o], rw_w2[o],
                                         rw_a0[o], rw_a1[o], rw_a2[o], rw_g1[o], rw_g2[o],
                                         rw_k_k[o], rw_k_a[o], rw_r_k[o], rw_w_r[o], rw_w_k[o],
                                         rw_w_v[o], rw_w_o[o], rw_ln_w[o], rw_ln_b[o])
                n_rw.append(ns)
                n_sh.append(nshift)
            h = h + mix
            xn = _rmsnorm(h, ln_ffn[l])
            ff, nc = _conv_ffn(xn, cv[l], ffn_w_in[l], ffn_conv_w[l], ffn_conv_b[l], ffn_w_down[l])
            n_cv.append(nc)
            h = h + ff
        return (_rmsnorm(h, ln_final), jnp.stack(n_s5r), jnp.stack(n_s5i), jnp.stack(n_hg),
                jnp.stack(n_rw), jnp.stack(n_sh), jnp.stack(n_cv))

    bsz = x_prompt.shape[0]
    f32 = jnp.float32
    hp = jnp.concatenate([jnp.broadcast_to(meta_tokens.astype(x_prompt.dtype)[None], (bsz, N_META, D_MODEL)),
                          x_prompt], axis=1)
    yp, p_s5r, p_s5i, p_hg, p_rw, p_sh, p_cv = trunk(
        hp,
        jnp.zeros((N_EVEN, bsz, S5_GROUPS, S5_STATE), f32),
        jnp.zeros((N_EVEN, bsz, S5_GROUPS, S5_STATE), f32),
        jnp.zeros((N_EVEN, bsz, HG_HEADS, HG_K, HG_V), f32),
        jnp.zeros((N_ODD, bsz, RW_HEADS, RW_HEAD, RW_HEAD), f32),
        jnp.zeros((N_ODD, bsz, D_MODEL), x_prompt.dtype),
        jnp.zeros((DEPTH, bsz, CONV_W - 1, D_FF), x_prompt.dtype),
        (-N_META) % HG_CHUNK)
    y_prompt = yp[:, N_META:]

    y_sample, s_s5r, s_s5i, s_hg, s_rw, s_sh, s_cv = trunk(
        x_sample, state_s5_re, state_s5_im, state_hgrn, state_rwkv, state_shift, state_conv,
        (PAST_LEN + (-N_META) % HG_CHUNK) % HG_CHUNK)

    return (y_prompt, y_sample, p_s5r, p_s5i, p_hg, p_rw, p_sh, p_cv, s_s5r, s_s5i, s_hg, s_rw, s_sh, s_cv)
```

```python
import math
from contextlib import ExitStack
import numpy as np
import concourse.bass as bass
import concourse.mybir as mybir
from concourse.bass_utils import run_bass_kernel_spmd

F32 = mybir.dt.float32
BF16 = mybir.dt.bfloat16
AF = mybir.ActivationFunctionType
ALU = mybir.AluOpType
AX = mybir.AxisListType

D = 2048
DC = 16
TP = 2064
NSEQ = 16
TS = 128
NT = TP + TS
SEGS = [(0, 512), (512, 512), (1024, 512), (1536, 512), (2048, 144)]
EVEN_IN = 5120
D_FF = 5632
FC = 44
EPS = 1e-6

COMPUTE = ("pe", "act", "dve", "pool")
NRING = 8


class Op:
    __slots__ = ("eng", "fn", "deps", "is_dma", "ring", "ring_val", "sig", "sigval", "idx")


class Prog:
    def __init__(self, nc):
        self.nc = nc
        self.ops = []
        self.last_w = {}
        self.readers = {}
        self.ring_pos = {"sp": 0, "act": 0, "pool": 0}
        self.ring_cnt = {}
        self.ring_last = {}
        self.out_dmas = []
        self.last_on = {}

    def _add(self, eng, fn, reads, writes, is_dma, extra_deps=()):
        op = Op()
        op.eng, op.fn, op.is_dma = eng, fn, is_dma
        op.idx = len(self.ops)
        op.sig = False
        op.sigval = None
        deps = set(extra_deps)
        for k in reads:
            w = self.last_w.get(k)
            if w is not None:
                deps.add(w)
        for k in writes:
            w = self.last_w.get(k)
            if w is not None:
                deps.add(w)
            for r in self.readers.get(k, ()):
                deps.add(r)
        if is_dma:
            q = eng
            slot = self.ring_pos[q] % NRING
            self.ring_pos[q] += 1
            key = (q, slot)
            prev = self.ring_last.get(key)
            if prev is not None:
                deps.add(prev)
            self.ring_cnt[key] = self.ring_cnt.get(key, 0) + 16
            op.ring = key
            op.ring_val = self.ring_cnt[key]
            self.ring_last[key] = op.idx
        else:
            op.ring = None
            op.ring_val = None
            if fn is not None:
                self.last_on[eng] = op.idx
        best = {}
        red = set()
        for dd in deps:
            dop = self.ops[dd]
            if dop.is_dma or dop.fn is None:
                red.add(dd)
            else:
                if best.get(dop.eng, -1) < dd:
                    best[dop.eng] = dd
        red.update(best.values())
        op.deps = red
        for k in writes:
            self.last_w[k] = op.idx
            self.readers[k] = []
        for k in reads:
            if k not in writes:
                self.readers.setdefault(k, []).append(op.idx)
        self.ops.append(op)
        return op

    def op(self, eng, fn, reads=(), writes=()):
        return self._add(eng, fn, tuple(reads), tuple(writes), False)

    def dma(self, q, out, in_, reads=(), writes=(), is_output=False, **kw):
        def fn(e):
            return e.dma_start(out=out, in_=in_, **kw)
        o = self._add(q, fn, tuple(reads), tuple(writes), True)
        if is_output:
            self.out_dmas.append(o.idx)
        return o

    def fence(self):
        deps = set(self.last_on.values()) | set(self.ring_last.values())
        for e in ("pe", "act", "dve", "pool", "sp"):
            self._add(e, None, (), (), False, extra_deps=deps)
        self.last_w = {}
        self.readers = {}

    def emit(self):
        nc = self.nc
        ops = self.ops
        for o in ops:
            for d in o.deps:
                dop = ops[d]
                if not dop.is_dma and dop.fn is not None:
                    if dop.eng == "pe" and o.eng == "pe" and not o.is_dma and o.fn is not None:
                        continue
                    dop.sig = True
        cnt = {e: 0 for e in COMPUTE}
        for o in ops:
            if not o.is_dma and o.sig:
                assert o.fn is not None
                cnt[o.eng] += 1
                o.sigval = cnt[o.eng]
        sems = {}
        for e in COMPUTE:
            sems[e] = nc.alloc_semaphore("c_" + e)
        for q in ("sp", "act", "pool"):
            for s in range(NRING):
                sems[(q, s)] = nc.alloc_semaphore("d_%s%d" % (q, s))
        streams = {e: [] for e in ("pe", "act", "dve", "pool", "sp")}
        for o in ops:
            streams[o.eng].append(o)
        final_waits = [(ops[i].ring, ops[i].ring_val) for i in self.out_dmas]

        def run(engname, e):
            waited = {}
            for o in streams[engname]:
                need = {}
                for d in o.deps:
                    dop = ops[d]
                    if dop.fn is None:
                        continue
                    if dop.is_dma:
                        k, v = dop.ring, dop.ring_val
                    else:
                        if dop.eng == "pe" and engname == "pe" and not o.is_dma and o.fn is not None:
                            continue
                        k, v = dop.eng, dop.sigval
                    if need.get(k, 0) < v:
                        need[k] = v
                for k, v in need.items():
                    if waited.get(k, 0) >= v:
                        continue
                    e.wait_ge(sems[k], v)
                    waited[k] = v
                if o.fn is None:
                    continue
                ins = o.fn(e)
                if o.is_dma:
                    ins.then_inc(sems[o.ring], 16)
                elif o.sig:
                    ins.then_inc(sems[o.eng], 1)
            if engname == "sp":
                for k, v in final_waits:
                    if waited.get(k, 0) >= v:
                        continue
                    e.wait_ge(sems[k], v)
                    waited[k] = v

        with nc.Block() as block:
            @block.sync
            def _(e):
                run("sp", e)

            @block.tensor
            def _(e):
                run("pe", e)

            @block.vector
            def _(e):
                run("dve", e)

            @block.scalar
            def _(e):
                run("act", e)

            @block.gpsimd
            def _(e):
                run("pool", e)
        return cnt


class BuilderBase:
    def __init__(self, upto="all", debug=()):
        self.nc = nc = bass.Bass("TRN2", target_bir_lowering=False)
        self.P = Prog(nc)
        self.upto = upto
        self.debug = set(debug)
        self.bank_i = 0
        self.din = {}
        self.dout = {}
        self.psbig = [nc.alloc_psum_tensor("psb%d" % i, [128, 1024], F32) for i in range(4)]
        self.ps = [self.psbig[i // 2][:, (i % 2) * 512:(i % 2) * 512 + 512] for i in range(8)]

    def inp(self, name, shape):
        t = self.nc.dram_tensor(name, list(shape), F32, kind="ExternalInput")
        self.din[name] = t
        return t

    def outp(self, name, shape):
        t = self.nc.dram_tensor(name, list(shape), F32, kind="ExternalOutput")
        self.dout[name] = t
        return t

    def scratch(self, name, shape, dt=F32):
        if name in getattr(self, "scr_in", ()):
            t = self.nc.dram_tensor(name, list(shape), dt, kind="ExternalInput")
            self.din[name] = t
            return t
        kind = "ExternalOutput" if name in self.debug else "Internal"
        t = self.nc.dram_tensor(name, list(shape), dt, kind=kind)
        if name in self.debug:
            self.dout[name] = t
        return t

    def bank(self):
        pool = getattr(self, "bank_pool", None) or list(range(8))
        b = pool[self.bank_i % len(pool)]
        self.bank_i += 1
        return b

    def sb(self, es, name, shape, dt=F32):
        self.uid = getattr(self, "uid", 0) + 1
        return es.enter_context(self.nc.sbuf_tensor("%s_%d" % (name, self.uid), list(shape), dt))

    def mm(self, out, lhsT, rhs, start, stop, r, w):
        self.P.op("pe", lambda e: e.matmul(out, lhsT=lhsT, rhs=rhs, start=start, stop=stop), r, w)

    def tr(self, out, in_, ident, r, w):
        self.P.op("pe", lambda e: e.transpose(out, in_, ident), r, w)

    def act(self, out, in_, func, r, w, bias=None, scale=None):
        kw = {}
        if bias is not None:
            kw["bias"] = bias
        if scale is not None:
            kw["scale"] = scale
        self.P.op("act", lambda e: e.activation(out=out, in_=in_, func=func, **kw), r, w)

    def tt(self, out, a, b, op, r, w, eng="dve"):
        self.P.op(eng, lambda e: e.tensor_tensor(out=out, in0=a, in1=b, op=op), r, w)

    def ts(self, out, a, s1, op0, r, w, s2=None, op1=None, eng="dve"):
        if op1 is None:
            self.P.op(eng, lambda e: e.tensor_scalar(out=out, in0=a, scalar1=s1, scalar2=None, op0=op0), r, w)
        else:
            self.P.op(eng, lambda e: e.tensor_scalar(out=out, in0=a, scalar1=s1, scalar2=s2, op0=op0, op1=op1), r, w)

    def stt(self, out, a, s, b, op0, op1, r, w):
        self.P.op("dve", lambda e: e.scalar_tensor_tensor(out=out, in0=a, scalar=s, in1=b, op0=op0, op1=op1), r, w)

    def cp(self, out, in_, r, w, eng="dve"):
        if eng == "act":
            self.P.op("act", lambda e: e.copy(out=out, in_=in_), r, w)
        else:
            self.P.op(eng, lambda e: e.tensor_copy(out=out, in_=in_), r, w)

    def memset(self, ap, v, w, eng="dve"):
        self.P.op(eng, lambda e: e.memset(ap, v), (), w)

    def scan(self, out, d0, d1, init, r, w):
        self.P.op("dve", lambda e: e.tensor_tensor_scan(out=out, data0=d0, data1=d1, initial=init, op0=ALU.mult, op1=ALU.add), r, w)

    def consts(self):
        nc, P = self.nc, self.P
        a = nc.alloc_sbuf_tensor
        self.ident = a("ident", [128, 128], F32)
        self.identb = a("identb", [128, 128], BF16)
        self.ones = a("ones", [128, 128], F32)
        self.bones = a("bones", [128, 128], F32)
        self.epsD = a("epsD", [128, 1], F32)
        self.m01 = a("m01", [128, 2], F32)
        self.tril = a("tril", [64, 64], F32)
        self.lcst = [a("lcst0", [128, 128], F32), a("lcst1", [128, 128], F32)]
        self.lc_i = 0
        P.op("pool", lambda e: e.memset(self.ident[:], 0.0), (), ["ident"])
        P.op("pool", lambda e: e.affine_select(out=self.ident[:], in_=self.ident[:], pattern=[[-1, 128]],
                                               compare_op=ALU.not_equal, fill=1.0, base=0, channel_multiplier=1),
             ["ident"], ["ident"])
        self.cp(self.identb[:], self.ident[:], ["ident"], ["identb"])
        self.memset(self.ones[:], 1.0, ["ones"])
        self.memset(self.epsD[:], EPS, ["epsD"])
        self.memset(self.bones[:], 0.0, ["bones"])
        self.memset(self.bones[0:64, 0:64], 1.0, ["bones"])
        self.memset(self.bones[64:128, 64:128], 1.0, ["bones"])
        self.memset(self.m01[:], 0.0, ["m01"])
        self.memset(self.m01[0:64, 0:1], 1.0, ["m01"])
        self.memset(self.m01[64:128, 1:2], 1.0, ["m01"])
        P.op("pool", lambda e: e.memset(self.tril[:], 1.0), (), ["tril"])
        P.op("pool", lambda e: e.affine_select(out=self.tril[:], in_=self.tril[:], pattern=[[1, 64]],
                                               compare_op=ALU.is_ge, fill=0.0, base=0, channel_multiplier=-1),
             ["tril"], ["tril"])

    def load_cols(self, dst, dram2d, R, key):
        P = self.P
        r0 = 0
        while r0 < R:
            n = min(128, R - r0)
            i = self.lc_i % 2
            self.lc_i += 1
            st = self.lcst[i]
            P.dma("sp", st[0:n, :], dram2d[r0:r0 + n, :], (), [("lcst", i)])
            b = self.bank()
            self.tr(self.ps[b][:, 0:n], st[0:n, :], self.ident[0:n, 0:n], [("lcst", i), "ident"], [("ps", b)])
            self.cp(dst[:, r0:r0 + n], self.ps[b][:, 0:n], [("ps", b)], [key])
            r0 += n

    def stage_in(self):
        P = self.P
        xp, meta, xs = self.din["xp"], self.din["meta"], self.din["xs"]
        hT = self.hT
        tiles = [(0, 128, [(meta.ap()[0:16, :], 0, 16), (xp.ap()[0:112, :], 16, 112)])]
        for i in range(1, 16):
            tiles.append((128 * i, 128, [(xp.ap()[128 * i - 16:128 * i + 112, :], 0, 128)]))
        tiles.append((2048, 16, [(xp.ap()[2032:2048, :], 0, 16)]))
        tiles.append((2064, 128, [(xs.ap()[:, :], 0, 128)]))
        with ExitStack() as es:
            tok = [self.sb(es, "tok%d" % i, [128, D], F32) for i in range(2)]
            hTt = [self.sb(es, "hTt%d" % i, [128, DC, 128], F32) for i in range(2)]
            for ti, (t0, n, srcs) in enumerate(tiles):
                tk = tok[ti % 2]
                ht = hTt[ti % 2]
                for (src, r0, nr) in srcs:
                    P.dma("sp", tk[r0:r0 + nr, :], src, (), [("tok", ti % 2)])
                for g in range(4):
                    b = self.bank()
                    for c in range(4):
                        cc = g * 4 + c
                        self.tr(self.ps[b][:, c * 128:c * 128 + n], tk[0:n, cc * 128:(cc + 1) * 128],
                                self.ident[0:n, 0:n], [("tok", ti % 2), "ident"], [("ps", b)])
                    src = self.ps[b][:, :].rearrange("p (c t) -> p c t", c=4)[:, :, 0:n]
                    self.cp(ht[:, g * 4:g * 4 + 4, 0:n], src, [("ps", b)], [("hTt", ti % 2)],
                            eng=("act" if g % 2 else "dve"))
                P.dma("sp", hT.ap()[:, :, t0:t0 + n].rearrange("c p t -> p c t"), ht[:, :, 0:n],
                      [("hTt", ti % 2)], ["hT"])
            P.fence()

    def norm(self, xnT, lnw, out32=None):
        P = self.P
        hT = self.hT
        with ExitStack() as es:
            hseg = [self.sb(es, "hseg%d" % i, [128, DC, 512], F32) for i in range(2)]
            sq = [self.sb(es, "sq%d" % i, [128, 512], F32) for i in range(2)]
            rs = [self.sb(es, "rs%d" % i, [128, 512], F32) for i in range(2)]
            for si, (t0, n) in enumerate(SEGS):
                hs = hseg[si % 2]
                hk = ("hseg", si % 2)
                P.dma("sp", hs[:, :, 0:n], hT.ap()[:, :, t0:t0 + n].rearrange("c p t -> p c t"), ["hT"], [hk])
                b = self.bank()
                for c in range(DC):
                    q = sq[c % 2]
                    self.act(q[:, 0:n], hs[:, c, 0:n], AF.Square, [hk], [("sq", c % 2)])
                    self.mm(self.ps[b][:, 0:n], self.ones[:, :], q[:, 0:n], c == 0, c == DC - 1,
                            [("sq", c % 2), "ones"], [("ps", b)])
                r = rs[si % 2]
                rk = ("rs", si % 2)
                self.act(r[:, 0:n], self.ps[b][:, 0:n], AF.Sqrt, [("ps", b), "epsD"], [rk],
                         bias=self.epsD[:, 0:1], scale=1.0 / D)
                self.P.op("dve", lambda e, r=r, n=n: e.reciprocal(out=r[:, 0:n], in_=r[:, 0:n]), [rk], [rk])
                for c in range(DC):
                    self.stt(xnT[:, c, t0:t0 + n], hs[:, c, 0:n], lnw[:, c:c + 1], r[:, 0:n], ALU.mult, ALU.mult,
                             [hk, rk, "cols"], [("xnT", c, si)])
                    if out32 is not None and si == len(SEGS) - 1:
                        self.stt(out32[:, c, 0:n], hs[:, c, 0:n], lnw[:, c:c + 1], r[:, 0:n], ALU.mult, ALU.mult,
                                 [hk, rk, "cols"], ["out32"])
            P.fence()

    def proj(self, es, xT, xkey, KT, w2d, col0, ncols, evac, wg=4, kparts=128, tag="w"):
        P = self.P
        ntile = (ncols + 127) // 128
        nbuf = 2
        wb = [self.sb(es, "%sb%d" % (tag, i), [128, KT, wg * 128], BF16) for i in range(nbuf)]
        gi = 0
        for j0 in range(0, ntile, wg):
            nj = min(wg, ntile - j0)
            w = wb[gi % nbuf]
            wk = (tag, gi % nbuf)
            gi += 1
            c_lo = j0 * 128
            c_hi = min(ncols, (j0 + nj) * 128)
            src = w2d[:, col0 + c_lo:col0 + c_hi].rearrange("(kt p) n -> p kt n", p=kparts)
            P.dma("pool", w[0:kparts, :, 0:c_hi - c_lo], src, (), [wk])
            for jj in range(nj):
                mw = min(128, ncols - (j0 + jj) * 128)
                for si, (t0, n) in enumerate(SEGS):
                    b = self.bank()
                    for kt in range(KT):
                        rhs = xT(kt, t0, n) if callable(xT) else xT[0:kparts, kt, t0:t0 + n]
                        self.mm(self.ps[b][0:mw, 0:n], w[0:kparts, kt, jj * 128:jj * 128 + mw],
                                rhs, kt == 0, kt == KT - 1, [wk, xkey(kt, si)], [("ps", b)])
                    evac(j0 + jj, si, t0, n, self.ps[b][0:mw, 0:n], ("ps", b))

    def declare(self):
        i = self.inp
        i("xp", [2048, D]); i("meta", [16, D]); i("xs", [TS, D])
        i("st_s5r", [NSEQ, 64, 64]); i("st_s5i", [NSEQ, 64, 64])
        i("st_hg", [NSEQ, 8, 128, 128]); i("st_rw", [NSEQ, 32, 64, 64])
        i("st_sh", [NSEQ, D]); i("st_cv", [2, NSEQ, 2, D_FF])
        i("ln_mix", [2, D]); i("ln_ffn", [2, D]); i("ln_final", [1, D])
        i("ev_w_in", [D, EVEN_IN]); i("ev_w_out", [D, D])
        i("s5_lam_re", [64, 64]); i("s5_lam_im", [64, 64]); i("s5_log_step", [1, 64])
        i("s5_b_re", [64, 64, 16]); i("s5_b_im", [64, 64, 16])
        i("s5_c_re", [64, 16, 64]); i("s5_c_im", [64, 16, 64]); i("s5_d", [64, 16])
        i("s5_w_glu", [1024, 1024]); i("hg_lb", [2, 1024]); i("hg_norm_w", [1, 128])
        i("rw_mu", [6, D]); i("rw_w0", [1, D]); i("rw_w1", [D, 96]); i("rw_w2", [96, D])
        i("rw_a0", [1, D]); i("rw_a1", [D, 96]); i("rw_a2", [96, D])
        i("rw_g1", [D, 256]); i("rw_g2", [256, D])
        i("rw_k_k", [1, D]); i("rw_k_a", [1, D]); i("rw_r_k", [1, D])
        i("rw_w_r", [D, D]); i("rw_w_k", [D, D]); i("rw_w_v", [D, D]); i("rw_w_o", [D, D])
        i("rw_ln_w", [1, D]); i("rw_ln_b", [1, D])
        i("ffn_w_in", [2, D, 2 * D_FF]); i("ffn_conv_w", [2, 3, D_FF]); i("ffn_conv_b", [2, D_FF])
        i("ffn_w_down", [2, D_FF, D])
        o = self.outp
        o("y_p", [2048, D]); o("y_s", [TS, D])
        o("p_s5r", [64, 64]); o("p_s5i", [64, 64]); o("p_hg", [8, 128, 128]); o("p_rw", [32, 64, 64])
        o("p_sh", [1, D]); o("p_cv", [2, 2, D_FF])
        o("s_s5r", [NSEQ, 64, 64]); o("s_s5i", [NSEQ, 64, 64]); o("s_hg", [NSEQ, 8, 128, 128])
        o("s_rw", [NSEQ, 32, 64, 64]); o("s_sh", [NSEQ, D]); o("s_cv", [2, NSEQ, 2, D_FF])
        self.hT = self.scratch("hT", [DC, 128, NT])
        self.zT = self.scratch("zT", [40, 128, NT])
        self.gS = self.scratch("gS", [len(SEGS), 128, FC, 512], BF16)

    def param_cols(self):
        a = self.nc.alloc_sbuf_tensor
        d = self.din
        self.c_ln = a("c_ln", [128, 5 * DC], F32)
        self.load_cols(self.c_ln[:, 0:32], d["ln_mix"].ap().rearrange("l (c p) -> (l c) p", p=128), 32, "cols")
        self.load_cols(self.c_ln[:, 32:64], d["ln_ffn"].ap().rearrange("l (c p) -> (l c) p", p=128), 32, "cols")
        self.load_cols(self.c_ln[:, 64:80], d["ln_final"].ap().rearrange("l (c p) -> (l c) p", p=128), 16, "cols")

    def l0_inproj(self, xnT):
        P = self.P
        with ExitStack() as es:
            zrow = [self.sb(es, "zrow%d" % i, [128, NT], F32) for i in range(2)]

            def evac(j, si, t0, n, ps, pk):
                z = zrow[j % 2]
                self.cp(z[:, t0:t0 + n], ps, [pk], [("zrow", j % 2, si)], eng=("act" if si % 2 else "dve"))
                if si == len(SEGS) - 1:
                    P.dma("sp", self.zT.ap()[j], z[:, :], [("zrow", j % 2, s) for s in range(len(SEGS))], ["zT"])

            self.proj(es, xnT, lambda kt, si: ("xnT", kt, si), DC, self.din["ev_w_in"].ap(), 0, EVEN_IN, evac)
            P.fence()


def _prep_inputs(inputs, core):
    b = core % 4
    f = lambda a: np.ascontiguousarray(np.asarray(a, dtype=np.float32))
    sl = slice(NSEQ * core, NSEQ * core + NSEQ)
    m = {
        "xp": f(inputs["x_prompt"][b]), "meta": f(inputs["meta_tokens"]),
        "xs": f(np.asarray(inputs["x_sample"])[sl].reshape(TS, D)),
        "st_s5r": f(np.asarray(inputs["state_s5_re"])[0, sl]), "st_s5i": f(np.asarray(inputs["state_s5_im"])[0, sl]),
        "st_hg": f(np.asarray(inputs["state_hgrn"])[0, sl]), "st_rw": f(np.asarray(inputs["state_rwkv"])[0, sl]),
        "st_sh": f(np.asarray(inputs["state_shift"])[0, sl]), "st_cv": f(np.asarray(inputs["state_conv"])[:, sl]),
        "ln_mix": f(inputs["ln_mix"]), "ln_ffn": f(inputs["ln_ffn"]), "ln_final": f(np.asarray(inputs["ln_final"]).reshape(1, D)),
        "ev_w_in": f(inputs["ev_w_in"][0]), "ev_w_out": f(inputs["ev_w_out"][0]),
        "s5_lam_re": f(inputs["s5_lam_re"][0]), "s5_lam_im": f(inputs["s5_lam_im"][0]),
        "s5_log_step": f(np.asarray(inputs["s5_log_step"]).reshape(1, 64)),
        "s5_b_re": f(inputs["s5_b_re"][0]), "s5_b_im": f(inputs["s5_b_im"][0]),
        "s5_c_re": f(inputs["s5_c_re"][0]), "s5_c_im": f(inputs["s5_c_im"][0]), "s5_d": f(inputs["s5_d"][0]),
        "s5_w_glu": f(inputs["s5_w_glu"][0]), "hg_lb": f(inputs["hg_lb"]), "hg_norm_w": f(inputs["hg_norm_w"]),
        "rw_mu": f(inputs["rw_mu"][0]), "rw_w0": f(inputs["rw_w0"]), "rw_w1": f(inputs["rw_w1"][0]),
        "rw_w2": f(inputs["rw_w2"][0]), "rw_a0": f(inputs["rw_a0"]), "rw_a1": f(inputs["rw_a1"][0]),
        "rw_a2": f(inputs["rw_a2"][0]), "rw_g1": f(inputs["rw_g1"][0]), "rw_g2": f(inputs["rw_g2"][0]),
        "rw_k_k": f(inputs["rw_k_k"]), "rw_k_a": f(inputs["rw_k_a"]),
        "rw_r_k": f(np.asarray(inputs["rw_r_k"]).reshape(1, D)),
        "rw_w_r": f(inputs["rw_w_r"][0]), "rw_w_k": f(inputs["rw_w_k"][0]), "rw_w_v": f(inputs["rw_w_v"][0]),
        "rw_w_o": f(inputs["rw_w_o"][0]), "rw_ln_w": f(inputs["rw_ln_w"]), "rw_ln_b": f(inputs["rw_ln_b"]),
        "ffn_w_in": f(inputs["ffn_w_in"]), "ffn_conv_w": f(inputs["ffn_conv_w"]), "ffn_conv_b": f(inputs["ffn_conv_b"]),
        "ffn_w_down": f(inputs["ffn_w_down"]),
    }
    return m


TWO_PI = 2.0 * math.pi
C1_2PI = 6.28125
C2_2PI = TWO_PI - C1_2PI
MAGIC = 12582912.0


class S5Mixin:
    def s5_setup(self):
        nc, P = self.nc, self.P
        self.es_s5 = ExitStack()
        a = lambda n, shp, dt: self.sb(self.es_s5, n, shp, dt)
        d = self.din
        K = "s5c"
        self.s5_mag = a("s5_mag", [128, 32], F32)
        self.s5_cth = a("s5_cth", [128, 32], F32)
        self.s5_sth = a("s5_sth", [128, 32], F32)
        self.s5_Bre = a("s5_Bre", [128, 32, 128], BF16)
        self.s5_Bim = a("s5_Bim", [128, 32, 128], BF16)
        self.s5_Cre = a("s5_Cre", [128, 32, 128], BF16)
        self.s5_Cim = a("s5_Cim", [128, 32, 128], BF16)
        self.s5_dD = a("s5_dD", [128, 8, 128], BF16)
        self.s5_ahr = a("s5_ahr", [128, NSEQ, 32], F32)
        self.s5_ahi = a("s5_ahi", [128, NSEQ, 32], F32)
        with ExitStack() as es:
            T = lambda n, shp=[128, 32]: self.sb(es, n, shp, F32)
            lr, li, ls, dt, th, k, r, r2, msk = (T(n) for n in ("s_lr", "s_li", "s_ls", "s_dt", "s_th", "s_k", "s_r", "s_r2", "s_msk"))
            ar, ai, am1, den, zr, zi, t1, t2 = (T(n) for n in ("s_ar", "s_ai", "s_am1", "s_den", "s_zr", "s_zi", "s_t1", "s_t2"))
            rowv = lambda t: t.ap().rearrange("(i gl) p -> i (gl p)", gl=2)
            self.load_cols(lr[:, :], rowv(d["s5_lam_re"]), 32, K)
            self.load_cols(li[:, :], rowv(d["s5_lam_im"]), 32, K)
            st = self.lcst[0]
            P.dma("sp", st[0:1, 0:64], d["s5_log_step"].ap(), (), [("lcst", 0)])
            b = self.bank()
            self.mm(self.ps[b][:, 0:64], self.ones[0:1, :], st[0:1, 0:64], True, True, [("lcst", 0), "ones"], [("ps", b)])
            self.ts(t1[:, :], self.ps[b][:, 0:64:2], self.m01[:, 0:1], ALU.mult, [("ps", b), "m01"], [K])
            self.stt(ls[:, :], self.ps[b][:, 1:64:2], self.m01[:, 1:2], t1[:, :], ALU.mult, ALU.add, [("ps", b), "m01", K], [K])
            R_, W_ = [K], [K]
            self.ts(lr[:, :], lr[:, :], -1e-4, ALU.min, R_, W_)
            self.act(dt[:, :], ls[:, :], AF.Exp, R_, W_)
            self.tt(t1[:, :], lr[:, :], dt[:, :], ALU.mult, R_, W_)
            self.act(self.s5_mag[:, :], t1[:, :], AF.Exp, R_, W_)
            self.tt(th[:, :], li[:, :], dt[:, :], ALU.mult, R_, W_)
            self.ts(k[:, :], th[:, :], 1.0 / TWO_PI, ALU.mult, R_, W_)
            self.ts(k[:, :], k[:, :], MAGIC, ALU.add, R_, W_)
            self.ts(k[:, :], k[:, :], -MAGIC, ALU.add, R_, W_)
            self.stt(r[:, :], k[:, :], -C1_2PI, th[:, :], ALU.mult, ALU.add, R_, W_)
            self.stt(r[:, :], k[:, :], -C2_2PI, r[:, :], ALU.mult, ALU.add, R_, W_)
            self.ts(r[:, :], r[:, :], math.pi, ALU.min, R_, W_, s2=-math.pi, op1=ALU.max)
            self.act(self.s5_sth[:, :], r[:, :], AF.Sin, R_, W_)
            self.ts(r2[:, :], r[:, :], math.pi / 2, ALU.add, R_, W_)
            self.ts(msk[:, :], r2[:, :], math.pi, ALU.is_gt, R_, W_)
            self.stt(r2[:, :], msk[:, :], -TWO_PI, r2[:, :], ALU.mult, ALU.add, R_, W_)
            self.ts(r2[:, :], r2[:, :], math.pi, ALU.min, R_, W_, s2=-math.pi, op1=ALU.max)
            self.act(self.s5_cth[:, :], r2[:, :], AF.Sin, R_, W_)
            self.tt(ar[:, :], self.s5_mag[:, :], self.s5_cth[:, :], ALU.mult, R_, W_)
            self.tt(ai[:, :], self.s5_mag[:, :], self.s5_sth[:, :], ALU.mult, R_, W_)
            self.ts(am1[:, :], ar[:, :], -1.0, ALU.add, R_, W_)
            self.tt(den[:, :], lr[:, :], lr[:, :], ALU.mult, R_, W_)
            self.tt(t1[:, :], li[:, :], li[:, :], ALU.mult, R_, W_)
            self.tt(den[:, :], den[:, :], t1[:, :], ALU.add, R_, W_)
            self.P.op("dve", lambda e: e.reciprocal(out=den[:, :], in_=den[:, :]), R_, W_)
            self.tt(t1[:, :], am1[:, :], lr[:, :], ALU.mult, R_, W_)
            self.tt(t2[:, :], ai[:, :], li[:, :], ALU.mult, R_, W_)
            self.tt(t1[:, :], t1[:, :], t2[:, :], ALU.add, R_, W_)
            self.tt(zr[:, :], t1[:, :], den[:, :], ALU.mult, R_, W_)
            self.tt(t1[:, :], ai[:, :], lr[:, :], ALU.mult, R_, W_)
            self.tt(t2[:, :], am1[:, :], li[:, :], ALU.mult, R_, W_)
            self.tt(t1[:, :], t1[:, :], t2[:, :], ALU.subtract, R_, W_)
            self.tt(zi[:, :], t1[:, :], den[:, :], ALU.mult, R_, W_)
            Bre = self.sb(es, "s_Bre", [128, 32, 16], F32)
            Bim = self.sb(es, "s_Bim", [128, 32, 16], F32)
            bbr = self.sb(es, "s_bbr", [128, 32, 16], F32)
            bbi = self.sb(es, "s_bbi", [128, 32, 16], F32)
            tb = self.sb(es, "s_tb", [128, 32, 16], F32)
            for (dst, nm) in ((Bre, "s5_b_re"), (Bim, "s5_b_im")):
                P.dma("sp", dst[:, :, :], d[nm].ap().rearrange("g p c -> (g p) c").rearrange("(i q) c -> q i c", q=128),
                      (), [K])
            zrb = zr[:, :].unsqueeze(2).to_broadcast([128, 32, 16])
            zib = zi[:, :].unsqueeze(2).to_broadcast([128, 32, 16])
            self.tt(bbr[:, :, :], Bre[:, :, :], zrb, ALU.mult, R_, W_)
            self.tt(tb[:, :, :], Bim[:, :, :], zib, ALU.mult, R_, W_)
            self.tt(bbr[:, :, :], bbr[:, :, :], tb[:, :, :], ALU.subtract, R_, W_)
            self.tt(bbi[:, :, :], Bim[:, :, :], zrb, ALU.mult, R_, W_)
            self.tt(tb[:, :, :], Bre[:, :, :], zib, ALU.mult, R_, W_)
            self.tt(bbi[:, :, :], bbi[:, :, :], tb[:, :, :], ALU.add, R_, W_)
            stg = [self.sb(es, "s_stg%d" % i, [128, 128], F32) for i in range(4)]
            n_st = 0
            for i in range(32):
                gp = i % 4
                for (srcb, dstT) in ((bbr, self.s5_Bre), (bbi, self.s5_Bim)):
                    s = stg[n_st % 4]; sk = ("s_stg", n_st % 4); n_st += 1
                    self.memset(s[:, :], 0.0, [sk])
                    self.cp(s[0:64, 32 * gp:32 * gp + 16], srcb[0:64, i, :], [K, sk], [sk])
                    self.cp(s[64:128, 32 * gp + 16:32 * gp + 32], srcb[64:128, i, :], [K, sk], [sk])
                    b = self.bank()
                    self.tr(self.ps[b][:, 0:128], s[:, :], self.ident[:, :], [sk, "ident"], [("ps", b)])
                    self.cp(dstT[:, i, :], self.ps[b][:, 0:128], [("ps", b)], [K], eng="act")
                for (nm, dstT, neg) in (("s5_c_re", self.s5_Cre, False), ("s5_c_im", self.s5_Cim, True)):
                    s = stg[n_st % 4]; sk = ("s_stg", n_st % 4); n_st += 1
                    self.memset(s[:, :], 0.0, [sk])
                    P.dma("sp", s[32 * gp:32 * gp + 16, 0:64], d[nm].ap()[2 * i], [sk], [sk])
                    P.dma("sp", s[32 * gp + 16:32 * gp + 32, 64:128], d[nm].ap()[2 * i + 1], [sk], [sk])
                    b = self.bank()
                    self.tr(self.ps[b][:, 0:128], s[:, :], self.ident[:, :], [sk, "ident"], [("ps", b)])
                    if neg:
                        self.ts(dstT[:, i, :], self.ps[b][:, 0:128], -1.0, ALU.mult, [("ps", b)], [K])
                    else:
                        self.cp(dstT[:, i, :], self.ps[b][:, 0:128], [("ps", b)], [K], eng="act")
            dcol = self.sb(es, "s_dcol", [128, 8], F32)
            self.load_cols(dcol[:, :], d["s5_d"].ap().rearrange("(m g) c -> m (g c)", g=8), 8, K)
            for m in range(8):
                self.ts(self.s5_dD[:, m, :], self.ident[:, :], dcol[:, m:m + 1], ALU.mult, [K, "ident"], [K])
            h0r = self.sb(es, "s_h0r", [128, NSEQ, 32], F32)
            h0i = self.sb(es, "s_h0i", [128, NSEQ, 32], F32)
            th0 = self.sb(es, "s_th0", [128, NSEQ, 32], F32)
            sv = lambda t: t.ap().rearrange("s (i gl) p -> (s i) (gl p)", gl=2)
            self.load_cols(h0r[:, :, :].rearrange("q s i -> q (s i)"), sv(d["st_s5r"]), NSEQ * 32, K)
            self.load_cols(h0i[:, :, :].rearrange("q s i -> q (s i)"), sv(d["st_s5i"]), NSEQ * 32, K)
            arb = ar[:, :].unsqueeze(1).to_broadcast([128, NSEQ, 32])
            aib = ai[:, :].unsqueeze(1).to_broadcast([128, NSEQ, 32])
            self.tt(self.s5_ahr[:, :, :], h0r[:, :, :], arb, ALU.mult, R_, W_)
            self.tt(th0[:, :, :], h0i[:, :, :], aib, ALU.mult, R_, W_)
            self.tt(self.s5_ahr[:, :, :], self.s5_ahr[:, :, :], th0[:, :, :], ALU.subtract, R_, W_)
            self.tt(self.s5_ahi[:, :, :], h0i[:, :, :], arb, ALU.mult, R_, W_)
            self.tt(th0[:, :, :], h0r[:, :, :], aib, ALU.mult, R_, W_)
            self.tt(self.s5_ahi[:, :, :], self.s5_ahi[:, :, :], th0[:, :, :], ALU.add, R_, W_)
            P.fence()

    def s5_main(self, yT):
        P = self.P
        K = "s5c"
        with ExitStack() as es:
            F = lambda n, shp: self.sb(es, n, shp, F32)
            cr = F("m_cr", [128, TP]); si = F("m_si", [128, TP])
            d0 = F("m_d0", [128, NT])
            wr = F("m_wr", [128, NT]); wi = F("m_wi", [128, NT])
            t1 = F("m_t1", [128, NT]); t2 = F("m_t2", [128, NT])
            ub = self.sb(es, "m_ub", [128, NT], BF16)
            xre = self.sb(es, "m_xre", [128, NT], BF16)
            xim = self.sb(es, "m_xim", [128, NT], BF16)
            pfr = F("m_pfr", [128, 32]); pfi = F("m_pfi", [128, 32])
            sfr = F("m_sfr", [128, NSEQ, 32]); sfi = F("m_sfi", [128, NSEQ, 32])
            ft = F("m_ft", [128, NSEQ])
            s3 = lambda ap: ap.rearrange("p (s t) -> p s t", t=8)
            for m in range(8):
                P.dma("pool", ub[:, :], self.zT.ap()[m], (), ["ub"])
                self.bank_pool = [5, 6, 7]
                for gp in range(4):
                    i = 4 * m + gp
                    self.memset(cr[:, 0:1], 1.0, ["cr"])
                    self.memset(si[:, 0:1], 0.0, ["si"])
                    self.cp(cr[:, 1:2], self.s5_cth[:, i:i + 1], [K], ["cr"])
                    self.cp(si[:, 1:2], self.s5_sth[:, i:i + 1], [K], ["si"])
                    L = 1
                    while L + 1 < TP:
                        n = min(L, TP - 1 - L)
                        cL, sL = cr[:, L:L + 1], si[:, L:L + 1]
                        self.ts(t1[:, 0:n], si[:, 1:1 + n], sL, ALU.mult, ["si"], ["t1"])
                        self.ts(t2[:, 0:n], cr[:, 1:1 + n], sL, ALU.mult, ["cr", "si"], ["t2"])
                        self.stt(cr[:, L + 1:L + 1 + n], cr[:, 1:1 + n], cL, t1[:, 0:n], ALU.mult, ALU.subtract, ["cr", "t1"], ["cr"])
                        self.stt(si[:, L + 1:L + 1 + n], si[:, 1:1 + n], cL, t2[:, 0:n], ALU.mult, ALU.add, ["si", "cr", "t2"], ["si"])
                        L += n
                    self.cp(d0[:, :], self.s5_mag[:, i:i + 1].to_broadcast([128, NT]), [K], ["d0"])
                    self.memset(d0[:, 0:1], 0.0, ["d0"])
                    self.memset(d0[:, TP:NT:8], 0.0, ["d0"])
                    crS = cr[:, 0:8].unsqueeze(1).to_broadcast([128, NSEQ, 8])
                    siS = si[:, 0:8].unsqueeze(1).to_broadcast([128, NSEQ, 8])
                    for sidx, (t0, n) in enumerate(SEGS):
                        br, bi = self.bank(), self.bank()
                        self.mm(self.ps[br][:, 0:n], self.s5_Bre[:, i, :], ub[:, t0:t0 + n], True, True, [K, "ub"], [("ps", br)])
                        self.mm(self.ps[bi][:, 0:n], self.s5_Bim[:, i, :], ub[:, t0:t0 + n], True, True, [K, "ub"], [("ps", bi)])
                        parts = [(0, min(n, TP - t0), False)]
                        if t0 + n > TP:
                            parts.append((TP - t0, n - (TP - t0), True))
                        for (o, ln, samp) in parts:
                            if samp:
                                pr, pi_ = s3(self.ps[br][:, o:o + ln]), s3(self.ps[bi][:, o:o + ln])
                                c_, s_ = crS, siS
                                v = lambda t: s3(t[:, t0 + o:t0 + o + ln])
                            else:
                                pr, pi_ = self.ps[br][:, o:o + ln], self.ps[bi][:, o:o + ln]
                                c_, s_ = cr[:, t0 + o:t0 + o + ln], si[:, t0 + o:t0 + o + ln]
                                v = lambda t: t[:, t0 + o:t0 + o + ln]
                            self.tt(v(t1), pr, c_, ALU.mult, [("ps", br), "cr"], ["t1"])
                            self.tt(v(t2), pi_, s_, ALU.mult, [("ps", bi), "si"], ["t2"])
                            self.tt(v(wr), v(t1), v(t2), ALU.add, ["t1", "t2"], ["wr"])
                            self.tt(v(t1), pi_, c_, ALU.mult, [("ps", bi), "cr"], ["t1"])
                            self.tt(v(t2), pr, s_, ALU.mult, [("ps", br), "si"], ["t2"])
                            self.tt(v(wi), v(t1), v(t2), ALU.subtract, ["t1", "t2"], ["wi"])
                    self.tt(wr[:, TP:NT:8], wr[:, TP:NT:8], self.s5_ahr[:, :, i], ALU.add, ["wr", K], ["wr"])
                    self.tt(wi[:, TP:NT:8], wi[:, TP:NT:8], self.s5_ahi[:, :, i], ALU.add, ["wi", K], ["wi"])
                    self.scan(wr[:, :], d0[:, :], wr[:, :], 0.0, ["d0", "wr"], ["wr"])
                    self.scan(wi[:, :], d0[:, :], wi[:, :], 0.0, ["d0", "wi"], ["wi"])
                    for samp in (False, True):
                        if samp:
                            v = lambda t: s3(t[:, TP:NT])
                            c_, s_ = crS, siS
                            vo = lambda t: s3(t[:, TP:NT])
                        else:
                            v = lambda t: t[:, 0:TP]
                            c_, s_ = cr[:, :], si[:, :]
                            vo = lambda t: t[:, 0:TP]
                        self.tt(v(t1), v(wr), c_, ALU.mult, ["wr", "cr"], ["t1"])
                        self.tt(v(t2), v(wi), s_, ALU.mult, ["wi", "si"], ["t2"])
                        self.tt(vo(xre), v(t1), v(t2), ALU.subtract, ["t1", "t2"], ["xre"])
                        self.tt(v(t1), v(wr), s_, ALU.mult, ["wr", "si"], ["t1"])
                        self.tt(v(t2), v(wi), c_, ALU.mult, ["wi", "cr"], ["t2"])
                        self.tt(vo(xim), v(t1), v(t2), ALU.add, ["t1", "t2"], ["xim"])
                    cP, sP = cr[:, TP - 1:TP], si[:, TP - 1:TP]
                    self.ts(ft[:, 0:1], wi[:, TP - 1:TP], sP, ALU.mult, ["wi", "si"], ["ft"])
                    self.stt(pfr[:, i:i + 1], wr[:, TP - 1:TP], cP, ft[:, 0:1], ALU.mult, ALU.subtract, ["wr", "cr", "ft"], ["pf"])
                    self.ts(ft[:, 0:1], wr[:, TP - 1:TP], sP, ALU.mult, ["wr", "si"], ["ft"])
                    self.stt(pfi[:, i:i + 1], wi[:, TP - 1:TP], cP, ft[:, 0:1], ALU.mult, ALU.add, ["wi", "cr", "ft"], ["pf"])
                    c7, s7 = cr[:, 7:8], si[:, 7:8]
                    self.ts(ft[:, :], wi[:, TP + 7:NT:8], s7, ALU.mult, ["wi", "si"], ["ft"])
                    self.stt(sfr[:, :, i], wr[:, TP + 7:NT:8], c7, ft[:, :], ALU.mult, ALU.subtract, ["wr", "cr", "ft"], ["sf"])
                    self.ts(ft[:, :], wr[:, TP + 7:NT:8], s7, ALU.mult, ["wr", "si"], ["ft"])
                    self.stt(sfi[:, :, i], wi[:, TP + 7:NT:8], c7, ft[:, :], ALU.mult, ALU.add, ["wi", "cr", "ft"], ["sf"])
                    for sidx, (t0, n) in enumerate(SEGS):
                        b = sidx
                        self.mm(self.ps[b][:, 0:n], self.s5_Cre[:, i, :], xre[:, t0:t0 + n], gp == 0, False, [K, "xre"], [("ps", b)])
                        self.mm(self.ps[b][:, 0:n], self.s5_Cim[:, i, :], xim[:, t0:t0 + n], False, False, [K, "xim"], [("ps", b)])
                for sidx, (t0, n) in enumerate(SEGS):
                    b = sidx
                    self.mm(self.ps[b][:, 0:n], self.s5_dD[:, m, :], ub[:, t0:t0 + n], False, True, [K, "ub"], [("ps", b)])
                    self.act(yT[:, 8 + m, t0:t0 + n], self.ps[b][:, 0:n], AF.Gelu_apprx_tanh, [("ps", b)], [("ya", m, sidx)])
                self.bank_pool = None
            stq = [self.sb(es, "m_stq%d" % j, [128, 128], F32) for j in range(2)]
            nq = 0
            for (srcT, dst) in ((pfr, "p_s5r"), (pfi, "p_s5i")):
                b = self.bank()
                s = stq[nq % 2]; sk = ("stq", nq % 2); nq += 1
                self.tr(self.ps[b][0:32, 0:128], srcT[:, :], self.ident[:, :], ["pf", "ident"], [("ps", b)])
                self.cp(s[0:32, :], self.ps[b][0:32, 0:128], [("ps", b)], [sk])
                P.dma("sp", self.dout[dst].ap().rearrange("(i gl) p -> i (gl p)", gl=2), s[0:32, :], [sk], [dst], is_output=True)
            for (srcT, dst) in ((sfr, "s_s5r"), (sfi, "s_s5i")):
                flat = srcT[:, :, :].rearrange("q s i -> q (s i)")
                dv = self.dout[dst].ap().rearrange("s (i gl) p -> (s i) (gl p)", gl=2)
                for c in range(4):
                    b = self.bank()
                    s = stq[nq % 2]; sk = ("stq", nq % 2); nq += 1
                    self.tr(self.ps[b][:, 0:128], flat[:, c * 128:(c + 1) * 128], self.ident[:, :], ["sf", "ident"], [("ps", b)])
                    self.cp(s[:, :], self.ps[b][:, 0:128], [("ps", b)], [sk])
                    P.dma("sp", dv[c * 128:(c + 1) * 128, :], s[:, :], [sk], [dst], is_output=True)
            P.fence()

    def s5_glu(self, yT):
        with ExitStack() as es:
            sg = [self.sb(es, "g_sg%d" % i, [128, 512], F32) for i in range(2)]
            cnt = [0]

            def evac(j, si, t0, n, ps, pk):
                s = sg[cnt[0] % 2]; sk = ("g_sg", cnt[0] % 2); cnt[0] += 1
                self.act(s[:, 0:n], ps, AF.Sigmoid, [pk], [sk])
                self.tt(yT[:, j, t0:t0 + n], s[:, 0:n], yT[:, 8 + j, t0:t0 + n], ALU.mult, [sk, ("ya", j, si)], [("yT", j, si)])

            self.proj(es, lambda kt, t0, n: yT[:, 8 + kt, t0:t0 + n], lambda kt, si: ("ya", kt, si), 8,
                      self.din["s5_w_glu"].ap(), 0, 1024, evac, tag="wg")
            self.P.fence()


HG_CHUNKS = [(64 * n, 64, None) for n in range(32)] + [(2048, 16, None)] + [(TP + 8 * q, 8, q) for q in range(NSEQ)]


class HgMixin:
    def hg_main(self, yT):
        P = self.P
        d = self.din
        K = "hgc"
        with ExitStack() as es:
            F = lambda n, shp=[128, NT]: self.sb(es, n, shp, F32)
            lbc = F("h_lbc", [128, 16]); lb = F("h_lb", [128, 8]); oml = F("h_oml", [128, 8]); nw = F("h_nw", [128, 1])
            cmask = F("h_cmask")
            self.load_cols(lbc[:, :], d["hg_lb"].ap().rearrange("l (c p) -> (l c) p", p=128), 16, K)
            self.load_cols(nw[:, :], d["hg_norm_w"].ap(), 1, K)
            self.tt(lb[:, :], lbc[:, 0:8], lbc[:, 8:16], ALU.subtract, [K], [K])
            self.act(lb[:, :], lb[:, :], AF.Sigmoid, [K], [K])
            self.ts(oml[:, :], lb[:, :], -1.0, ALU.mult, [K], [K], s2=1.0, op1=ALU.add)
            self.memset(cmask[:, :], 1.0, [K])
            self.memset(cmask[:, 0:2048:64], 0.0, [K])
            self.memset(cmask[:, 2048:2049], 0.0, [K])
            self.memset(cmask[:, TP:NT:8], 0.0, [K])
            qT, fg, iT, gT, kk, bc, btf, ex, kend, oall = (F(n) for n in ("h_q", "h_f", "h_i", "h_g", "h_kk", "h_bc", "h_btf", "h_ex", "h_kend", "h_o"))
            qin = self.sb(es, "h_qin", [128, NT], BF16)
            kin = self.sb(es, "h_kin", [128, NT], BF16)
            dec = F("h_dec", [128, 49])
            S = [F("h_S%d" % i, [128, 128]) for i in range(2)]
            Sb = [self.sb(es, "h_Sb%d" % i, [128, 128], BF16) for i in range(2)]
            attm = [self.sb(es, "h_att%d" % i, [64, 64], BF16) for i in range(2)]
            vt = [self.sb(es, "h_vt%d" % i, [64, 128], BF16) for i in range(2)]
            ket = [self.sb(es, "h_ket%d" % i, [64, 128], BF16) for i in range(2)]
            sq = [F("h_sq%d" % i, [128, 512]) for i in range(2)]
            rs = [F("h_rs%d" % i, [128, 512]) for i in range(2)]
            eps = F("h_eps", [128, 1])
            self.memset(eps[:, :], EPS, [K])
            s_i = 0
            for hh in range(8):
                for (t, j) in ((qT, 8 + hh), (fg, 16 + hh), (iT, 24 + hh), (gT, 32 + hh)):
                    P.dma("sp", t[:, :], self.zT.ap()[j], (), [t.name])
                R = lambda *ts: [t.name if hasattr(t, "name") else t for t in ts]
                self.act(fg[:, :], fg[:, :], AF.Sigmoid, R(fg), R(fg))
                self.ts(fg[:, :], fg[:, :], oml[:, hh:hh + 1], ALU.mult, R(fg, K), R(fg), s2=lb[:, hh:hh + 1], op1=ALU.add)
                self.ts(kk[:, :], fg[:, :], -1.0, ALU.mult, R(fg), R(kk), s2=1.0, op1=ALU.add)
                self.act(bc[:, :], fg[:, :], AF.Ln, R(fg), R(bc))
                self.scan(bc[:, :], cmask[:, :], bc[:, :], 0.0, R(bc, K), R(bc))
                v3 = lambda ap, c: ap.rearrange("p (n c) -> p n c", c=c)
                self.cp(v3(btf[:, 0:2048], 64), v3(bc[:, 0:2048], 64)[:, :, 63:64].to_broadcast([128, 32, 64]), R(bc), R(btf))
                self.cp(btf[:, 2048:TP], bc[:, TP - 1:TP].to_broadcast([128, 16]), R(bc), R(btf))
                self.cp(v3(btf[:, TP:NT], 8), v3(bc[:, TP:NT], 8)[:, :, 7:8].to_broadcast([128, NSEQ, 8]), R(bc), R(btf))
                self.act(dec[:, 0:32], bc[:, 63:2048:64], AF.Exp, R(bc), R(dec))
                self.act(dec[:, 32:33], bc[:, TP - 1:TP], AF.Exp, R(bc), R(dec))
                self.act(dec[:, 33:49], bc[:, TP + 7:NT:8], AF.Exp, R(bc), R(dec))
                self.act(qT[:, :], qT[:, :], AF.Silu, R(qT), R(qT))
                self.act(ex[:, :], bc[:, :], AF.Exp, R(bc), R(ex))
                self.tt(qin[:, :], qT[:, :], ex[:, :], ALU.mult, R(qT, ex), R(qin))
                self.act(ex[:, :], bc[:, :], AF.Exp, R(bc, qin), R(ex), scale=-1.0)
                self.tt(kin[:, :], kk[:, :], ex[:, :], ALU.mult, R(kk, ex), R(kin))
                self.tt(btf[:, :], btf[:, :], bc[:, :], ALU.subtract, R(btf, bc), R(btf))
                self.act(btf[:, :], btf[:, :], AF.Exp, R(btf), R(btf))
                self.tt(kend[:, :], kk[:, :], btf[:, :], ALU.mult, R(kk, btf), R(kend))
                cur = s_i % 2
                self.memset(S[cur][:, :], 0.0, [("S", cur)])
                self.memset(Sb[cur][:, :], 0.0, [("Sb", cur)])
                for n, (t0, C, q) in enumerate(HG_CHUNKS):
                    if q is not None:
                        s_i += 1
                        cur = s_i % 2
                        P.dma("sp", S[cur][:, :], d["st_hg"].ap()[q, hh], (), [("S", cur)])
                        self.cp(Sb[cur][:, :], S[cur][:, :], [("S", cur)], [("Sb", cur)], eng="act")
                    a_i = n % 2
                    ba, bv, bk, bo, bs = (self.bank() for _ in range(5))
                    self.mm(self.ps[ba][0:C, 0:C], kin[:, t0:t0 + C], qin[:, t0:t0 + C], True, True, R(kin, qin), [("ps", ba)])
                    self.tt(attm[a_i][0:C, 0:C], self.ps[ba][0:C, 0:C], self.tril[0:C, 0:C], ALU.mult, [("ps", ba), "tril"], [("att", a_i)])
                    self.tr(self.ps[bv][0:C, 0:128], iT[:, t0:t0 + C], self.ident[:, :], R(iT, "ident"), [("ps", bv)])
                    self.cp(vt[a_i][0:C, :], self.ps[bv][0:C, 0:128], [("ps", bv)], [("vt", a_i)], eng="act")
                    self.tr(self.ps[bk][0:C, 0:128], kend[:, t0:t0 + C], self.ident[:, :], R(kend, "ident"), [("ps", bk)])
                    self.cp(ket[a_i][0:C, :], self.ps[bk][0:C, 0:128], [("ps", bk)], [("ket", a_i)], eng="act")
                    self.mm(self.ps[bo][:, 0:C], vt[a_i][0:C, :], attm[a_i][0:C, 0:C], True, False, [("vt", a_i), ("att", a_i)], [("ps", bo)])
                    self.mm(self.ps[bo][:, 0:C], Sb[cur][:, :], qin[:, t0:t0 + C], False, True, [("Sb", cur), "h_qin"], [("ps", bo)])
                    self.cp(oall[:, t0:t0 + C], self.ps[bo][:, 0:C], [("ps", bo)], R(oall), eng="act")
                    self.mm(self.ps[bs][:, 0:128], ket[a_i][0:C, :], vt[a_i][0:C, :], True, True, [("ket", a_i), ("vt", a_i)], [("ps", bs)])
                    nxt = (s_i + 1) % 2 if q is None else cur
                    if q is None:
                        s_i += 1
                    self.stt(S[nxt][:, :], S[cur][:, :], dec[:, n:n + 1], self.ps[bs][:, 0:128], ALU.mult, ALU.add,
                             [("S", cur), ("ps", bs)] + R(dec), [("S", nxt)])
                    if q is None:
                        self.cp(Sb[nxt][:, :], S[nxt][:, :], [("S", nxt)], [("Sb", nxt)], eng="act")
                        cur = nxt
                        if n == 32:
                            P.dma("sp", self.dout["p_hg"].ap()[hh], S[cur][:, :], [("S", cur)], ["p_hg"], is_output=True)
                    else:
                        P.dma("sp", self.dout["s_hg"].ap()[q, hh], S[cur][:, :], [("S", cur)], ["s_hg"], is_output=True)
                self.act(gT[:, :], gT[:, :], AF.Silu, R(gT), R(gT))
                for si, (t0, n) in enumerate(SEGS):
                    q_ = sq[si % 2]; r_ = rs[si % 2]
                    b = self.bank()
                    self.act(q_[:, 0:n], oall[:, t0:t0 + n], AF.Square, R(oall), [("hsq", si % 2)])
                    self.mm(self.ps[b][:, 0:n], self.ones[:, :], q_[:, 0:n], True, True, [("hsq", si % 2), "ones"], [("ps", b)])
                    self.act(r_[:, 0:n], self.ps[b][:, 0:n], AF.Sqrt, [("ps", b), K], [("hrs", si % 2)], bias=eps[:, 0:1], scale=1.0 / 128)
                    self.P.op("dve", lambda e, r_=r_, n=n: e.reciprocal(out=r_[:, 0:n], in_=r_[:, 0:n]), [("hrs", si % 2)], [("hrs", si % 2)])
                    self.stt(r_[:, 0:n], oall[:, t0:t0 + n], nw[:, 0:1], r_[:, 0:n], ALU.mult, ALU.mult, R(oall, K) + [("hrs", si % 2)], [("hrs", si % 2)])
                    self.tt(yT[:, 8 + hh, t0:t0 + n], r_[:, 0:n], gT[:, t0:t0 + n], ALU.mult, [("hrs", si % 2)] + R(gT), [("yT", 8 + hh, si)])
            P.fence()


class FfnMixin:
    def out_proj_res(self, xT, xkey, KT, w2d, tag="wo"):
        P = self.P
        with ExitStack() as es:
            hcol = [self.sb(es, "hcol%d" % i, [128, NT], F32) for i in range(3)]

            def evac(j, si, t0, n, ps, pk):
                h = hcol[j % 3]; hk = ("hcol", j % 3)
                if si == 0:
                    P.dma("sp", h[:, :], self.hT.ap()[j], [("hT", j)], [hk])
                self.tt(h[:, t0:t0 + n], h[:, t0:t0 + n], ps, ALU.add, [hk, pk], [hk])
                if si == len(SEGS) - 1:
                    P.dma("sp", self.hT.ap()[j], h[:, :], [hk], [("hT", j)])

            self.proj(es, xT, xkey, KT, w2d, 0, D, evac, tag=tag)
            P.fence()

    def ffn(self, l, xnT):
        P = self.P
        d = self.din
        K = "ffc"
        v3 = lambda ap: ap.rearrange("p (s t) -> p s t", t=8)
        gS = self.gS
        with ExitStack() as es:
            F = lambda n, shp: self.sb(es, n, shp, F32)
            cw = F("f_cw", [128, 3 * FC]); cb = F("f_cb", [128, FC])
            cvin = F("f_cvin", [128, FC, 32]); cvP = F("f_cvP", [128, FC, 2]); cvS = F("f_cvS", [128, FC, 32])
            self.load_cols(cw[:, :], d["ffn_conv_w"].ap()[l].rearrange("r (c p) -> (r c) p", p=128), 3 * FC, K)
            self.load_cols(cb[:, :], d["ffn_conv_b"].ap()[l:l + 1, :].rearrange("o (c p) -> (o c) p", p=128), FC, K)
            with ExitStack() as es2:
                rows = self.sb(es2, "f_rows", [32, D_FF], F32)
                P.dma("sp", rows[:, :], d["st_cv"].ap()[l].rearrange("s r f -> (s r) f"), (), ["f_rows"])
                for c in range(FC):
                    b = self.bank()
                    self.tr(self.ps[b][:, 0:32], rows[0:32, c * 128:(c + 1) * 128], self.ident[0:32, 0:32], ["f_rows", "ident"], [("ps", b)])
                    self.cp(cvin[:, c, :], self.ps[b][:, 0:32], [("ps", b)], [K], eng=("act" if c % 2 else "dve"))
                P.fence()
            wf = [self.sb(es, "f_wf%d" % i, [128, DC, 512], BF16) for i in range(2)]
            aP = [F("f_aP%d" % i, [128, TP + 2]) for i in range(2)]
            aS = [F("f_aS%d" % i, [128, NSEQ, 10]) for i in range(2)]
            cc = [F("f_cc%d" % i, [128, NT]) for i in range(2)]
            go = [self.sb(es, "f_go%d" % i, [128, NT], BF16) for i in range(2)]
            for i in range(2):
                self.memset(aP[i][:, 0:2], 0.0, [("aP", i)])
            w_in = d["ffn_w_in"].ap()[l]
            for g0 in range(0, FC, 2):
                w = wf[(g0 // 2) % 2]; wk = ("f_wf", (g0 // 2) % 2)
                P.dma("pool", w[:, :, 0:256], w_in[:, g0 * 128:(g0 + 2) * 128].rearrange("(kt p) n -> p kt n", p=128), (), [wk])
                P.dma("pool", w[:, :, 256:512], w_in[:, D_FF + g0 * 128:D_FF + (g0 + 2) * 128].rearrange("(kt p) n -> p kt n", p=128), (), [wk])
                for jj in range(2):
                    j = g0 + jj
                    ap_, as_, c_, g_ = aP[j % 2], aS[j % 2], cc[j % 2], go[j % 2]
                    ak, ask, ck, gk = ("aP", j % 2), ("aS", j % 2), ("cc", j % 2), ("go", j % 2)
                    self.cp(as_[:, :, 0:2], cvin[:, j, :].rearrange("p (s r) -> p s r", r=2), [K], [ask])
                    for si, (t0, n) in enumerate(SEGS):
                        b = self.bank()
                        for kt in range(DC):
                            self.mm(self.ps[b][:, 0:n], w[:, kt, jj * 128:(jj + 1) * 128], xnT[:, kt, t0:t0 + n], kt == 0, kt == DC - 1,
                                    [wk, ("xnT", kt, si)], [("ps", b)])
                        npr = min(n, TP - t0)
                        self.cp(ap_[:, 2 + t0:2 + t0 + npr], self.ps[b][:, 0:npr], [("ps", b)], [ak], eng="act")
                        if npr < n:
                            self.cp(as_[:, :, 2:10], v3(self.ps[b][:, npr:n]), [("ps", b)], [ask], eng="act")
                    w0, w1, w2 = (cw[:, r * FC + j:r * FC + j + 1] for r in range(3))
                    self.ts(c_[:, 0:TP], ap_[:, 2:2 + TP], w2, ALU.mult, [ak, K], [ck], s2=cb[:, j:j + 1], op1=ALU.add)
                    self.stt(c_[:, 0:TP], ap_[:, 1:1 + TP], w1, c_[:, 0:TP], ALU.mult, ALU.add, [ak, K, ck], [ck])
                    self.stt(c_[:, 0:TP], ap_[:, 0:TP], w0, c_[:, 0:TP], ALU.mult, ALU.add, [ak, K, ck], [ck])
                    cs = v3(c_[:, TP:NT])
                    self.ts(cs, as_[:, :, 2:10], w2, ALU.mult, [ask, K], [ck], s2=cb[:, j:j + 1], op1=ALU.add)
                    self.stt(cs, as_[:, :, 1:9], w1, cs, ALU.mult, ALU.add, [ask, K, ck], [ck])
                    self.stt(cs, as_[:, :, 0:8], w0, cs, ALU.mult, ALU.add, [ask, K, ck], [ck])
                    self.cp(cvP[:, j, :], ap_[:, TP:TP + 2], [ak], [K])
                    self.cp(cvS[:, j, :].rearrange("p (s r) -> p s r", r=2), as_[:, :, 8:10], [ask], [K])
                    self.act(c_[:, :], c_[:, :], AF.Gelu_apprx_tanh, [ck], [ck])
                    for si, (t0, n) in enumerate(SEGS):
                        b = self.bank()
                        for kt in range(DC):
                            self.mm(self.ps[b][:, 0:n], w[:, kt, 256 + jj * 128:256 + (jj + 1) * 128], xnT[:, kt, t0:t0 + n], kt == 0, kt == DC - 1,
                                    [wk, ("xnT", kt, si)], [("ps", b)])
                        self.tt(g_[:, t0:t0 + n], c_[:, t0:t0 + n], self.ps[b][:, 0:n], ALU.mult, [ck, ("ps", b)], [gk])
                    for si, (t0, n) in enumerate(SEGS):
                        P.dma("sp", gS.ap()[si, :, j, 0:n], g_[:, t0:t0 + n], [gk], [("gS", j)])
            with ExitStack() as es2:
                rows = self.sb(es2, "f_rowo", [32, D_FF], F32)
                rowp = self.sb(es2, "f_rowp", [2, D_FF], F32)
                for c in range(FC):
                    b = self.bank()
                    self.tr(self.ps[b][0:32, 0:128], cvS[:, c, :], self.ident[:, :], [K, "ident"], [("ps", b)])
                    self.cp(rows[0:32, c * 128:(c + 1) * 128], self.ps[b][0:32, 0:128], [("ps", b)], ["f_rowo"], eng=("act" if c % 2 else "dve"))
                    b = self.bank()
                    self.tr(self.ps[b][0:2, 0:128], cvP[:, c, :], self.ident[:, :], [K, "ident"], [("ps", b)])
                    self.cp(rowp[0:2, c * 128:(c + 1) * 128], self.ps[b][0:2, 0:128], [("ps", b)], ["f_rowp"], eng=("dve" if c % 2 else "act"))
                P.dma("sp", self.dout["s_cv"].ap()[l].rearrange("s r f -> (s r) f"), rows[:, :], ["f_rowo"], ["s_cv"], is_output=True)
                P.dma("sp", self.dout["p_cv"].ap()[l], rowp[:, :], ["f_rowp"], ["p_cv"], is_output=True)
                P.fence()
            P.fence()

    def ffn_down(self, l):
        P = self.P
        gS = self.gS
        w_dn = self.din["ffn_w_down"].ap()[l]
        with ExitStack() as es:
            wd = [self.sb(es, "d_wd%d" % i, [128, FC, 256], BF16) for i in range(2)]
            gs = [self.sb(es, "d_gs%d" % i, [128, FC, 512], BF16) for i in range(2)]
            hc = [self.sb(es, "d_hc%d" % i, [128, 2, NT], F32) for i in range(2)]
            gi = 0
            for g0 in range(0, DC, 2):
                w = wd[(g0 // 2) % 2]; wk = ("d_wd", (g0 // 2) % 2)
                h = hc[(g0 // 2) % 2]; hk = ("d_hc", (g0 // 2) % 2)
                P.dma("pool", w[:, :, :], w_dn[:, g0 * 128:(g0 + 2) * 128].rearrange("(kt p) n -> p kt n", p=128), (), [wk])
                P.dma("sp", h[:, :, :], self.hT.ap()[g0:g0 + 2].rearrange("c p t -> p c t"), [("hT", g0), ("hT", g0 + 1)], [hk])
                for si, (t0, n) in enumerate(SEGS):
                    g = gs[gi % 2]; gk = ("d_gs", gi % 2); gi += 1
                    P.dma("act", g[:, :, 0:n], gS.ap()[si, :, :, 0:n], [("gS", j) for j in range(FC)], [gk])
                    for ii in range(2):
                        b = self.bank()
                        for kt in range(FC):
                            self.mm(self.ps[b][:, 0:n], w[:, kt, ii * 128:(ii + 1) * 128], g[:, kt, 0:n], kt == 0, kt == FC - 1, [wk, gk], [("ps", b)])
                        self.tt(h[:, ii, t0:t0 + n], h[:, ii, t0:t0 + n], self.ps[b][:, 0:n], ALU.add, [hk, ("ps", b)], [hk])
                P.dma("sp", self.hT.ap()[g0:g0 + 2].rearrange("c p t -> p c t"), h[:, :, :], [hk], [("hT", g0), ("hT", g0 + 1)])
            P.fence()


class RwMixin:
    def rw_declare(self):
        sc = self.scratch
        fm = lambda n, dt=F32: sc(n, [DC, 128, NT], dt)
        self.rS, self.kS, self.vS, self.aS = fm("rS"), fm("kS"), fm("vS"), fm("aS")
        self.dS, self.g2S, self.bonS = fm("dS"), fm("g2S"), fm("bonS")
        self.rbS, self.avS = fm("rbS", BF16), fm("avS", BF16)
        self.bkv = sc("bkv", [3, NT, D], BF16)
        self.ytok = sc("ytok", [NT, D], BF16)

    def rw_shift_out(self, xn32):
        P = self.P
        if True:
            with ExitStack() as es2:
                tokA = self.sb(es2, "r_tokA", [16, D], F32)
                tokB = self.sb(es2, "r_tokB", [128, D], F32)
                for c in range(DC):
                    b = self.bank()
                    self.tr(self.ps[b][0:16, 0:128], xn32[:, c, 0:16], self.ident[:, :], ["out32", "ident"], [("ps", b)])
                    self.cp(tokA[0:16, c * 128:(c + 1) * 128], self.ps[b][0:16, 0:128], [("ps", b)], ["tokA"], eng="act")
                    b = self.bank()
                    self.tr(self.ps[b][:, 0:128], xn32[:, c, 16:144], self.ident[:, :], ["out32", "ident"], [("ps", b)])
                    self.cp(tokB[:, c * 128:(c + 1) * 128], self.ps[b][:, 0:128], [("ps", b)], ["tokB"])
                P.dma("sp", self.dout["p_sh"].ap(), tokA[15:16, :], ["tokA"], ["p_sh"], is_output=True)
                for q in range(NSEQ):
                    P.dma("sp", self.dout["s_sh"].ap()[q:q + 1, :], tokB[8 * q + 7:8 * q + 8, :], ["tokB"], ["s_sh"], is_output=True)
                P.fence()

    def rw_proj(self, xnT):
        P = self.P
        d = self.din
        K = "rwc"
        v3 = lambda ap: ap.rearrange("p (s t) -> p s t", t=8)
        with ExitStack() as es:
            F = lambda n, shp: self.sb(es, n, shp, F32)
            mu = F("r_mu", [128, 96]); omm = F("r_omm", [128, 96])
            self.load_cols(mu[:, :], d["rw_mu"].ap().rearrange("j (c p) -> (j c) p", p=128), 96, K)
            self.ts(omm[:, :], mu[:, :], -1.0, ALU.mult, [K], [K], s2=1.0, op1=ALU.add)
            w0c = F("r_w0", [128, DC]); a0c = F("r_a0", [128, DC])
            self.load_cols(w0c[:, :], d["rw_w0"].ap().rearrange("o (c p) -> (o c) p", p=128), DC, K)
            self.load_cols(a0c[:, :], d["rw_a0"].ap().rearrange("o (c p) -> (o c) p", p=128), DC, K)
            shin = F("r_shin", [128, DC, NSEQ])
            for c in range(DC):
                self.load_cols(shin[:, c, :], d["st_sh"].ap()[:, c * 128:(c + 1) * 128], NSEQ, K)
            xj = self.sb(es, "r_xj", [128, DC, NT], BF16)
            t1s = [self.sb(es, "r_t1%d" % i, [128, NT], BF16) for i in range(2)]
            zrow = [F("r_zrow%d" % i, [128, NT]) for i in range(2)]

            def build_mix(j):
                for c in range(DC):
                    m_, o_ = mu[:, j * 16 + c:j * 16 + c + 1], omm[:, j * 16 + c:j * 16 + c + 1]
                    xk = [("xnT", c, si) for si in range(5)]
                    wk = [("xj", c, si) for si in range(5)]
                    t1 = t1s[c % 2]; tk1 = ("r_t1", c % 2)
                    self.act(t1[:, :], xnT[:, c, :], AF.Copy, xk + [K], [tk1], scale=o_)
                    self.stt(xj[:, c, 1:TP], xnT[:, c, 0:TP - 1], m_, t1[:, 1:TP], ALU.mult, ALU.add, xk + [tk1, K], wk)
                    self.cp(xj[:, c, 0:1], t1[:, 0:1], [tk1], wk)
                    self.stt(v3(xj[:, c, TP:NT])[:, :, 1:8], v3(xnT[:, c, TP:NT])[:, :, 0:7], m_, v3(t1[:, TP:NT])[:, :, 1:8],
                             ALU.mult, ALU.add, xk + [tk1, K], wk)
                    self.stt(xj[:, c, TP:NT:8], shin[:, c, :], m_, t1[:, TP:NT:8], ALU.mult, ALU.add, [tk1, K], wk)

            xkey = lambda kt, si: ("xj", kt, si)

            def to_scratch(dst, func=None, bias=None, scale=None):
                def evac(j, si, t0, n, ps, pk):
                    z = zrow[j % 2]
                    if func is None:
                        self.cp(z[:, t0:t0 + n], ps, [pk], [("zrow", j % 2, si)], eng=("act" if si % 2 else "dve"))
                    else:
                        self.act(z[:, t0:t0 + n], ps, func, [pk, K], [("zrow", j % 2, si)],
                                 bias=(bias[:, j:j + 1] if bias is not None else None), scale=scale)
                    if si == len(SEGS) - 1:
                        P.dma("sp", dst.ap()[j], z[:, :], [("zrow", j % 2, s_) for s_ in range(len(SEGS))], [dst.name])
                return evac

            if "dbg_xj" in self.debug:
                build_mix(0)
                dbg = self.scratch("dbg_xj", [DC, 128, NT], BF16)
                dbg2 = self.scratch("dbg_xn", [DC, 128, NT], BF16)
                P.dma("sp", dbg.ap().rearrange("c p t -> p c t"), xj[:, :, :], [("xj", c, si) for c in range(DC) for si in range(5)], ["dbg"], is_output=True)
                P.dma("sp", dbg2.ap().rearrange("c p t -> p c t"), xnT[:, :, :], [("xnT", c, si) for c in range(DC) for si in range(5)], ["dbg2"], is_output=True)
                P.fence()
                return
            with ExitStack() as esw:
                build_mix(0)
                self.proj(esw, xj, xkey, DC, d["rw_w_r"].ap(), 0, D, to_scratch(self.rS), tag="wr", wg=2)
                P.fence()
            with ExitStack() as esw:
                build_mix(2)
                self.proj(esw, xj, xkey, DC, d["rw_w_k"].ap(), 0, D, to_scratch(self.kS), tag="wk", wg=2)
                P.fence()
            with ExitStack() as esw:
                build_mix(3)
                self.proj(esw, xj, xkey, DC, d["rw_w_v"].ap(), 0, D, to_scratch(self.vS), tag="wv", wg=2)
                P.fence()
            mid = self.sb(es, "r_mid", [128, 2, NT], BF16)

            def lora(jmix, w1, r1, f1, w2, f2, bias2, dst, sc2=None, tag="l"):
                build_mix(jmix)
                kp = 128 if r1 > 128 else r1
                kt2 = (r1 + 127) // 128

                def ev1(j, si, t0, n, ps, pk):
                    mw = min(128, r1 - j * 128)
                    if f1 is None:
                        self.cp(mid[0:mw, j, t0:t0 + n], ps, [pk], [("mid", j, si)], eng="act")
                    else:
                        self.act(mid[0:mw, j, t0:t0 + n], ps, f1, [pk], [("mid", j, si)])
                with ExitStack() as esw:
                    self.proj(esw, xj, xkey, DC, w1.ap(), 0, r1, ev1, tag=tag + "1", wg=2)
                    P.fence()
                with ExitStack() as esw:
                    self.proj(esw, mid, lambda kt, si: ("mid", kt, si), kt2, w2.ap(), 0, D,
                              to_scratch(dst, f2, bias2, sc2), kparts=kp, tag=tag + "2", wg=2)
                    P.fence()

            lora(1, d["rw_w1"], 96, AF.Tanh, d["rw_w2"], AF.Sigmoid, w0c, self.dS, tag="lw")
            lora(4, d["rw_a1"], 96, None, d["rw_a2"], AF.Sigmoid, a0c, self.aS, tag="la")
            lora(5, d["rw_g1"], 256, AF.Sigmoid, d["rw_g2"], None, None, self.g2S, tag="lg")
            P.fence()

    def rw_post(self):
        P = self.P
        d = self.din
        K = "rwc2"
        with ExitStack() as es:
            F = lambda n, shp=[128, NT]: self.sb(es, n, shp, F32)
            kkc, kac, rkc = F("q_kk", [128, DC]), F("q_ka", [128, DC]), F("q_rk", [128, DC])
            for (t, nm) in ((kkc, "rw_k_k"), (kac, "rw_k_a"), (rkc, "rw_r_k")):
                self.load_cols(t[:, :], d[nm].ap().rearrange("o (c p) -> (o c) p", p=128), DC, K)
            nb = 2
            r_, k_, v_, a_, dd = ([F("q_%s%d" % (n, i)) for i in range(nb)] for n in ("r", "k", "v", "a", "d"))
            kk, t1, t2 = F("q_kkn"), F("q_t1"), F("q_t2")
            rb = [self.sb(es, "q_rb%d" % i, [128, NT], BF16) for i in range(nb)]
            av = [self.sb(es, "q_av%d" % i, [128, NT], BF16) for i in range(nb)]
            tok = [self.sb(es, "q_tok%d" % i, [128, 3, 128], BF16) for i in range(4)]
            ntok = 0
            blocks = [(128 * i, 128) for i in range(17)] + [(2176, 16)]
            for c in range(DC):
                u = c % nb
                kr, kk_, kv, ka, kd = (("q", n, u) for n in ("r", "k", "v", "a", "d"))
                P.dma("sp", r_[u][:, :], self.rS.ap()[c], ["rS"], [kr])
                P.dma("sp", k_[u][:, :], self.kS.ap()[c], ["kS"], [kk_])
                P.dma("sp", v_[u][:, :], self.vS.ap()[c], ["vS"], [kv])
                P.dma("sp", a_[u][:, :], self.aS.ap()[c], ["aS"], [ka])
                P.dma("sp", dd[u][:, :], self.dS.ap()[c], ["dS"], [kd])
                self.act(dd[u][:, :], dd[u][:, :], AF.Exp, [kd], [kd], scale=-math.exp(-0.5))
                P.dma("sp", self.dS.ap()[c], dd[u][:, :], [kd], [("dS2", c)])
                self.ts(kk[:, :], k_[u][:, :], kkc[:, c:c + 1], ALU.mult, [kk_, K], ["kk"])
                self.act(t1[:, :], kk[:, :], AF.Square, ["kk"], ["t1"])
                for si, (t0, n) in enumerate(SEGS):
                    b = self.bank()
                    self.mm(self.ps[b][:, 0:n], self.bones[:, :], t1[:, t0:t0 + n], True, True, ["t1", "bones"], [("ps", b)])
                    self.act(t2[:, t0:t0 + n], self.ps[b][:, 0:n], AF.Sqrt, [("ps", b)], ["t2"])
                self.ts(t2[:, :], t2[:, :], 1e-12, ALU.max, ["t2"], ["t2"])
                self.P.op("dve", lambda e: e.reciprocal(out=t2[:, :], in_=t2[:, :]), ["t2"], ["t2"])
                self.tt(kk[:, :], kk[:, :], t2[:, :], ALU.mult, ["kk", "t2"], ["kk"])
                self.ts(av[u][:, :], kk[:, :], -1.0, ALU.mult, ["kk"], [("q", "av", u)])
                P.dma("sp", self.avS.ap()[c], av[u][:, :], [("q", "av", u)], ["avS"])
                self.cp(rb[u][:, :], r_[u][:, :], [kr], [("q", "rb", u)], eng="act")
                P.dma("sp", self.rbS.ap()[c], rb[u][:, :], [("q", "rb", u)], ["rbS"])
                self.tt(kk[:, :], kk[:, :], a_[u][:, :], ALU.mult, ["kk", ka], ["kk"])
                self.ts(t1[:, :], a_[u][:, :], -1.0, ALU.add, [ka, K], ["t1"], s2=kac[:, c:c + 1], op1=ALU.mult)
                self.ts(t1[:, :], t1[:, :], 1.0, ALU.add, ["t1"], ["t1"])
                self.tt(k_[u][:, :], k_[u][:, :], t1[:, :], ALU.mult, [kk_, "t1"], [kk_])
                self.stt(t1[:, :], r_[u][:, :], rkc[:, c:c + 1], k_[u][:, :], ALU.mult, ALU.mult, [kr, kk_, K], ["t1"])
                for si, (t0, n) in enumerate(SEGS):
                    b = self.bank()
                    self.mm(self.ps[b][:, 0:n], self.bones[:, :], t1[:, t0:t0 + n], True, True, ["t1", "bones"], [("ps", b)])
                    self.tt(t2[:, t0:t0 + n], v_[u][:, t0:t0 + n], self.ps[b][:, 0:n], ALU.mult, [kv, ("ps", b)], ["t2"])
                P.dma("sp", self.bonS.ap()[c], t2[:, :], ["t2"], ["bonS"])
                for (t0, n) in blocks:
                    tk = tok[ntok % 4]; tkk = ("q_tok", ntok % 4); ntok += 1
                    for ti, (src, sk) in enumerate(((kk, "kk"), (k_[u], kk_), (v_[u], kv))):
                        b = self.bank()
                        self.tr(self.ps[b][0:n, 0:128], src[:, t0:t0 + n], self.ident[:, :], [sk, "ident"], [("ps", b)])
                        self.cp(tk[0:n, ti, :], self.ps[b][0:n, 0:128], [("ps", b)], [tkk], eng=("act" if ti != 1 else "dve"))
                    P.dma("sp", self.bkv.ap()[:, t0:t0 + n, c * 128:(c + 1) * 128].rearrange("a t f -> t a f"), tk[0:n, :, :], [tkk], ["bkv"])
            P.fence()


NSL = 16


class RwScanMixin:
    def rw_scan(self, nsplit=4):
        import os
        PE_ = "dve" if os.environ.get("RW_NOPOOL") else "pool"
        P = self.P
        d = self.din
        HH = 16 // nsplit
        with ExitStack() as es:
            F = lambda n, shp: self.sb(es, n, shp, F32)
            Bf = lambda n, shp: self.sb(es, n, shp, BF16)
            ST = [F("w_ST%d" % i, [128, 16, 64]) for i in range(2)]
            Sb = [Bf("w_Sb%d" % i, [128, 16, 64]) for i in range(2)]
            tmp = [F("w_tmp%d" % i, [128, 16, 64]) for i in range(2)]
            ZV = Bf("w_ZV", [6, NSL, 1024])
            BK = Bf("w_BK", [6, NSL, 16, 128])
            AR = [Bf("w_AR%d" % i, [128, 16, 64, 4]) for i in range(2)]
            WD = [F("w_WD%d" % i, [128, 16, 64]) for i in range(2)]
            rblk = [Bf("w_rb%d" % i, [128, 16, 64]) for i in range(2)]
            ablk = [Bf("w_ab%d" % i, [128, 16, 64]) for i in range(2)]
            stin = [F("w_sti%d" % i, [64, 16, 2, 64]) for i in range(2)]
            stout = [F("w_sto%d" % i, [64, 16, 128]) for i in range(2)]
            pzb = [self.ps[h] for h in range(nsplit)]
            pub = [self.ps[4 + h] for h in range(nsplit)]
            pt = self.psbig[3]
            W = HH * 64
            self.memset(BK[:, :, :, :], 0.0, [("BK", s_) for s_ in range(NSL)], eng=PE_)
            self.memset(ZV[:, :, :], 0.0, [("ZV", s_, h) for s_ in range(NSL) for h in range(nsplit)] + [("ZVv", s_) for s_ in range(NSL)])
            hp4 = lambda ap: ap.rearrange("t (hg hp k) -> hp t hg k", hp=2, k=64)
            G = [0]
            blkc = [0]
            nst = [0]

            def chunks(lo, hi, g0):
                out = []
                a = lo
                while a < hi:
                    ga = g0 + a
                    b = min(hi, a + (8 - ga % 8))
                    out.append((a, b, ga % NSL))
                    a = b
                return out

            def run_seq(t0, L, sq, q):
                g0 = G[0]
                if q is None:
                    self.memset(ST[sq][:, :, :], 0.0, [("ST", sq, h) for h in range(nsplit)])
                    self.memset(Sb[sq][:, :, :], 0.0, [("Sb", sq, h) for h in range(nsplit)])
                else:
                    si_ = stin[q % 2]
                    P.dma("sp", si_[:, :, :, :], d["st_rw"].ap()[q].rearrange("(hg hp) v k -> v hg hp k", hp=2), (), [("sti", q % 2)])
                    for hg in range(16):
                        self.tr(pt[:, hg * 64:(hg + 1) * 64], si_[:, hg, :, :].rearrange("v hp k -> v (hp k)"), self.ident[0:64, 0:64],
                                [("sti", q % 2), "ident"], ["pt"])
                    for h in range(nsplit):
                        sl = slice(h * HH, (h + 1) * HH)
                        src = pt[:, h * W:(h + 1) * W].rearrange("p (g v) -> p g v", v=64)
                        self.cp(ST[sq][:, sl, :], src, ["pt"], [("ST", sq, h)])
                        self.cp(Sb[sq][:, sl, :], ST[sq][:, sl, :], [("ST", sq, h)], [("Sb", sq, h)], eng="act")
                import os as _os
                ch = chunks(0, L, g0)
                ch_at = {a: i for i, (a, b, s0) in enumerate(ch)}
                bi_base = blkc[0]
                nblk = (L + 1 + 63) // 64
                blkc[0] += nblk

                def build_tables(Bk):
                    j = 64 * Bk
                    bi = (bi_base + Bk) % 2
                    nb = min(64, L + 1 - j)
                    ark, wdk = ("AR", bi), ("WD", bi)
                    self.memset(AR[bi][:, :, :, :], 0.0, [ark], eng="dve")
                    ga = max(j, 1)
                    nr = j + nb - ga
                    if nr > 0:
                        P.dma("act", rblk[bi][:, :, 0:nr], self.rbS.ap()[:, :, t0 + ga - 1:t0 + ga - 1 + nr].rearrange("c p t -> p c t"), (), [("rb", bi)])
                        for hp in range(2):
                            self.ts(AR[bi][:, :, ga - j:ga - j + nr, hp], rblk[bi][:, :, 0:nr], self.m01[:, hp:hp + 1], ALU.mult,
                                    [("rb", bi), "m01"], [ark], eng="dve")
                    na = min(j + nb, L) - j
                    if na > 0:
                        P.dma("act", ablk[bi][:, :, 0:na], self.avS.ap()[:, :, t0 + j:t0 + j + na].rearrange("c p t -> p c t"), (), [("ab", bi)])
                        for hp in range(2):
                            self.ts(AR[bi][:, :, 0:na, 2 + hp], ablk[bi][:, :, 0:na], self.m01[:, hp:hp + 1], ALU.mult,
                                    [("ab", bi), "m01"], [ark], eng="dve")
                        P.dma("act", WD[bi][:, :, 0:na], self.dS.ap()[:, :, t0 + j:t0 + j + na].rearrange("c p t -> p c t"), (), [wdk])

                def fill(i):
                    (a, b, s0) = ch[i]
                    n = b - a
                    ta, tb = t0 + a, t0 + b
                    wk_ = [("BK", (s0 + x) % NSL) for x in range(n)]
                    P.dma("sp", BK[2:3, s0:s0 + n, :, 0:64], hp4(self.bkv.ap()[0, ta:tb, :])[0:1], (), wk_)
                    P.dma("sp", BK[3:4, s0:s0 + n, :, 64:128], hp4(self.bkv.ap()[0, ta:tb, :])[1:2], (), wk_)
                    P.dma("sp", BK[4:5, s0:s0 + n, :, 0:64], hp4(self.bkv.ap()[1, ta:tb, :])[0:1], (), wk_)
                    P.dma("sp", BK[5:6, s0:s0 + n, :, 64:128], hp4(self.bkv.ap()[1, ta:tb, :])[1:2], (), wk_)
                    P.dma("sp", ZV[4:6, s0:s0 + n, :].rearrange("r s (g v) -> r s g v", v=64), hp4(self.bkv.ap()[2, ta:tb, :]), (),
                          [("ZVv", (s0 + x) % NSL) for x in range(n)])

                for j in range(L + 1 if not _os.environ.get("RW_SKIPGRP") else 0):
                    jj = j % 64
                    if jj == 0:
                        if j == 0:
                            build_tables(0)
                        if j // 64 + 1 < nblk:
                            build_tables(j // 64 + 1)
                    slot = (g0 + j) % NSL
                    if j in ch_at:
                        i = ch_at[j]
                        if i == 0:
                            fill(0)
                        if i + 1 < len(ch):
                            fill(i + 1)
                    bi = (bi_base + j // 64) % 2
                    for h in range(nsplit):
                        for g_ in range(HH):
                            hg = h * HH + g_
                            self.mm(pzb[h][0:4, g_ * 64:(g_ + 1) * 64], AR[bi][:, hg, jj, :], Sb[sq][:, hg, :], True, True,
                                    [("AR", bi), ("Sb", sq, h)], [("pz", h)])
                        cs = slice(h * W, (h + 1) * W)
                        self.cp(ZV[0:4, slot, cs], pzb[h][0:4, 0:W], [("pz", h)], [("ZV", slot, h)], eng="act")
                    if j < L:
                        ub = j % 2
                        for h in range(nsplit):
                            sl = slice(h * HH, (h + 1) * HH)
                            for g_ in range(HH):
                                hg = h * HH + g_
                                self.mm(pub[h][:, g_ * 64:(g_ + 1) * 64], BK[0:6, slot, hg, :], ZV[0:6, slot, hg * 64:(hg + 1) * 64], True, True,
                                        [("BK", slot), ("ZV", slot, h), ("ZVv", slot)], [("pu", h)] + (["pt"] if 4 + h >= 6 else []))
                            wb_ = WD[bi][:, sl, jj:jj + 1].to_broadcast([128, HH, 64])
                            self.tt(tmp[ub][:, sl, :], ST[sq][:, sl, :], wb_, ALU.mult, [("ST", sq, h), ("WD", bi)], [("tmp", ub, h)], eng=(PE_ if h < nsplit - 1 else "dve"))
                            self.tt(ST[sq][:, sl, :], tmp[ub][:, sl, :], pub[h][:, 0:W].rearrange("p (g v) -> p g v", v=64), ALU.add,
                                    [("tmp", ub, h), ("pu", h)], [("ST", sq, h)])
                            self.cp(Sb[sq][:, sl, :], ST[sq][:, sl, :], [("ST", sq, h)], [("Sb", sq, h)], eng="act")
                    if j >= 1 and ((g0 + j) % 8 == 7 or j == L):
                        ga = max(1, j - ((g0 + j) % 8))
                        n = j + 1 - ga
                        s0 = (g0 + ga) % NSL
                        P.dma("sp", hp4(self.ytok.ap()[t0 + ga - 1:t0 + ga - 1 + n, :]),
                              ZV[0:2, s0:s0 + n, :].rearrange("r s (g v) -> r s g v", v=64),
                              [("ZV", (s0 + x) % NSL, h) for x in range(n) for h in range(nsplit)], ["ytok"])
                G[0] = g0 + L + 1
                import os as _os
                if _os.environ.get("RW_SKIPFIN"):
                    return
                if _os.environ.get("RW_SKIPFIN_P") and q is None:
                    return
                if _os.environ.get("RW_SKIPFIN_S") and q is not None:
                    return
                so = stout[nst[0] % 2]; sok = ("sto", nst[0] % 2); nst[0] += 1
                for half in range(2):
                    for g_ in range(8):
                        hg = half * 8 + g_
                        self.tr(pt[0:64, g_ * 128:(g_ + 1) * 128], ST[sq][:, hg, :], self.ident[:, :], [("ST", sq, hg // HH), "ident"], ["pt"])
                    for bb in range(2):
                        self.cp(so[:, half * 8 + bb * 4:half * 8 + bb * 4 + 4, :],
                                pt[0:64, bb * 512:(bb + 1) * 512].rearrange("p (g k) -> p g k", k=128), ["pt"], [sok],
                                eng=("act" if bb else "dve"))
                dst = self.dout["p_rw"].ap() if q is None else self.dout["s_rw"].ap()[q]
                P.dma("sp", dst.rearrange("(hg hp) v k -> v hg hp k", hp=2), so[:, :, :].rearrange("v g (hp k) -> v g hp k", hp=2), [sok],
                      ["rw_out"], is_output=True)
                P.fence()

            import os
            lim = int(os.environ.get("RW_LIMIT", "-1"))
            if lim < 0:
                run_seq(0, TP, 0, None)
                for q in range(NSEQ):
                    run_seq(TP + 8 * q, 8, (q + 1) % 2, q)
            else:
                run_seq(0, lim, 0, None)
                for q in range(int(os.environ.get("RW_NQ", "0"))):
                    run_seq(TP + 8 * q, 8, (q + 1) % 2, q)
            P.fence()

    def rw_out(self):
        P = self.P
        d = self.din
        K = "rwo"
        with ExitStack() as es:
            F = lambda n, shp: self.sb(es, n, shp, F32)
            lnw, lnb = F("o_lnw", [128, DC]), F("o_lnb", [128, DC])
            self.load_cols(lnw[:, :], d["rw_ln_w"].ap().rearrange("o (c p) -> (o c) p", p=128), DC, K)
            self.load_cols(lnb[:, :], d["rw_ln_b"].ap().rearrange("o (c p) -> (o c) p", p=128), DC, K)
            eps2 = F("o_eps", [128, 1])
            self.memset(eps2[:, :], 64e-5, [K])
            bonesb = self.sb(es, "o_bonesb", [128, 128], BF16)
            self.cp(bonesb[:, :], self.bones[:, :], ["bones"], [K])
            xo = self.sb(es, "o_xo", [128, DC, NT], BF16)
            with ExitStack() as es2:
                F2 = lambda n, shp: self.sb(es2, n, shp, F32)
                ytk = [F2("o_ytk%d" % i, [128, D]) for i in range(2)]
                blocks = [(128 * i, 128) for i in range(17)] + [(2176, 16)]
                for bi_, (t0, n) in enumerate(blocks):
                    u = bi_ % 2
                    P.dma("pool", ytk[u][0:n, :], self.ytok.ap()[t0:t0 + n, :], (), [("ytk", u)])
                    for g in range(4):
                        b = self.bank()
                        for c4 in range(4):
                            c = g * 4 + c4
                            self.tr(self.ps[b][:, c4 * 128:c4 * 128 + n], ytk[u][0:n, c * 128:(c + 1) * 128], self.ident[0:n, 0:n],
                                    [("ytk", u), "ident"], [("ps", b)])
                        self.cp(xo[:, g * 4:g * 4 + 4, t0:t0 + n], self.ps[b][:, :].rearrange("p (c t) -> p c t", c=4)[:, :, 0:n],
                                [("ps", b)], [("xo", g * 4 + c4) for c4 in range(4)], eng=("act" if g % 2 else "dve"))
                P.fence()
            with ExitStack() as es2:
                F2 = lambda n, shp: self.sb(es2, n, shp, F32)
                bon = [F2("o_bon%d" % i, [128, NT]) for i in range(2)]
                g2 = [F2("o_g2%d" % i, [128, NT]) for i in range(2)]
                yc, sq, rs = F2("o_yc", [128, NT]), F2("o_sq", [128, NT]), F2("o_rs", [128, NT])
                for c in range(DC):
                    u = c % 2
                    P.dma("sp", bon[u][:, :], self.bonS.ap()[c], (), [("bon", u)])
                    P.dma("act", g2[u][:, :], self.g2S.ap()[c], (), [("g2", u)])
                    for si, (t0, n) in enumerate(SEGS):
                        b = self.bank()
                        self.mm(self.ps[b][:, 0:n], bonesb[:, :], xo[:, c, t0:t0 + n], True, True, [("xo", c), K], [("ps", b)])
                        self.stt(yc[:, t0:t0 + n], self.ps[b][:, 0:n], -1.0 / 64, xo[:, c, t0:t0 + n], ALU.mult, ALU.add, [("ps", b), ("xo", c)], ["oyc"])
                    self.act(sq[:, :], yc[:, :], AF.Square, ["oyc"], ["osq"])
                    for si, (t0, n) in enumerate(SEGS):
                        b = self.bank()
                        self.mm(self.ps[b][:, 0:n], self.bones[:, :], sq[:, t0:t0 + n], True, True, ["osq", "bones"], [("ps", b)])
                        self.act(rs[:, t0:t0 + n], self.ps[b][:, 0:n], AF.Sqrt, [("ps", b), K], ["ors"], bias=eps2[:, 0:1], scale=1.0 / 64)
                    self.P.op("dve", lambda e: e.reciprocal(out=rs[:, :], in_=rs[:, :]), ["ors"], ["ors"])
                    self.tt(yc[:, :], yc[:, :], rs[:, :], ALU.mult, ["oyc", "ors"], ["oyc"])
                    self.ts(yc[:, :], yc[:, :], lnw[:, c:c + 1], ALU.mult, ["oyc", K], ["oyc"], s2=lnb[:, c:c + 1], op1=ALU.add)
                    self.tt(yc[:, :], yc[:, :], bon[u][:, :], ALU.add, ["oyc", ("bon", u)], ["oyc"])
                    self.tt(xo[:, c, :], yc[:, :], g2[u][:, :], ALU.mult, ["oyc", ("g2", u)], [("xo", c)])
                P.fence()
            self.out_proj_res(xo, lambda kt, si: ("xo", kt), DC, d["rw_w_o"].ap(), tag="wo2")

    def final_out(self):
        P = self.P
        lnw = self.c_ln[:, 64:80]
        tiles = [(128 * i, 128) for i in range(16)] + [(2048, 16), (2064, 128)]
        with ExitStack() as es:
            F = lambda n, shp: self.sb(es, n, shp, F32)
            hs = [F("z_hs%d" % i, [128, DC, 128]) for i in range(2)]
            sq = [F("z_sq%d" % i, [128, 128]) for i in range(2)]
            rs = [F("z_rs%d" % i, [128, 128]) for i in range(2)]
            xn = [F("z_xn%d" % i, [128, DC, 128]) for i in range(2)]
            tok = [F("z_tok%d" % i, [128, D]) for i in range(2)]
            for ti, (t0, n) in enumerate(tiles):
                u = ti % 2
                P.dma("sp", hs[u][:, :, 0:n], self.hT.ap()[:, :, t0:t0 + n].rearrange("c p t -> p c t"), (), [("zhs", u)])
                b = self.bank()
                for c in range(DC):
                    self.act(sq[c % 2][:, 0:n], hs[u][:, c, 0:n], AF.Square, [("zhs", u)], [("zsq", c % 2)])
                    self.mm(self.ps[b][:, 0:n], self.ones[:, :], sq[c % 2][:, 0:n], c == 0, c == DC - 1, [("zsq", c % 2), "ones"], [("ps", b)])
                self.act(rs[u][:, 0:n], self.ps[b][:, 0:n], AF.Sqrt, [("ps", b), "epsD"], [("zrs", u)], bias=self.epsD[:, 0:1], scale=1.0 / D)
                self.P.op("dve", lambda e, r_=rs[u], n=n: e.reciprocal(out=r_[:, 0:n], in_=r_[:, 0:n]), [("zrs", u)], [("zrs", u)])
                for c in range(DC):
                    self.stt(xn[u][:, c, 0:n], hs[u][:, c, 0:n], lnw[:, c:c + 1], rs[u][:, 0:n], ALU.mult, ALU.mult, [("zhs", u), ("zrs", u), "cols"], [("zxn", u)])
                for g in range(4):
                    b = self.bank()
                    for c4 in range(4):
                        c = g * 4 + c4
                        self.tr(self.ps[b][0:n, c4 * 128:(c4 + 1) * 128], xn[u][:, c, 0:n], self.ident[:, :], [("zxn", u), "ident"], [("ps", b)])
                    self.cp(tok[u][0:n, g * 512:(g + 1) * 512], self.ps[b][0:n, :], [("ps", b)], [("ztok", u)], eng=("act" if g % 2 else "dve"))
                if t0 == 0:
                    P.dma("sp", self.dout["y_p"].ap()[0:112, :], tok[u][16:128, :], [("ztok", u)], ["y_p"], is_output=True)
                elif t0 < TP:
                    P.dma("sp", self.dout["y_p"].ap()[t0 - 16:t0 - 16 + n, :], tok[u][0:n, :], [("ztok", u)], ["y_p"], is_output=True)
                else:
                    P.dma("sp", self.dout["y_s"].ap()[:, :], tok[u][0:n, :], [("ztok", u)], ["y_s"], is_output=True)
            P.fence()

class Builder(BuilderBase, S5Mixin, HgMixin, FfnMixin, RwMixin, RwScanMixin):
    def build(self):
        if self.upto == "scanonly":
            self.inp("st_rw", [NSEQ, 32, 64, 64])
            self.outp("p_rw", [32, 64, 64]); self.outp("s_rw", [NSEQ, 32, 64, 64])
            self.scr_in = {"rbS", "avS", "dS", "bkv"}
            self.debug = {"ytok"}
            self.consts()
            self.rw_declare()
            self.rw_scan()
            return self.finish()
        self.declare()
        self.consts()
        self.param_cols()
        self.stage_in()
        if self.upto == "in":
            return self.finish()
        es0 = ExitStack()
        xnT = self.sb(es0, "xnT", [128, DC, NT], BF16)
        self.norm(xnT, self.c_ln[:, 0:16])
        self.l0_inproj(xnT)
        es0.close()
        if self.upto == "inproj":
            return self.finish()
        es1 = ExitStack()
        yT = self.sb(es1, "yT", [128, DC, NT], BF16)
        self.s5_setup()
        self.s5_main(yT)
        self.es_s5.close()
        self.s5_glu(yT)
        if self.upto != "s5":
            self.hg_main(yT)
        if self.upto == "hg":
            dbg = self.scratch("dbg_yb", [8, 128, NT], BF16)
            self.P.dma("sp", dbg.ap().rearrange("c p t -> p c t"), yT[:, 8:16, :], [("yT", j, si) for j in range(8, 16) for si in range(5)], ["dbg"], is_output=True)
            return self.finish()
        if self.upto not in ("s5", "hg"):
            self.out_proj_res(yT, lambda kt, si: ("yT", kt, si), DC, self.din["ev_w_out"].ap())
            es1.close()
            es3 = ExitStack()
            xnT = self.sb(es3, "xnT", [128, DC, NT], BF16)
            self.norm(xnT, self.c_ln[:, 32:48])
            self.ffn(0, xnT)
            es3.close()
            self.ffn_down(0)
        if self.upto == "l0":
            return self.finish()
        if self.upto not in ("s5", "hg"):
            self.layer1()
            return self.finish()
        if self.upto == "s5":
            dbg = self.scratch("dbg_ya", [8, 128, NT], BF16)
            self.P.dma("sp", dbg.ap().rearrange("c p t -> p c t"), yT[:, 0:8, :], [("yT", j, si) for j in range(8) for si in range(5)], ["dbg"], is_output=True)
            return self.finish()
        return self.finish()

    def layer1(self):
        self.rw_declare()
        es3 = ExitStack()
        xnT = self.sb(es3, "xnT", [128, DC, NT], BF16)
        es4 = ExitStack()
        xn32 = self.sb(es4, "xn32", [128, DC, 144], F32)
        self.norm(xnT, self.c_ln[:, 16:32], out32=xn32)
        self.rw_shift_out(xn32)
        es4.close()
        self.rw_proj(xnT)
        es3.close()
        if "dbg_xj" in self.debug:
            return
        self.rw_post()
        if self.upto == "rwpost":
            return
        self.rw_scan()
        if self.upto == "rwscan":
            return
        self.rw_out()
        if self.upto == "l1a":
            return
        es3 = ExitStack()
        xnT = self.sb(es3, "xnT", [128, DC, NT], BF16)
        self.norm(xnT, self.c_ln[:, 48:64])
        self.ffn(1, xnT)
        es3.close()
        self.ffn_down(1)
        self.final_out()

    def finish(self):
        cnt = self.P.emit()
        return self.nc, cnt


_CACHE = {}


def kernel(**inputs):
    n_cores = 8
    if "nc" not in _CACHE:
        B = Builder(upto="all")
        nc, _ = B.build()
        _CACHE["nc"] = nc
        _CACHE["names"] = list(B.din)
    nc = _CACHE["nc"]
    names = _CACHE["names"]
    shared = None
    in_maps = []
    for c in range(n_cores):
        m = _prep_inputs(inputs, c)
        in_maps.append({k: m[k] for k in names})
    res = run_bass_kernel_spmd(nc, in_maps, core_ids=list(range(n_cores))).results
    f = np.float32
    st = lambda k, cores: np.stack([np.asarray(res[c][k], dtype=f) for c in cores])
    cat = lambda k: np.concatenate([np.asarray(res[c][k], dtype=f) for c in range(n_cores)], axis=0)
    p = range(4)
    y_prompt = st("y_p", p)
    y_sample = cat("y_s").reshape(128, 8, D)
    p_s5r = st("p_s5r", p)[None]
    p_s5i = st("p_s5i", p)[None]
    p_hg = st("p_hg", p)[None]
    p_rw = st("p_rw", p)[None]
    p_sh = st("p_sh", p).reshape(1, 4, D)
    p_cv = np.stack([np.asarray(res[c]["p_cv"], dtype=f) for c in p], axis=1)
    s_s5r = cat("s_s5r")[None]
    s_s5i = cat("s_s5i")[None]
    s_hg = cat("s_hg")[None]
    s_rw = cat("s_rw")[None]
    s_sh = cat("s_sh")[None]
    s_cv = np.concatenate([np.asarray(res[c]["s_cv"], dtype=f) for c in range(n_cores)], axis=1)
    return (y_prompt, y_sample, p_s5r, p_s5i, p_hg, p_rw, p_sh, p_cv, s_s5r, s_s5i, s_hg, s_rw, s_sh, s_cv)
```

```python
import math
from contextlib import ExitStack
import numpy as np
import concourse.bass as bass
import concourse.mybir as mybir
from concourse.bass_utils import run_bass_kernel_spmd

F32 = mybir.dt.float32
BF16 = mybir.dt.bfloat16
AF = mybir.ActivationFunctionType
ALU = mybir.AluOpType
AX = mybir.AxisListType

D = 2048
DC = 16
TP = 2064
NSEQ = 16
TS = 128
NT = TP + TS
SEGS = [(0, 512), (512, 512), (1024, 512), (1536, 512), (2048, 144)]
EVEN_IN = 5120
D_FF = 5632
FC = 44
EPS = 1e-6

COMPUTE = ("pe", "act", "dve", "pool")
NRING = 8


class Op:
    __slots__ = ("eng", "fn", "deps", "is_dma", "ring", "ring_val", "sig", "sigval", "idx")


class Prog:
    def __init__(self, nc):
        self.nc = nc
        self.ops = []
        self.last_w = {}
        self.readers = {}
        self.ring_pos = {"sp": 0, "act": 0, "pool": 0}
        self.ring_cnt = {}
        self.ring_last = {}
        self.out_dmas = []
        self.last_on = {}

    def _add(self, eng, fn, reads, writes, is_dma, extra_deps=()):
        op = Op()
        op.eng, op.fn, op.is_dma = eng, fn, is_dma
        op.idx = len(self.ops)
        op.sig = False
        op.sigval = None
        deps = set(extra_deps)
        for k in reads:
            w = self.last_w.get(k)
            if w is not None:
                deps.add(w)
        for k in writes:
            w = self.last_w.get(k)
            if w is not None:
                deps.add(w)
            for r in self.readers.get(k, ()):
                deps.add(r)
        if is_dma:
            q = eng
            slot = self.ring_pos[q] % NRING
            self.ring_pos[q] += 1
            key = (q, slot)
            prev = self.ring_last.get(key)
            if prev is not None:
                deps.add(prev)
            self.ring_cnt[key] = self.ring_cnt.get(key, 0) + 16
            op.ring = key
            op.ring_val = self.ring_cnt[key]
            self.ring_last[key] = op.idx
        else:
            op.ring = None
            op.ring_val = None
            if fn is not None:
                self.last_on[eng] = op.idx
        best = {}
        red = set()
        for dd in deps:
            dop = self.ops[dd]
            if dop.is_dma or dop.fn is None:
                red.add(dd)
            else:
                if best.get(dop.eng, -1) < dd:
                    best[dop.eng] = dd
        red.update(best.values())
        op.deps = red
        for k in writes:
            self.last_w[k] = op.idx
            self.readers[k] = []
        for k in reads:
            if k not in writes:
                self.readers.setdefault(k, []).append(op.idx)
        self.ops.append(op)
        return op

    def op(self, eng, fn, reads=(), writes=()):
        return self._add(eng, fn, tuple(reads), tuple(writes), False)

    def dma(self, q, out, in_, reads=(), writes=(), is_output=False, **kw):
        def fn(e):
            return e.dma_start(out=out, in_=in_, **kw)
        o = self._add(q, fn, tuple(reads), tuple(writes), True)
        if is_output:
            self.out_dmas.append(o.idx)
        return o

    def fence(self):
        deps = set(self.last_on.values()) | set(self.ring_last.values())
        for e in ("pe", "act", "dve", "pool", "sp"):
            self._add(e, None, (), (), False, extra_deps=deps)
        self.last_w = {}
        self.readers = {}

    def emit(self):
        nc = self.nc
        ops = self.ops
        for o in ops:
            for d in o.deps:
                dop = ops[d]
                if not dop.is_dma and dop.fn is not None:
                    if dop.eng == "pe" and o.eng == "pe" and not o.is_dma and o.fn is not None:
                        continue
                    dop.sig = True
        cnt = {e: 0 for e in COMPUTE}
        for o in ops:
            if not o.is_dma and o.sig:
                assert o.fn is not None
                cnt[o.eng] += 1
                o.sigval = cnt[o.eng]
        sems = {}
        for e in COMPUTE:
            sems[e] = nc.alloc_semaphore("c_" + e)
        for q in ("sp", "act", "pool"):
            for s in range(NRING):
                sems[(q, s)] = nc.alloc_semaphore("d_%s%d" % (q, s))
        streams = {e: [] for e in ("pe", "act", "dve", "pool", "sp")}
        for o in ops:
            streams[o.eng].append(o)
        final_waits = [(ops[i].ring, ops[i].ring_val) for i in self.out_dmas]

        def run(engname, e):
            waited = {}
            for o in streams[engname]:
                need = {}
                for d in o.deps:
                    dop = ops[d]
                    if dop.fn is None:
                        continue
                    if dop.is_dma:
                        k, v = dop.ring, dop.ring_val
                    else:
                        if dop.eng == "pe" and engname == "pe" and not o.is_dma and o.fn is not None:
                            continue
                        k, v = dop.eng, dop.sigval
                    if need.get(k, 0) < v:
                        need[k] = v
                for k, v in need.items():
                    if waited.get(k, 0) >= v:
                        continue
                    e.wait_ge(sems[k], v)
                    waited[k] = v
                if o.fn is None:
                    continue
                ins = o.fn(e)
                if o.is_dma:
                    ins.then_inc(sems[o.ring], 16)
                elif o.sig:
                    ins.then_inc(sems[o.eng], 1)
            if engname == "sp":
                for k, v in final_waits:
                    if waited.get(k, 0) >= v:
                        continue
                    e.wait_ge(sems[k], v)
                    waited[k] = v

        with nc.Block() as block:
            @block.sync
            def _(e):
                run("sp", e)

            @block.tensor
            def _(e):
                run("pe", e)

            @block.vector
            def _(e):
                run("dve", e)

            @block.scalar
            def _(e):
                run("act", e)

            @block.gpsimd
            def _(e):
                run("pool", e)
        return cnt


class BuilderBase:
    def __init__(self, upto="all", debug=()):
        self.nc = nc = bass.Bass("TRN2", target_bir_lowering=False)
        self.P = Prog(nc)
        self.upto = upto
        self.debug = set(debug)
        self.bank_i = 0
        self.din = {}
        self.dout = {}
        self.psbig = [nc.alloc_psum_tensor("psb%d" % i, [128, 1024], F32) for i in range(4)]
        self.ps = [self.psbig[i // 2][:, (i % 2) * 512:(i % 2) * 512 + 512] for i in range(8)]

    def inp(self, name, shape):
        t = self.nc.dram_tensor(name, list(shape), F32, kind="ExternalInput")
        self.din[name] = t
        return t

    def outp(self, name, shape):
        t = self.nc.dram_tensor(name, list(shape), F32, kind="ExternalOutput")
        self.dout[name] = t
        return t

    def scratch(self, name, shape, dt=F32):
        if name in getattr(self, "scr_in", ()):
            t = self.nc.dram_tensor(name, list(shape), dt, kind="ExternalInput")
            self.din[name] = t
            return t
        kind = "ExternalOutput" if name in self.debug else "Internal"
        t = self.nc.dram_tensor(name, list(shape), dt, kind=kind)
        if name in self.debug:
            self.dout[name] = t
        return t

    def bank(self):
        pool = getattr(self, "bank_pool", None) or list(range(8))
        b = pool[self.bank_i % len(pool)]
        self.bank_i += 1
        return b

    def sb(self, es, name, shape, dt=F32):
        self.uid = getattr(self, "uid", 0) + 1
        return es.enter_context(self.nc.sbuf_tensor("%s_%d" % (name, self.uid), list(shape), dt))

    def mm(self, out, lhsT, rhs, start, stop, r, w):
        self.P.op("pe", lambda e: e.matmul(out, lhsT=lhsT, rhs=rhs, start=start, stop=stop), r, w)

    def tr(self, out, in_, ident, r, w):
        self.P.op("pe", lambda e: e.transpose(out, in_, ident), r, w)

    def act(self, out, in_, func, r, w, bias=None, scale=None):
        kw = {}
        if bias is not None:
            kw["bias"] = bias
        if scale is not None:
            kw["scale"] = scale
        self.P.op("act", lambda e: e.activation(out=out, in_=in_, func=func, **kw), r, w)

    def tt(self, out, a, b, op, r, w, eng="dve"):
        self.P.op(eng, lambda e: e.tensor_tensor(out=out, in0=a, in1=b, op=op), r, w)

    def ts(self, out, a, s1, op0, r, w, s2=None, op1=None, eng="dve"):
        if op1 is None:
            self.P.op(eng, lambda e: e.tensor_scalar(out=out, in0=a, scalar1=s1, scalar2=None, op0=op0), r, w)
        else:
            self.P.op(eng, lambda e: e.tensor_scalar(out=out, in0=a, scalar1=s1, scalar2=s2, op0=op0, op1=op1), r, w)

    def stt(self, out, a, s, b, op0, op1, r, w):
        self.P.op("dve", lambda e: e.scalar_tensor_tensor(out=out, in0=a, scalar=s, in1=b, op0=op0, op1=op1), r, w)

    def cp(self, out, in_, r, w, eng="dve"):
        if eng == "act":
            self.P.op("act", lambda e: e.copy(out=out, in_=in_), r, w)
        else:
            self.P.op(eng, lambda e: e.tensor_copy(out=out, in_=in_), r, w)

    def memset(self, ap, v, w, eng="dve"):
        self.P.op(eng, lambda e: e.memset(ap, v), (), w)

    def scan(self, out, d0, d1, init, r, w):
        self.P.op("dve", lambda e: e.tensor_tensor_scan(out=out, data0=d0, data1=d1, initial=init, op0=ALU.mult, op1=ALU.add), r, w)

    def consts(self):
        nc, P = self.nc, self.P
        a = nc.alloc_sbuf_tensor
        self.ident = a("ident", [128, 128], F32)
        self.identb = a("identb", [128, 128], BF16)
        self.ones = a("ones", [128, 128], F32)
        self.bones = a("bones", [128, 128], F32)
        self.epsD = a("epsD", [128, 1], F32)
        self.m01 = a("m01", [128, 2], F32)
        self.tril = a("tril", [64, 64], F32)
        self.lcst = [a("lcst0", [128, 128], F32), a("lcst1", [128, 128], F32)]
        self.lc_i = 0
        P.op("pool", lambda e: e.memset(self.ident[:], 0.0), (), ["ident"])
        P.op("pool", lambda e: e.affine_select(out=self.ident[:], in_=self.ident[:], pattern=[[-1, 128]],
                                               compare_op=ALU.not_equal, fill=1.0, base=0, channel_multiplier=1),
             ["ident"], ["ident"])
        self.cp(self.identb[:], self.ident[:], ["ident"], ["identb"])
        self.memset(self.ones[:], 1.0, ["ones"])
        self.memset(self.epsD[:], EPS, ["epsD"])
        self.memset(self.bones[:], 0.0, ["bones"])
        self.memset(self.bones[0:64, 0:64], 1.0, ["bones"])
        self.memset(self.bones[64:128, 64:128], 1.0, ["bones"])
        self.memset(self.m01[:], 0.0, ["m01"])
        self.memset(self.m01[0:64, 0:1], 1.0, ["m01"])
        self.memset(self.m01[64:128, 1:2], 1.0, ["m01"])
        P.op("pool", lambda e: e.memset(self.tril[:], 1.0), (), ["tril"])
        P.op("pool", lambda e: e.affine_select(out=self.tril[:], in_=self.tril[:], pattern=[[1, 64]],
                                               compare_op=ALU.is_ge, fill=0.0, base=0, channel_multiplier=-1),
             ["tril"], ["tril"])

    def load_cols(self, dst, dram2d, R, key):
        P = self.P
        r0 = 0
        while r0 < R:
            n = min(128, R - r0)
            i = self.lc_i % 2
            self.lc_i += 1
            st = self.lcst[i]
            P.dma("sp", st[0:n, :], dram2d[r0:r0 + n, :], (), [("lcst", i)])
            b = self.bank()
            self.tr(self.ps[b][:, 0:n], st[0:n, :], self.ident[0:n, 0:n], [("lcst", i), "ident"], [("ps", b)])
            self.cp(dst[:, r0:r0 + n], self.ps[b][:, 0:n], [("ps", b)], [key])
            r0 += n

    def stage_in(self):
        P = self.P
        xp, meta, xs = self.din["xp"], self.din["meta"], self.din["xs"]
        hT = self.hT
        tiles = [(0, 128, [(meta.ap()[0:16, :], 0, 16), (xp.ap()[0:112, :], 16, 112)])]
        for i in range(1, 16):
            tiles.append((128 * i, 128, [(xp.ap()[128 * i - 16:128 * i + 112, :], 0, 128)]))
        tiles.append((2048, 16, [(xp.ap()[2032:2048, :], 0, 16)]))
        tiles.append((2064, 128, [(xs.ap()[:, :], 0, 128)]))
        with ExitStack() as es:
            tok = [self.sb(es, "tok%d" % i, [128, D], F32) for i in range(2)]
            hTt = [self.sb(es, "hTt%d" % i, [128, DC, 128], F32) for i in range(2)]
            for ti, (t0, n, srcs) in enumerate(tiles):
                tk = tok[ti % 2]
                ht = hTt[ti % 2]
                for (src, r0, nr) in srcs:
                    P.dma("sp", tk[r0:r0 + nr, :], src, (), [("tok", ti % 2)])
                for g in range(4):
                    b = self.bank()
                    for c in range(4):
                        cc = g * 4 + c
                        self.tr(self.ps[b][:, c * 128:c * 128 + n], tk[0:n, cc * 128:(cc + 1) * 128],
                                self.ident[0:n, 0:n], [("tok", ti % 2), "ident"], [("ps", b)])
                    src = self.ps[b][:, :].rearrange("p (c t) -> p c t", c=4)[:, :, 0:n]
                    self.cp(ht[:, g * 4:g * 4 + 4, 0:n], src, [("ps", b)], [("hTt", ti % 2)],
                            eng=("act" if g % 2 else "dve"))
                P.dma("sp", hT.ap()[:, :, t0:t0 + n].rearrange("c p t -> p c t"), ht[:, :, 0:n],
                      [("hTt", ti % 2)], ["hT"])
            P.fence()

    def norm(self, xnT, lnw, out32=None):
        P = self.P
        hT = self.hT
        with ExitStack() as es:
            hseg = [self.sb(es, "hseg%d" % i, [128, DC, 512], F32) for i in range(2)]
            sq = [self.sb(es, "sq%d" % i, [128, 512], F32) for i in range(2)]
            rs = [self.sb(es, "rs%d" % i, [128, 512], F32) for i in range(2)]
            for si, (t0, n) in enumerate(SEGS):
                hs = hseg[si % 2]
                hk = ("hseg", si % 2)
                P.dma("sp", hs[:, :, 0:n], hT.ap()[:, :, t0:t0 + n].rearrange("c p t -> p c t"), ["hT"], [hk])
                b = self.bank()
                for c in range(DC):
                    q = sq[c % 2]
                    self.act(q[:, 0:n], hs[:, c, 0:n], AF.Square, [hk], [("sq", c % 2)])
                    self.mm(self.ps[b][:, 0:n], self.ones[:, :], q[:, 0:n], c == 0, c == DC - 1,
                            [("sq", c % 2), "ones"], [("ps", b)])
                r = rs[si % 2]
                rk = ("rs", si % 2)
                self.act(r[:, 0:n], self.ps[b][:, 0:n], AF.Sqrt, [("ps", b), "epsD"], [rk],
                         bias=self.epsD[:, 0:1], scale=1.0 / D)
                self.P.op("dve", lambda e, r=r, n=n: e.reciprocal(out=r[:, 0:n], in_=r[:, 0:n]), [rk], [rk])
                for c in range(DC):
                    self.stt(xnT[:, c, t0:t0 + n], hs[:, c, 0:n], lnw[:, c:c + 1], r[:, 0:n], ALU.mult, ALU.mult,
                             [hk, rk, "cols"], [("xnT", c, si)])
                    if out32 is not None and si == len(SEGS) - 1:
                        self.stt(out32[:, c, 0:n], hs[:, c, 0:n], lnw[:, c:c + 1], r[:, 0:n], ALU.mult, ALU.mult,
                                 [hk, rk, "cols"], ["out32"])
            P.fence()

    def proj(self, es, xT, xkey, KT, w2d, col0, ncols, evac, wg=4, kparts=128, tag="w"):
        P = self.P
        ntile = (ncols + 127) // 128
        nbuf = 2
        wb = [self.sb(es, "%sb%d" % (tag, i), [128, KT, wg * 128], BF16) for i in range(nbuf)]
        gi = 0
        for j0 in range(0, ntile, wg):
            nj = min(wg, ntile - j0)
            w = wb[gi % nbuf]
            wk = (tag, gi % nbuf)
            gi += 1
            c_lo = j0 * 128
            c_hi = min(ncols, (j0 + nj) * 128)
            src = w2d[:, col0 + c_lo:col0 + c_hi].rearrange("(kt p) n -> p kt n", p=kparts)
            P.dma("pool", w[0:kparts, :, 0:c_hi - c_lo], src, (), [wk])
            for jj in range(nj):
                mw = min(128, ncols - (j0 + jj) * 128)
                for si, (t0, n) in enumerate(SEGS):
                    b = self.bank()
                    for kt in range(KT):
                        rhs = xT(kt, t0, n) if callable(xT) else xT[0:kparts, kt, t0:t0 + n]
                        self.mm(self.ps[b][0:mw, 0:n], w[0:kparts, kt, jj * 128:jj * 128 + mw],
                                rhs, kt == 0, kt == KT - 1, [wk, xkey(kt, si)], [("ps", b)])
                    evac(j0 + jj, si, t0, n, self.ps[b][0:mw, 0:n], ("ps", b))

    def declare(self):
        i = self.inp
        i("xp", [2048, D]); i("meta", [16, D]); i("xs", [TS, D])
        i("st_s5r", [NSEQ, 64, 64]); i("st_s5i", [NSEQ, 64, 64])
        i("st_hg", [NSEQ, 8, 128, 128]); i("st_rw", [NSEQ, 32, 64, 64])
        i("st_sh", [NSEQ, D]); i("st_cv", [2, NSEQ, 2, D_FF])
        i("ln_mix", [2, D]); i("ln_ffn", [2, D]); i("ln_final", [1, D])
        i("ev_w_in", [D, EVEN_IN]); i("ev_w_out", [D, D])
        i("s5_lam_re", [64, 64]); i("s5_lam_im", [64, 64]); i("s5_log_step", [1, 64])
        i("s5_b_re", [64, 64, 16]); i("s5_b_im", [64, 64, 16])
        i("s5_c_re", [64, 16, 64]); i("s5_c_im", [64, 16, 64]); i("s5_d", [64, 16])
        i("s5_w_glu", [1024, 1024]); i("hg_lb", [2, 1024]); i("hg_norm_w", [1, 128])
        i("rw_mu", [6, D]); i("rw_w0", [1, D]); i("rw_w1", [D, 96]); i("rw_w2", [96, D])
        i("rw_a0", [1, D]); i("rw_a1", [D, 96]); i("rw_a2", [96, D])
        i("rw_g1", [D, 256]); i("rw_g2", [256, D])
        i("rw_k_k", [1, D]); i("rw_k_a", [1, D]); i("rw_r_k", [1, D])
        i("rw_w_r", [D, D]); i("rw_w_k", [D, D]); i("rw_w_v", [D, D]); i("rw_w_o", [D, D])
        i("rw_ln_w", [1, D]); i("rw_ln_b", [1, D])
        i("ffn_w_in", [2, D, 2 * D_FF]); i("ffn_conv_w", [2, 3, D_FF]); i("ffn_conv_b", [2, D_FF])
        i("ffn_w_down", [2, D_FF, D])
        o = self.outp
        o("y_p", [2048, D]); o("y_s", [TS, D])
        o("p_s5r", [64, 64]); o("p_s5i", [64, 64]); o("p_hg", [8, 128, 128]); o("p_rw", [32, 64, 64])
        o("p_sh", [1, D]); o("p_cv", [2, 2, D_FF])
        o("s_s5r", [NSEQ, 64, 64]); o("s_s5i", [NSEQ, 64, 64]); o("s_hg", [NSEQ, 8, 128, 128])
        o("s_rw", [NSEQ, 32, 64, 64]); o("s_sh", [NSEQ, D]); o("s_cv", [2, NSEQ, 2, D_FF])
        self.hT = self.scratch("hT", [DC, 128, NT])
        self.zT = self.scratch("zT", [40, 128, NT])
        self.gS = self.scratch("gS", [len(SEGS), 128, FC, 512], BF16)

    def param_cols(self):
        a = self.nc.alloc_sbuf_tensor
        d = self.din
        self.c_ln = a("c_ln", [128, 5 * DC], F32)
        self.load_cols(self.c_ln[:, 0:32], d["ln_mix"].ap().rearrange("l (c p) -> (l c) p", p=128), 32, "cols")
        self.load_cols(self.c_ln[:, 32:64], d["ln_ffn"].ap().rearrange("l (c p) -> (l c) p", p=128), 32, "cols")
        self.load_cols(self.c_ln[:, 64:80], d["ln_final"].ap().rearrange("l (c p) -> (l c) p", p=128), 16, "cols")

    def l0_inproj(self, xnT):
        P = self.P
        with ExitStack() as es:
            zrow = [self.sb(es, "zrow%d" % i, [128, NT], F32) for i in range(2)]

            def evac(j, si, t0, n, ps, pk):
                z = zrow[j % 2]
                self.cp(z[:, t0:t0 + n], ps, [pk], [("zrow", j % 2, si)], eng=("act" if si % 2 else "dve"))
                if si == len(SEGS) - 1:
                    P.dma("sp", self.zT.ap()[j], z[:, :], [("zrow", j % 2, s) for s in range(len(SEGS))], ["zT"])

            self.proj(es, xnT, lambda kt, si: ("xnT", kt, si), DC, self.din["ev_w_in"].ap(), 0, EVEN_IN, evac)
            P.fence()


def _prep_inputs(inputs, core):
    b = core % 4
    f = lambda a: np.ascontiguousarray(np.asarray(a, dtype=np.float32))
    sl = slice(NSEQ * core, NSEQ * core + NSEQ)
    m = {
        "xp": f(inputs["x_prompt"][b]), "meta": f(inputs["meta_tokens"]),
        "xs": f(np.asarray(inputs["x_sample"])[sl].reshape(TS, D)),
        "st_s5r": f(np.asarray(inputs["state_s5_re"])[0, sl]), "st_s5i": f(np.asarray(inputs["state_s5_im"])[0, sl]),
        "st_hg": f(np.asarray(inputs["state_hgrn"])[0, sl]), "st_rw": f(np.asarray(inputs["state_rwkv"])[0, sl]),
        "st_sh": f(np.asarray(inputs["state_shift"])[0, sl]), "st_cv": f(np.asarray(inputs["state_conv"])[:, sl]),
        "ln_mix": f(inputs["ln_mix"]), "ln_ffn": f(inputs["ln_ffn"]), "ln_final": f(np.asarray(inputs["ln_final"]).reshape(1, D)),
        "ev_w_in": f(inputs["ev_w_in"][0]), "ev_w_out": f(inputs["ev_w_out"][0]),
        "s5_lam_re": f(inputs["s5_lam_re"][0]), "s5_lam_im": f(inputs["s5_lam_im"][0]),
        "s5_log_step": f(np.asarray(inputs["s5_log_step"]).reshape(1, 64)),
        "s5_b_re": f(inputs["s5_b_re"][0]), "s5_b_im": f(inputs["s5_b_im"][0]),
        "s5_c_re": f(inputs["s5_c_re"][0]), "s5_c_im": f(inputs["s5_c_im"][0]), "s5_d": f(inputs["s5_d"][0]),
        "s5_w_glu": f(inputs["s5_w_glu"][0]), "hg_lb": f(inputs["hg_lb"]), "hg_norm_w": f(inputs["hg_norm_w"]),
        "rw_mu": f(inputs["rw_mu"][0]), "rw_w0": f(inputs["rw_w0"]), "rw_w1": f(inputs["rw_w1"][0]),
        "rw_w2": f(inputs["rw_w2"][0]), "rw_a0": f(inputs["rw_a0"]), "rw_a1": f(inputs["rw_a1"][0]),
        "rw_a2": f(inputs["rw_a2"][0]), "rw_g1": f(inputs["rw_g1"][0]), "rw_g2": f(inputs["rw_g2"][0]),
        "rw_k_k": f(inputs["rw_k_k"]), "rw_k_a": f(inputs["rw_k_a"]),
        "rw_r_k": f(np.asarray(inputs["rw_r_k"]).reshape(1, D)),
        "rw_w_r": f(inputs["rw_w_r"][0]), "rw_w_k": f(inputs["rw_w_k"][0]), "rw_w_v": f(inputs["rw_w_v"][0]),
        "rw_w_o": f(inputs["rw_w_o"][0]), "rw_ln_w": f(inputs["rw_ln_w"]), "rw_ln_b": f(inputs["rw_ln_b"]),
        "ffn_w_in": f(inputs["ffn_w_in"]), "ffn_conv_w": f(inputs["ffn_conv_w"]), "ffn_conv_b": f(inputs["ffn_conv_b"]),
        "ffn_w_down": f(inputs["ffn_w_down"]),
    }
    return m


TWO_PI = 2.0 * math.pi
C1_2PI = 6.28125
C2_2PI = TWO_PI - C1_2PI
MAGIC = 12582912.0


class S5Mixin:
    def s5_setup(self):
        nc, P = self.nc, self.P
        self.es_s5 = ExitStack()
        a = lambda n, shp, dt: self.sb(self.es_s5, n, shp, dt)
        d = self.din
        K = "s5c"
        self.s5_mag = a("s5_mag", [128, 32], F32)
        self.s5_cth = a("s5_cth", [128, 32], F32)
        self.s5_sth = a("s5_sth", [128, 32], F32)
        self.s5_Bre = a("s5_Bre", [128, 32, 128], BF16)
        self.s5_Bim = a("s5_Bim", [128, 32, 128], BF16)
        self.s5_Cre = a("s5_Cre", [128, 32, 128], BF16)
        self.s5_Cim = a("s5_Cim", [128, 32, 128], BF16)
        self.s5_dD = a("s5_dD", [128, 8, 128], BF16)
        self.s5_ahr = a("s5_ahr", [128, NSEQ, 32], F32)
        self.s5_ahi = a("s5_ahi", [128, NSEQ, 32], F32)
        with ExitStack() as es:
            T = lambda n, shp=[128, 32]: self.sb(es, n, shp, F32)
            lr, li, ls, dt, th, k, r, r2, msk = (T(n) for n in ("s_lr", "s_li", "s_ls", "s_dt", "s_th", "s_k", "s_r", "s_r2", "s_msk"))
            ar, ai, am1, den, zr, zi, t1, t2 = (T(n) for n in ("s_ar", "s_ai", "s_am1", "s_den", "s_zr", "s_zi", "s_t1", "s_t2"))
            rowv = lambda t: t.ap().rearrange("(i gl) p -> i (gl p)", gl=2)
            self.load_cols(lr[:, :], rowv(d["s5_lam_re"]), 32, K)
            self.load_cols(li[:, :], rowv(d["s5_lam_im"]), 32, K)
            st = self.lcst[0]
            P.dma("sp", st[0:1, 0:64], d["s5_log_step"].ap(), (), [("lcst", 0)])
            b = self.bank()
            self.mm(self.ps[b][:, 0:64], self.ones[0:1, :], st[0:1, 0:64], True, True, [("lcst", 0), "ones"], [("ps", b)])
            self.ts(t1[:, :], self.ps[b][:, 0:64:2], self.m01[:, 0:1], ALU.mult, [("ps", b), "m01"], [K])
            self.stt(ls[:, :], self.ps[b][:, 1:64:2], self.m01[:, 1:2], t1[:, :], ALU.mult, ALU.add, [("ps", b), "m01", K], [K])
            R_, W_ = [K], [K]
            self.ts(lr[:, :], lr[:, :], -1e-4, ALU.min, R_, W_)
            self.act(dt[:, :], ls[:, :], AF.Exp, R_, W_)
            self.tt(t1[:, :], lr[:, :], dt[:, :], ALU.mult, R_, W_)
            self.act(self.s5_mag[:, :], t1[:, :], AF.Exp, R_, W_)
            self.tt(th[:, :], li[:, :], dt[:, :], ALU.mult, R_, W_)
            self.ts(k[:, :], th[:, :], 1.0 / TWO_PI, ALU.mult, R_, W_)
            self.ts(k[:, :], k[:, :], MAGIC, ALU.add, R_, W_)
            self.ts(k[:, :], k[:, :], -MAGIC, ALU.add, R_, W_)
            self.stt(r[:, :], k[:, :], -C1_2PI, th[:, :], ALU.mult, ALU.add, R_, W_)
            self.stt(r[:, :], k[:, :], -C2_2PI, r[:, :], ALU.mult, ALU.add, R_, W_)
            self.ts(r[:, :], r[:, :], math.pi, ALU.min, R_, W_, s2=-math.pi, op1=ALU.max)
            self.act(self.s5_sth[:, :], r[:, :], AF.Sin, R_, W_)
            self.ts(r2[:, :], r[:, :], math.pi / 2, ALU.add, R_, W_)
            self.ts(msk[:, :], r2[:, :], math.pi, ALU.is_gt, R_, W_)
            self.stt(r2[:, :], msk[:, :], -TWO_PI, r2[:, :], ALU.mult, ALU.add, R_, W_)
            self.ts(r2[:, :], r2[:, :], math.pi, ALU.min, R_, W_, s2=-math.pi, op1=ALU.max)
            self.act(self.s5_cth[:, :], r2[:, :], AF.Sin, R_, W_)
            self.tt(ar[:, :], self.s5_mag[:, :], self.s5_cth[:, :], ALU.mult, R_, W_)
            self.tt(ai[:, :], self.s5_mag[:, :], self.s5_sth[:, :], ALU.mult, R_, W_)
            self.ts(am1[:, :], ar[:, :], -1.0, ALU.add, R_, W_)
            self.tt(den[:, :], lr[:, :], lr[:, :], ALU.mult, R_, W_)
            self.tt(t1[:, :], li[:, :], li[:, :], ALU.mult, R_, W_)
            self.tt(den[:, :], den[:, :], t1[:, :], ALU.add, R_, W_)
            self.P.op("dve", lambda e: e.reciprocal(out=den[:, :], in_=den[:, :]), R_, W_)
            self.tt(t1[:, :], am1[:, :], lr[:, :], ALU.mult, R_, W_)
            self.tt(t2[:, :], ai[:, :], li[:, :], ALU.mult, R_, W_)
            self.tt(t1[:, :], t1[:, :], t2[:, :], ALU.add, R_, W_)
            self.tt(zr[:, :], t1[:, :], den[:, :], ALU.mult, R_, W_)
            self.tt(t1[:, :], ai[:, :], lr[:, :], ALU.mult, R_, W_)
            self.tt(t2[:, :], am1[:, :], li[:, :], ALU.mult, R_, W_)
            self.tt(t1[:, :], t1[:, :], t2[:, :], ALU.subtract, R_, W_)
            self.tt(zi[:, :], t1[:, :], den[:, :], ALU.mult, R_, W_)
            Bre = self.sb(es, "s_Bre", [128, 32, 16], F32)
            Bim = self.sb(es, "s_Bim", [128, 32, 16], F32)
            bbr = self.sb(es, "s_bbr", [128, 32, 16], F32)
            bbi = self.sb(es, "s_bbi", [128, 32, 16], F32)
            tb = self.sb(es, "s_tb", [128, 32, 16], F32)
            for (dst, nm) in ((Bre, "s5_b_re"), (Bim, "s5_b_im")):
                P.dma("sp", dst[:, :, :], d[nm].ap().rearrange("g p c -> (g p) c").rearrange("(i q) c -> q i c", q=128),
                      (), [K])
            zrb = zr[:, :].unsqueeze(2).to_broadcast([128, 32, 16])
            zib = zi[:, :].unsqueeze(2).to_broadcast([128, 32, 16])
            self.tt(bbr[:, :, :], Bre[:, :, :], zrb, ALU.mult, R_, W_)
            self.tt(tb[:, :, :], Bim[:, :, :], zib, ALU.mult, R_, W_)
            self.tt(bbr[:, :, :], bbr[:, :, :], tb[:, :, :], ALU.subtract, R_, W_)
            self.tt(bbi[:, :, :], Bim[:, :, :], zrb, ALU.mult, R_, W_)
            self.tt(tb[:, :, :], Bre[:, :, :], zib, ALU.mult, R_, W_)
            self.tt(bbi[:, :, :], bbi[:, :, :], tb[:, :, :], ALU.add, R_, W_)
            stg = [self.sb(es, "s_stg%d" % i, [128, 128], F32) for i in range(4)]
            n_st = 0
            for i in range(32):
                gp = i % 4
                for (srcb, dstT) in ((bbr, self.s5_Bre), (bbi, self.s5_Bim)):
                    s = stg[n_st % 4]; sk = ("s_stg", n_st % 4); n_st += 1
                    self.memset(s[:, :], 0.0, [sk])
                    self.cp(s[0:64, 32 * gp:32 * gp + 16], srcb[0:64, i, :], [K, sk], [sk])
                    self.cp(s[64:128, 32 * gp + 16:32 * gp + 32], srcb[64:128, i, :], [K, sk], [sk])
                    b = self.bank()
                    self.tr(self.ps[b][:, 0:128], s[:, :], self.ident[:, :], [sk, "ident"], [("ps", b)])
                    self.cp(dstT[:, i, :], self.ps[b][:, 0:128], [("ps", b)], [K], eng="act")
                for (nm, dstT, neg) in (("s5_c_re", self.s5_Cre, False), ("s5_c_im", self.s5_Cim, True)):
                    s = stg[n_st % 4]; sk = ("s_stg", n_st % 4); n_st += 1
                    self.memset(s[:, :], 0.0, [sk])
                    P.dma("sp", s[32 * gp:32 * gp + 16, 0:64], d[nm].ap()[2 * i], [sk], [sk])
                    P.dma("sp", s[32 * gp + 16:32 * gp + 32, 64:128], d[nm].ap()[2 * i + 1], [sk], [sk])
                    b = self.bank()
                    self.tr(self.ps[b][:, 0:128], s[:, :], self.ident[:, :], [sk, "ident"], [("ps", b)])
                    if neg:
                        self.ts(dstT[:, i, :], self.ps[b][:, 0:128], -1.0, ALU.mult, [("ps", b)], [K])
                    else:
                        self.cp(dstT[:, i, :], self.ps[b][:, 0:128], [("ps", b)], [K], eng="act")
            dcol = self.sb(es, "s_dcol", [128, 8], F32)
            self.load_cols(dcol[:, :], d["s5_d"].ap().rearrange("(m g) c -> m (g c)", g=8), 8, K)
            for m in range(8):
                self.ts(self.s5_dD[:, m, :], self.ident[:, :], dcol[:, m:m + 1], ALU.mult, [K, "ident"], [K])
            h0r = self.sb(es, "s_h0r", [128, NSEQ, 32], F32)
            h0i = self.sb(es, "s_h0i", [128, NSEQ, 32], F32)
            th0 = self.sb(es, "s_th0", [128, NSEQ, 32], F32)
            sv = lambda t: t.ap().rearrange("s (i gl) p -> (s i) (gl p)", gl=2)
            self.load_cols(h0r[:, :, :].rearrange("q s i -> q (s i)"), sv(d["st_s5r"]), NSEQ * 32, K)
            self.load_cols(h0i[:, :, :].rearrange("q s i -> q (s i)"), sv(d["st_s5i"]), NSEQ * 32, K)
            arb = ar[:, :].unsqueeze(1).to_broadcast([128, NSEQ, 32])
            aib = ai[:, :].unsqueeze(1).to_broadcast([128, NSEQ, 32])
            self.tt(self.s5_ahr[:, :, :], h0r[:, :, :], arb, ALU.mult, R_, W_)
            self.tt(th0[:, :, :], h0i[:, :, :], aib, ALU.mult, R_, W_)
            self.tt(self.s5_ahr[:, :, :], self.s5_ahr[:, :, :], th0[:, :, :], ALU.subtract, R_, W_)
            self.tt(self.s5_ahi[:, :, :], h0i[:, :, :], arb, ALU.mult, R_, W_)
            self.tt(th0[:, :, :], h0r[:, :, :], aib, ALU.mult, R_, W_)
            self.tt(self.s5_ahi[:, :, :], self.s5_ahi[:, :, :], th0[:, :, :], ALU.add, R_, W_)
            P.fence()

    def s5_main(self, yT):
        P = self.P
        K = "s5c"
        with ExitStack() as es:
            F = lambda n, shp: self.sb(es, n, shp, F32)
            cr = F("m_cr", [128, TP]); si = F("m_si", [128, TP])
            d0 = F("m_d0", [128, NT])
            wr = F("m_wr", [128, NT]); wi = F("m_wi", [128, NT])
            t1 = F("m_t1", [128, NT]); t2 = F("m_t2", [128, NT])
            ub = self.sb(es, "m_ub", [128, NT], BF16)
            xre = self.sb(es, "m_xre", [128, NT], BF16)
            xim = self.sb(es, "m_xim", [128, NT], BF16)
            pfr = F("m_pfr", [128, 32]); pfi = F("m_pfi", [128, 32])
            sfr = F("m_sfr", [128, NSEQ, 32]); sfi = F("m_sfi", [128, NSEQ, 32])
            ft = F("m_ft", [128, NSEQ])
            s3 = lambda ap: ap.rearrange("p (s t) -> p s t", t=8)
            for m in range(8):
                P.dma("pool", ub[:, :], self.zT.ap()[m], (), ["ub"])
                self.bank_pool = [5, 6, 7]
                for gp in range(4):
                    i = 4 * m + gp
                    self.memset(cr[:, 0:1], 1.0, ["cr"])
                    self.memset(si[:, 0:1], 0.0, ["si"])
                    self.cp(cr[:, 1:2], self.s5_cth[:, i:i + 1], [K], ["cr"])
                    self.cp(si[:, 1:2], self.s5_sth[:, i:i + 1], [K], ["si"])
                    L = 1
                    while L + 1 < TP:
                        n = min(L, TP - 1 - L)
                        cL, sL = cr[:, L:L + 1], si[:, L:L + 1]
                        self.ts(t1[:, 0:n], si[:, 1:1 + n], sL, ALU.mult, ["si"], ["t1"])
                        self.ts(t2[:, 0:n], cr[:, 1:1 + n], sL, ALU.mult, ["cr", "si"], ["t2"])
                        self.stt(cr[:, L + 1:L + 1 + n], cr[:, 1:1 + n], cL, t1[:, 0:n], ALU.mult, ALU.subtract, ["cr", "t1"], ["cr"])
                        self.stt(si[:, L + 1:L + 1 + n], si[:, 1:1 + n], cL, t2[:, 0:n], ALU.mult, ALU.add, ["si", "cr", "t2"], ["si"])
                        L += n
                    self.cp(d0[:, :], self.s5_mag[:, i:i + 1].to_broadcast([128, NT]), [K], ["d0"])
                    self.memset(d0[:, 0:1], 0.0, ["d0"])
                    self.memset(d0[:, TP:NT:8], 0.0, ["d0"])
                    crS = cr[:, 0:8].unsqueeze(1).to_broadcast([128, NSEQ, 8])
                    siS = si[:, 0:8].unsqueeze(1).to_broadcast([128, NSEQ, 8])
                    for sidx, (t0, n) in enumerate(SEGS):
                        br, bi = self.bank(), self.bank()
                        self.mm(self.ps[br][:, 0:n], self.s5_Bre[:, i, :], ub[:, t0:t0 + n], True, True, [K, "ub"], [("ps", br)])
                        self.mm(self.ps[bi][:, 0:n], self.s5_Bim[:, i, :], ub[:, t0:t0 + n], True, True, [K, "ub"], [("ps", bi)])
                        parts = [(0, min(n, TP - t0), False)]
                        if t0 + n > TP:
                            parts.append((TP - t0, n - (TP - t0), True))
                        for (o, ln, samp) in parts:
                            if samp:
                                pr, pi_ = s3(self.ps[br][:, o:o + ln]), s3(self.ps[bi][:, o:o + ln])
                                c_, s_ = crS, siS
                                v = lambda t: s3(t[:, t0 + o:t0 + o + ln])
                            else:
                                pr, pi_ = self.ps[br][:, o:o + ln], self.ps[bi][:, o:o + ln]
                                c_, s_ = cr[:, t0 + o:t0 + o + ln], si[:, t0 + o:t0 + o + ln]
                                v = lambda t: t[:, t0 + o:t0 + o + ln]
                            self.tt(v(t1), pr, c_, ALU.mult, [("ps", br), "cr"], ["t1"])
                            self.tt(v(t2), pi_, s_, ALU.mult, [("ps", bi), "si"], ["t2"])
                            self.tt(v(wr), v(t1), v(t2), ALU.add, ["t1", "t2"], ["wr"])
                            self.tt(v(t1), pi_, c_, ALU.mult, [("ps", bi), "cr"], ["t1"])
                            self.tt(v(t2), pr, s_, ALU.mult, [("ps", br), "si"], ["t2"])
                            self.tt(v(wi), v(t1), v(t2), ALU.subtract, ["t1", "t2"], ["wi"])
                    self.tt(wr[:, TP:NT:8], wr[:, TP:NT:8], self.s5_ahr[:, :, i], ALU.add, ["wr", K], ["wr"])
                    self.tt(wi[:, TP:NT:8], wi[:, TP:NT:8], self.s5_ahi[:, :, i], ALU.add, ["wi", K], ["wi"])
                    self.scan(wr[:, :], d0[:, :], wr[:, :], 0.0, ["d0", "wr"], ["wr"])
                    self.scan(wi[:, :], d0[:, :], wi[:, :], 0.0, ["d0", "wi"], ["wi"])
                    for samp in (False, True):
                        if samp:
                            v = lambda t: s3(t[:, TP:NT])
                            c_, s_ = crS, siS
                            vo = lambda t: s3(t[:, TP:NT])
                        else:
                            v = lambda t: t[:, 0:TP]
                            c_, s_ = cr[:, :], si[:, :]
                            vo = lambda t: t[:, 0:TP]
                        self.tt(v(t1), v(wr), c_, ALU.mult, ["wr", "cr"], ["t1"])
                        self.tt(v(t2), v(wi), s_, ALU.mult, ["wi", "si"], ["t2"])
                        self.tt(vo(xre), v(t1), v(t2), ALU.subtract, ["t1", "t2"], ["xre"])
                        self.tt(v(t1), v(wr), s_, ALU.mult, ["wr", "si"], ["t1"])
                        self.tt(v(t2), v(wi), c_, ALU.mult, ["wi", "cr"], ["t2"])
                        self.tt(vo(xim), v(t1), v(t2), ALU.add, ["t1", "t2"], ["xim"])
                    cP, sP = cr[:, TP - 1:TP], si[:, TP - 1:TP]
                    self.ts(ft[:, 0:1], wi[:, TP - 1:TP], sP, ALU.mult, ["wi", "si"], ["ft"])
                    self.stt(pfr[:, i:i + 1], wr[:, TP - 1:TP], cP, ft[:, 0:1], ALU.mult, ALU.subtract, ["wr", "cr", "ft"], ["pf"])
                    self.ts(ft[:, 0:1], wr[:, TP - 1:TP], sP, ALU.mult, ["wr", "si"], ["ft"])
                    self.stt(pfi[:, i:i + 1], wi[:, TP - 1:TP], cP, ft[:, 0:1], ALU.mult, ALU.add, ["wi", "cr", "ft"], ["pf"])
                    c7, s7 = cr[:, 7:8], si[:, 7:8]
                    self.ts(ft[:, :], wi[:, TP + 7:NT:8], s7, ALU.mult, ["wi", "si"], ["ft"])
                    self.stt(sfr[:, :, i], wr[:, TP + 7:NT:8], c7, ft[:, :], ALU.mult, ALU.subtract, ["wr", "cr", "ft"], ["sf"])
                    self.ts(ft[:, :], wr[:, TP + 7:NT:8], s7, ALU.mult, ["wr", "si"], ["ft"])
                    self.stt(sfi[:, :, i], wi[:, TP + 7:NT:8], c7, ft[:, :], ALU.mult, ALU.add, ["wi", "cr", "ft"], ["sf"])
                    for sidx, (t0, n) in enumerate(SEGS):
                        b = sidx
                        self.mm(self.ps[b][:, 0:n], self.s5_Cre[:, i, :], xre[:, t0:t0 + n], gp == 0, False, [K, "xre"], [("ps", b)])
                        self.mm(self.ps[b][:, 0:n], self.s5_Cim[:, i, :], xim[:, t0:t0 + n], False, False, [K, "xim"], [("ps", b)])
                for sidx, (t0, n) in enumerate(SEGS):
                    b = sidx
                    self.mm(self.ps[b][:, 0:n], self.s5_dD[:, m, :], ub[:, t0:t0 + n], False, True, [K, "ub"], [("ps", b)])
                    self.act(yT[:, 8 + m, t0:t0 + n], self.ps[b][:, 0:n], AF.Gelu_apprx_tanh, [("ps", b)], [("ya", m, sidx)])
                self.bank_pool = None
            stq = [self.sb(es, "m_stq%d" % j, [128, 128], F32) for j in range(2)]
            nq = 0
            for (srcT, dst) in ((pfr, "p_s5r"), (pfi, "p_s5i")):
                b = self.bank()
                s = stq[nq % 2]; sk = ("stq", nq % 2); nq += 1
                self.tr(self.ps[b][0:32, 0:128], srcT[:, :], self.ident[:, :], ["pf", "ident"], [("ps", b)])
                self.cp(s[0:32, :], self.ps[b][0:32, 0:128], [("ps", b)], [sk])
                P.dma("sp", self.dout[dst].ap().rearrange("(i gl) p -> i (gl p)", gl=2), s[0:32, :], [sk], [dst], is_output=True)
            for (srcT, dst) in ((sfr, "s_s5r"), (sfi, "s_s5i")):
                flat = srcT[:, :, :].rearrange("q s i -> q (s i)")
                dv = self.dout[dst].ap().rearrange("s (i gl) p -> (s i) (gl p)", gl=2)
                for c in range(4):
                    b = self.bank()
                    s = stq[nq % 2]; sk = ("stq", nq % 2); nq += 1
                    self.tr(self.ps[b][:, 0:128], flat[:, c * 128:(c + 1) * 128], self.ident[:, :], ["sf", "ident"], [("ps", b)])
                    self.cp(s[:, :], self.ps[b][:, 0:128], [("ps", b)], [sk])
                    P.dma("sp", dv[c * 128:(c + 1) * 128, :], s[:, :], [sk], [dst], is_output=True)
            P.fence()

    def s5_glu(self, yT):
        with ExitStack() as es:
            sg = [self.sb(es, "g_sg%d" % i, [128, 512], F32) for i in range(2)]
            cnt = [0]

            def evac(j, si, t0, n, ps, pk):
                s = sg[cnt[0] % 2]; sk = ("g_sg", cnt[0] % 2); cnt[0] += 1
                self.act(s[:, 0:n], ps, AF.Sigmoid, [pk], [sk])
                self.tt(yT[:, j, t0:t0 + n], s[:, 0:n], yT[:, 8 + j, t0:t0 + n], ALU.mult, [sk, ("ya", j, si)], [("yT", j, si)])

            self.proj(es, lambda kt, t0, n: yT[:, 8 + kt, t0:t0 + n], lambda kt, si: ("ya", kt, si), 8,
                      self.din["s5_w_glu"].ap(), 0, 1024, evac, tag="wg")
            self.P.fence()


HG_CHUNKS = [(64 * n, 64, None) for n in range(32)] + [(2048, 16, None)] + [(TP + 8 * q, 8, q) for q in range(NSEQ)]


class HgMixin:
    def hg_main(self, yT):
        P = self.P
        d = self.din
        K = "hgc"
        with ExitStack() as es:
            F = lambda n, shp=[128, NT]: self.sb(es, n, shp, F32)
            lbc = F("h_lbc", [128, 16]); lb = F("h_lb", [128, 8]); oml = F("h_oml", [128, 8]); nw = F("h_nw", [128, 1])
            cmask = F("h_cmask")
            self.load_cols(lbc[:, :], d["hg_lb"].ap().rearrange("l (c p) -> (l c) p", p=128), 16, K)
            self.load_cols(nw[:, :], d["hg_norm_w"].ap(), 1, K)
            self.tt(lb[:, :], lbc[:, 0:8], lbc[:, 8:16], ALU.subtract, [K], [K])
            self.act(lb[:, :], lb[:, :], AF.Sigmoid, [K], [K])
            self.ts(oml[:, :], lb[:, :], -1.0, ALU.mult, [K], [K], s2=1.0, op1=ALU.add)
            self.memset(cmask[:, :], 1.0, [K])
            self.memset(cmask[:, 0:2048:64], 0.0, [K])
            self.memset(cmask[:, 2048:2049], 0.0, [K])
            self.memset(cmask[:, TP:NT:8], 0.0, [K])
            qT, fg, iT, gT, kk, bc, btf, ex, kend, oall = (F(n) for n in ("h_q", "h_f", "h_i", "h_g", "h_kk", "h_bc", "h_btf", "h_ex", "h_kend", "h_o"))
            qin = self.sb(es, "h_qin", [128, NT], BF16)
            kin = self.sb(es, "h_kin", [128, NT], BF16)
            dec = F("h_dec", [128, 49])
            S = [F("h_S%d" % i, [128, 128]) for i in range(2)]
            Sb = [self.sb(es, "h_Sb%d" % i, [128, 128], BF16) for i in range(2)]
            SS = [F("h_SS%d" % i, [128, 128]) for i in range(2)]
            SSb = [self.sb(es, "h_SSb%d" % i, [128, 128], BF16) for i in range(2)]
            attm = [self.sb(es, "h_att%d" % i, [64, 64], BF16) for i in range(2)]
            vt = [self.sb(es, "h_vt%d" % i, [64, 128], BF16) for i in range(2)]
            ket = [self.sb(es, "h_ket%d" % i, [64, 128], BF16) for i in range(2)]
            sq = [F("h_sq%d" % i, [128, 512]) for i in range(2)]
            rs = [F("h_rs%d" % i, [128, 512]) for i in range(2)]
            eps = F("h_eps", [128, 1])
            self.memset(eps[:, :], EPS, [K])
            s_i = 0
            for hh in range(8):
                for (t, j) in ((qT, 8 + hh), (fg, 16 + hh), (iT, 24 + hh), (gT, 32 + hh)):
                    P.dma("sp", t[:, :], self.zT.ap()[j], (), [t.name])
                R = lambda *ts: [t.name if hasattr(t, "name") else t for t in ts]
                self.act(fg[:, :], fg[:, :], AF.Sigmoid, R(fg), R(fg))
                self.ts(fg[:, :], fg[:, :], oml[:, hh:hh + 1], ALU.mult, R(fg, K), R(fg), s2=lb[:, hh:hh + 1], op1=ALU.add)
                self.ts(kk[:, :], fg[:, :], -1.0, ALU.mult, R(fg), R(kk), s2=1.0, op1=ALU.add)
                self.act(bc[:, :], fg[:, :], AF.Ln, R(fg), R(bc))
                self.scan(bc[:, :], cmask[:, :], bc[:, :], 0.0, R(bc, K), R(bc))
                v3 = lambda ap, c: ap.rearrange("p (n c) -> p n c", c=c)
                self.cp(v3(btf[:, 0:2048], 64), v3(bc[:, 0:2048], 64)[:, :, 63:64].to_broadcast([128, 32, 64]), R(bc), R(btf))
                self.cp(btf[:, 2048:TP], bc[:, TP - 1:TP].to_broadcast([128, 16]), R(bc), R(btf))
                self.cp(v3(btf[:, TP:NT], 8), v3(bc[:, TP:NT], 8)[:, :, 7:8].to_broadcast([128, NSEQ, 8]), R(bc), R(btf))
                self.act(dec[:, 0:32], bc[:, 63:2048:64], AF.Exp, R(bc), R(dec))
                self.act(dec[:, 32:33], bc[:, TP - 1:TP], AF.Exp, R(bc), R(dec))
                self.act(dec[:, 33:49], bc[:, TP + 7:NT:8], AF.Exp, R(bc), R(dec))
                self.act(qT[:, :], qT[:, :], AF.Silu, R(qT), R(qT))
                self.act(ex[:, :], bc[:, :], AF.Exp, R(bc), R(ex))
                self.tt(qin[:, :], qT[:, :], ex[:, :], ALU.mult, R(qT, ex), R(qin))
                self.act(ex[:, :], bc[:, :], AF.Exp, R(bc, qin), R(ex), scale=-1.0)
                self.tt(kin[:, :], kk[:, :], ex[:, :], ALU.mult, R(kk, ex), R(kin))
                self.tt(btf[:, :], btf[:, :], bc[:, :], ALU.subtract, R(btf, bc), R(btf))
                self.act(btf[:, :], btf[:, :], AF.Exp, R(btf), R(btf))
                self.tt(kend[:, :], kk[:, :], btf[:, :], ALU.mult, R(kk, btf), R(kend))
                curP = 0
                self.memset(S[0][:, :], 0.0, [("S", 0)])
                self.memset(Sb[0][:, :], 0.0, [("Sb", 0)])
                pr_ = [n for n, c_ in enumerate(HG_CHUNKS) if c_[2] is None]
                sa_ = [n for n, c_ in enumerate(HG_CHUNKS) if c_[2] is not None]
                order = []
                while pr_ or sa_:
                    order += pr_[:2]; pr_ = pr_[2:]
                    order += sa_[:1]; sa_ = sa_[1:]
                for it_, n in enumerate(order):
                    (t0, C, q) = HG_CHUNKS[n]
                    if q is None:
                        Sc, Sbc, kS, kSb = S[curP], Sb[curP], ("S", curP), ("Sb", curP)
                    else:
                        k_ = q % 2
                        Sc, Sbc, kS, kSb = SS[k_], SSb[k_], ("SS", k_), ("SSb", k_)
                        P.dma("sp", Sc[:, :], d["st_hg"].ap()[q, hh], (), [kS])
                        self.cp(Sbc[:, :], Sc[:, :], [kS], [kSb], eng="act")
                    a_i = it_ % 2
                    ba, bv, bk, bo, bs = (self.bank() for _ in range(5))
                    self.mm(self.ps[ba][0:C, 0:C], kin[:, t0:t0 + C], qin[:, t0:t0 + C], True, True, R(kin, qin), [("ps", ba)])
                    self.tt(attm[a_i][0:C, 0:C], self.ps[ba][0:C, 0:C], self.tril[0:C, 0:C], ALU.mult, [("ps", ba), "tril"], [("att", a_i)])
                    self.tr(self.ps[bv][0:C, 0:128], iT[:, t0:t0 + C], self.ident[:, :], R(iT, "ident"), [("ps", bv)])
                    self.cp(vt[a_i][0:C, :], self.ps[bv][0:C, 0:128], [("ps", bv)], [("vt", a_i)], eng="act")
                    self.tr(self.ps[bk][0:C, 0:128], kend[:, t0:t0 + C], self.ident[:, :], R(kend, "ident"), [("ps", bk)])
                    self.cp(ket[a_i][0:C, :], self.ps[bk][0:C, 0:128], [("ps", bk)], [("ket", a_i)], eng="act")
                    self.mm(self.ps[bo][:, 0:C], vt[a_i][0:C, :], attm[a_i][0:C, 0:C], True, False, [("vt", a_i), ("att", a_i)], [("ps", bo)])
                    self.mm(self.ps[bo][:, 0:C], Sbc[:, :], qin[:, t0:t0 + C], False, True, [kSb, "h_qin"], [("ps", bo)])
                    self.cp(oall[:, t0:t0 + C], self.ps[bo][:, 0:C], [("ps", bo)], R(oall), eng="act")
                    self.mm(self.ps[bs][:, 0:128], ket[a_i][0:C, :], vt[a_i][0:C, :], True, True, [("ket", a_i), ("vt", a_i)], [("ps", bs)])
                    if q is None:
                        nxt = 1 - curP
                        self.stt(S[nxt][:, :], Sc[:, :], dec[:, n:n + 1], self.ps[bs][:, 0:128], ALU.mult, ALU.add,
                                 [kS, ("ps", bs)] + R(dec), [("S", nxt)])
                        self.cp(Sb[nxt][:, :], S[nxt][:, :], [("S", nxt)], [("Sb", nxt)], eng="act")
                        curP = nxt
                        if n == 32:
                            P.dma("sp", self.dout["p_hg"].ap()[hh], S[curP][:, :], [("S", curP)], ["p_hg"], is_output=True)
                    else:
                        self.stt(Sc[:, :], Sc[:, :], dec[:, n:n + 1], self.ps[bs][:, 0:128], ALU.mult, ALU.add,
                                 [kS, ("ps", bs)] + R(dec), [kS])
                        P.dma("sp", self.dout["s_hg"].ap()[q, hh], Sc[:, :], [kS], ["s_hg"], is_output=True)
                self.act(gT[:, :], gT[:, :], AF.Silu, R(gT), R(gT))
                for si, (t0, n) in enumerate(SEGS):
                    q_ = sq[si % 2]; r_ = rs[si % 2]
                    b = self.bank()
                    self.act(q_[:, 0:n], oall[:, t0:t0 + n], AF.Square, R(oall), [("hsq", si % 2)])
                    self.mm(self.ps[b][:, 0:n], self.ones[:, :], q_[:, 0:n], True, True, [("hsq", si % 2), "ones"], [("ps", b)])
                    self.act(r_[:, 0:n], self.ps[b][:, 0:n], AF.Sqrt, [("ps", b), K], [("hrs", si % 2)], bias=eps[:, 0:1], scale=1.0 / 128)
                    self.P.op("dve", lambda e, r_=r_, n=n: e.reciprocal(out=r_[:, 0:n], in_=r_[:, 0:n]), [("hrs", si % 2)], [("hrs", si % 2)])
                    self.stt(r_[:, 0:n], oall[:, t0:t0 + n], nw[:, 0:1], r_[:, 0:n], ALU.mult, ALU.mult, R(oall, K) + [("hrs", si % 2)], [("hrs", si % 2)])
                    self.tt(yT[:, 8 + hh, t0:t0 + n], r_[:, 0:n], gT[:, t0:t0 + n], ALU.mult, [("hrs", si % 2)] + R(gT), [("yT", 8 + hh, si)])
            P.fence()


class FfnMixin:
    def out_proj_res(self, xT, xkey, KT, w2d, tag="wo"):
        P = self.P
        with ExitStack() as es:
            hcol = [self.sb(es, "hcol%d" % i, [128, NT], F32) for i in range(3)]

            def evac(j, si, t0, n, ps, pk):
                h = hcol[j % 3]; hk = ("hcol", j % 3)
                if si == 0:
                    P.dma("sp", h[:, :], self.hT.ap()[j], [("hT", j)], [hk])
                self.tt(h[:, t0:t0 + n], h[:, t0:t0 + n], ps, ALU.add, [hk, pk], [hk])
                if si == len(SEGS) - 1:
                    P.dma("sp", self.hT.ap()[j], h[:, :], [hk], [("hT", j)])

            self.proj(es, xT, xkey, KT, w2d, 0, D, evac, tag=tag)
            P.fence()

    def ffn(self, l, xnT):
        P = self.P
        d = self.din
        K = "ffc"
        v3 = lambda ap: ap.rearrange("p (s t) -> p s t", t=8)
        gS = self.gS
        with ExitStack() as es:
            F = lambda n, shp: self.sb(es, n, shp, F32)
            cw = F("f_cw", [128, 3 * FC]); cb = F("f_cb", [128, FC])
            cvin = F("f_cvin", [128, FC, 32]); cvP = F("f_cvP", [128, FC, 2]); cvS = F("f_cvS", [128, FC, 32])
            self.load_cols(cw[:, :], d["ffn_conv_w"].ap()[l].rearrange("r (c p) -> (r c) p", p=128), 3 * FC, K)
            self.load_cols(cb[:, :], d["ffn_conv_b"].ap()[l:l + 1, :].rearrange("o (c p) -> (o c) p", p=128), FC, K)
            with ExitStack() as es2:
                rows = self.sb(es2, "f_rows", [32, D_FF], F32)
                P.dma("sp", rows[:, :], d["st_cv"].ap()[l].rearrange("s r f -> (s r) f"), (), ["f_rows"])
                for c in range(FC):
                    b = self.bank()
                    self.tr(self.ps[b][:, 0:32], rows[0:32, c * 128:(c + 1) * 128], self.ident[0:32, 0:32], ["f_rows", "ident"], [("ps", b)])
                    self.cp(cvin[:, c, :], self.ps[b][:, 0:32], [("ps", b)], [K], eng=("act" if c % 2 else "dve"))
                P.fence()
            wf = [self.sb(es, "f_wf%d" % i, [128, DC, 512], BF16) for i in range(2)]
            aP = [F("f_aP%d" % i, [128, TP + 2]) for i in range(2)]
            aS = [F("f_aS%d" % i, [128, NSEQ, 10]) for i in range(2)]
            cc = [F("f_cc%d" % i, [128, NT]) for i in range(2)]
            go = [self.sb(es, "f_go%d" % i, [128, NT], BF16) for i in range(2)]
            for i in range(2):
                self.memset(aP[i][:, 0:2], 0.0, [("aP", i)])
            w_in = d["ffn_w_in"].ap()[l]
            for g0 in range(0, FC, 2):
                w = wf[(g0 // 2) % 2]; wk = ("f_wf", (g0 // 2) % 2)
                P.dma("pool", w[:, :, 0:256], w_in[:, g0 * 128:(g0 + 2) * 128].rearrange("(kt p) n -> p kt n", p=128), (), [wk])
                P.dma("pool", w[:, :, 256:512], w_in[:, D_FF + g0 * 128:D_FF + (g0 + 2) * 128].rearrange("(kt p) n -> p kt n", p=128), (), [wk])
                for jj in range(2):
                    j = g0 + jj
                    ap_, as_, c_, g_ = aP[j % 2], aS[j % 2], cc[j % 2], go[j % 2]
                    ak, ask, ck, gk = ("aP", j % 2), ("aS", j % 2), ("cc", j % 2), ("go", j % 2)
                    self.cp(as_[:, :, 0:2], cvin[:, j, :].rearrange("p (s r) -> p s r", r=2), [K], [ask])
                    for si, (t0, n) in enumerate(SEGS):
                        b = self.bank()
                        for kt in range(DC):
                            self.mm(self.ps[b][:, 0:n], w[:, kt, jj * 128:(jj + 1) * 128], xnT[:, kt, t0:t0 + n], kt == 0, kt == DC - 1,
                                    [wk, ("xnT", kt, si)], [("ps", b)])
                        npr = min(n, TP - t0)
                        self.cp(ap_[:, 2 + t0:2 + t0 + npr], self.ps[b][:, 0:npr], [("ps", b)], [ak], eng="act")
                        if npr < n:
                            self.cp(as_[:, :, 2:10], v3(self.ps[b][:, npr:n]), [("ps", b)], [ask], eng="act")
                    w0, w1, w2 = (cw[:, r * FC + j:r * FC + j + 1] for r in range(3))
                    self.ts(c_[:, 0:TP], ap_[:, 2:2 + TP], w2, ALU.mult, [ak, K], [ck], s2=cb[:, j:j + 1], op1=ALU.add)
                    self.stt(c_[:, 0:TP], ap_[:, 1:1 + TP], w1, c_[:, 0:TP], ALU.mult, ALU.add, [ak, K, ck], [ck])
                    self.stt(c_[:, 0:TP], ap_[:, 0:TP], w0, c_[:, 0:TP], ALU.mult, ALU.add, [ak, K, ck], [ck])
                    cs = v3(c_[:, TP:NT])
                    self.ts(cs, as_[:, :, 2:10], w2, ALU.mult, [ask, K], [ck], s2=cb[:, j:j + 1], op1=ALU.add)
                    self.stt(cs, as_[:, :, 1:9], w1, cs, ALU.mult, ALU.add, [ask, K, ck], [ck])
                    self.stt(cs, as_[:, :, 0:8], w0, cs, ALU.mult, ALU.add, [ask, K, ck], [ck])
                    self.cp(cvP[:, j, :], ap_[:, TP:TP + 2], [ak], [K])
                    self.cp(cvS[:, j, :].rearrange("p (s r) -> p s r", r=2), as_[:, :, 8:10], [ask], [K])
                    self.act(c_[:, :], c_[:, :], AF.Gelu_apprx_tanh, [ck], [ck])
                    for si, (t0, n) in enumerate(SEGS):
                        b = self.bank()
                        for kt in range(DC):
                            self.mm(self.ps[b][:, 0:n], w[:, kt, 256 + jj * 128:256 + (jj + 1) * 128], xnT[:, kt, t0:t0 + n], kt == 0, kt == DC - 1,
                                    [wk, ("xnT", kt, si)], [("ps", b)])
                        self.tt(g_[:, t0:t0 + n], c_[:, t0:t0 + n], self.ps[b][:, 0:n], ALU.mult, [ck, ("ps", b)], [gk])
                    for si, (t0, n) in enumerate(SEGS):
                        P.dma("sp", gS.ap()[si, :, j, 0:n], g_[:, t0:t0 + n], [gk], [("gS", j)])
            with ExitStack() as es2:
                rows = self.sb(es2, "f_rowo", [32, D_FF], F32)
                rowp = self.sb(es2, "f_rowp", [2, D_FF], F32)
                for c in range(FC):
                    b = self.bank()
                    self.tr(self.ps[b][0:32, 0:128], cvS[:, c, :], self.ident[:, :], [K, "ident"], [("ps", b)])
                    self.cp(rows[0:32, c * 128:(c + 1) * 128], self.ps[b][0:32, 0:128], [("ps", b)], ["f_rowo"], eng=("act" if c % 2 else "dve"))
                    b = self.bank()
                    self.tr(self.ps[b][0:2, 0:128], cvP[:, c, :], self.ident[:, :], [K, "ident"], [("ps", b)])
                    self.cp(rowp[0:2, c * 128:(c + 1) * 128], self.ps[b][0:2, 0:128], [("ps", b)], ["f_rowp"], eng=("dve" if c % 2 else "act"))
                P.dma("sp", self.dout["s_cv"].ap()[l].rearrange("s r f -> (s r) f"), rows[:, :], ["f_rowo"], ["s_cv"], is_output=True)
                P.dma("sp", self.dout["p_cv"].ap()[l], rowp[:, :], ["f_rowp"], ["p_cv"], is_output=True)
                P.fence()
            P.fence()

    def ffn_down(self, l):
        P = self.P
        gS = self.gS
        w_dn = self.din["ffn_w_down"].ap()[l]
        with ExitStack() as es:
            wd = [self.sb(es, "d_wd%d" % i, [128, FC, 256], BF16) for i in range(2)]
            gs = [self.sb(es, "d_gs%d" % i, [128, FC, 512], BF16) for i in range(2)]
            hc = [self.sb(es, "d_hc%d" % i, [128, 2, NT], F32) for i in range(2)]
            gi = 0
            for g0 in range(0, DC, 2):
                w = wd[(g0 // 2) % 2]; wk = ("d_wd", (g0 // 2) % 2)
                h = hc[(g0 // 2) % 2]; hk = ("d_hc", (g0 // 2) % 2)
                P.dma("pool", w[:, :, :], w_dn[:, g0 * 128:(g0 + 2) * 128].rearrange("(kt p) n -> p kt n", p=128), (), [wk])
                P.dma("sp", h[:, :, :], self.hT.ap()[g0:g0 + 2].rearrange("c p t -> p c t"), [("hT", g0), ("hT", g0 + 1)], [hk])
                for si, (t0, n) in enumerate(SEGS):
                    g = gs[gi % 2]; gk = ("d_gs", gi % 2); gi += 1
                    P.dma("act", g[:, :, 0:n], gS.ap()[si, :, :, 0:n], [("gS", j) for j in range(FC)], [gk])
                    for ii in range(2):
                        b = self.bank()
                        for kt in range(FC):
                            self.mm(self.ps[b][:, 0:n], w[:, kt, ii * 128:(ii + 1) * 128], g[:, kt, 0:n], kt == 0, kt == FC - 1, [wk, gk], [("ps", b)])
                        self.tt(h[:, ii, t0:t0 + n], h[:, ii, t0:t0 + n], self.ps[b][:, 0:n], ALU.add, [hk, ("ps", b)], [hk])
                P.dma("sp", self.hT.ap()[g0:g0 + 2].rearrange("c p t -> p c t"), h[:, :, :], [hk], [("hT", g0), ("hT", g0 + 1)])
            P.fence()


class RwMixin:
    def rw_declare(self):
        sc = self.scratch
        fm = lambda n, dt=F32: sc(n, [DC, 128, NT], dt)
        self.rS, self.kS, self.vS, self.aS = fm("rS"), fm("kS"), fm("vS"), fm("aS")
        self.dS, self.g2S, self.bonS = fm("dS"), fm("g2S"), fm("bonS")
        self.rbS, self.avS = fm("rbS", BF16), fm("avS", BF16)
        self.btok = sc("btok", [NT, D], BF16)
        self.ktok = sc("ktok", [NT, D], BF16)
        self.vtok = sc("vtok", [NT, D], BF16)
        self.ytok = sc("ytok", [NT, D], BF16)

    def rw_shift_out(self, xn32):
        P = self.P
        if True:
            with ExitStack() as es2:
                tokA = self.sb(es2, "r_tokA", [16, D], F32)
                tokB = self.sb(es2, "r_tokB", [128, D], F32)
                for c in range(DC):
                    b = self.bank()
                    self.tr(self.ps[b][0:16, 0:128], xn32[:, c, 0:16], self.ident[:, :], ["out32", "ident"], [("ps", b)])
                    self.cp(tokA[0:16, c * 128:(c + 1) * 128], self.ps[b][0:16, 0:128], [("ps", b)], ["tokA"], eng="act")
                    b = self.bank()
                    self.tr(self.ps[b][:, 0:128], xn32[:, c, 16:144], self.ident[:, :], ["out32", "ident"], [("ps", b)])
                    self.cp(tokB[:, c * 128:(c + 1) * 128], self.ps[b][:, 0:128], [("ps", b)], ["tokB"])
                P.dma("sp", self.dout["p_sh"].ap(), tokA[15:16, :], ["tokA"], ["p_sh"], is_output=True)
                for q in range(NSEQ):
                    P.dma("sp", self.dout["s_sh"].ap()[q:q + 1, :], tokB[8 * q + 7:8 * q + 8, :], ["tokB"], ["s_sh"], is_output=True)
                P.fence()

    def rw_proj(self, xnT):
        P = self.P
        d = self.din
        K = "rwc"
        v3 = lambda ap: ap.rearrange("p (s t) -> p s t", t=8)
        with ExitStack() as es:
            F = lambda n, shp: self.sb(es, n, shp, F32)
            mu = F("r_mu", [128, 96]); omm = F("r_omm", [128, 96])
            self.load_cols(mu[:, :], d["rw_mu"].ap().rearrange("j (c p) -> (j c) p", p=128), 96, K)
            self.ts(omm[:, :], mu[:, :], -1.0, ALU.mult, [K], [K], s2=1.0, op1=ALU.add)
            w0c = F("r_w0", [128, DC]); a0c = F("r_a0", [128, DC])
            self.load_cols(w0c[:, :], d["rw_w0"].ap().rearrange("o (c p) -> (o c) p", p=128), DC, K)
            self.load_cols(a0c[:, :], d["rw_a0"].ap().rearrange("o (c p) -> (o c) p", p=128), DC, K)
            shin = F("r_shin", [128, DC, NSEQ])
            for c in range(DC):
                self.load_cols(shin[:, c, :], d["st_sh"].ap()[:, c * 128:(c + 1) * 128], NSEQ, K)
            xj = self.sb(es, "r_xj", [128, DC, NT], BF16)
            t1s = [self.sb(es, "r_t1%d" % i, [128, NT], BF16) for i in range(2)]
            zrow = [F("r_zrow%d" % i, [128, NT]) for i in range(2)]

            def build_mix(j):
                for c in range(DC):
                    m_, o_ = mu[:, j * 16 + c:j * 16 + c + 1], omm[:, j * 16 + c:j * 16 + c + 1]
                    xk = [("xnT", c, si) for si in range(5)]
                    wk = [("xj", c, si) for si in range(5)]
                    t1 = t1s[c % 2]; tk1 = ("r_t1", c % 2)
                    self.act(t1[:, :], xnT[:, c, :], AF.Copy, xk + [K], [tk1], scale=o_)
                    self.stt(xj[:, c, 1:TP], xnT[:, c, 0:TP - 1], m_, t1[:, 1:TP], ALU.mult, ALU.add, xk + [tk1, K], wk)
                    self.cp(xj[:, c, 0:1], t1[:, 0:1], [tk1], wk)
                    self.stt(v3(xj[:, c, TP:NT])[:, :, 1:8], v3(xnT[:, c, TP:NT])[:, :, 0:7], m_, v3(t1[:, TP:NT])[:, :, 1:8],
                             ALU.mult, ALU.add, xk + [tk1, K], wk)
                    self.stt(xj[:, c, TP:NT:8], shin[:, c, :], m_, t1[:, TP:NT:8], ALU.mult, ALU.add, [tk1, K], wk)

            xkey = lambda kt, si: ("xj", kt, si)

            def to_scratch(dst, func=None, bias=None, scale=None):
                def evac(j, si, t0, n, ps, pk):
                    z = zrow[j % 2]
                    if func is None:
                        self.cp(z[:, t0:t0 + n], ps, [pk], [("zrow", j % 2, si)], eng=("act" if si % 2 else "dve"))
                    else:
                        self.act(z[:, t0:t0 + n], ps, func, [pk, K], [("zrow", j % 2, si)],
                                 bias=(bias[:, j:j + 1] if bias is not None else None), scale=scale)
                    if si == len(SEGS) - 1:
                        P.dma("sp", dst.ap()[j], z[:, :], [("zrow", j % 2, s_) for s_ in range(len(SEGS))], [dst.name])
                return evac

            if "dbg_xj" in self.debug:
                build_mix(0)
                dbg = self.scratch("dbg_xj", [DC, 128, NT], BF16)
                dbg2 = self.scratch("dbg_xn", [DC, 128, NT], BF16)
                P.dma("sp", dbg.ap().rearrange("c p t -> p c t"), xj[:, :, :], [("xj", c, si) for c in range(DC) for si in range(5)], ["dbg"], is_output=True)
                P.dma("sp", dbg2.ap().rearrange("c p t -> p c t"), xnT[:, :, :], [("xnT", c, si) for c in range(DC) for si in range(5)], ["dbg2"], is_output=True)
                P.fence()
                return
            with ExitStack() as esw:
                build_mix(0)
                self.proj(esw, xj, xkey, DC, d["rw_w_r"].ap(), 0, D, to_scratch(self.rS), tag="wr", wg=2)
                P.fence()
            with ExitStack() as esw:
                build_mix(2)
                self.proj(esw, xj, xkey, DC, d["rw_w_k"].ap(), 0, D, to_scratch(self.kS), tag="wk", wg=2)
                P.fence()
            with ExitStack() as esw:
                build_mix(3)
                self.proj(esw, xj, xkey, DC, d["rw_w_v"].ap(), 0, D, to_scratch(self.vS), tag="wv", wg=2)
                P.fence()
            mid = self.sb(es, "r_mid", [128, 2, NT], BF16)

            def lora(jmix, w1, r1, f1, w2, f2, bias2, dst, sc2=None, tag="l"):
                build_mix(jmix)
                kp = 128 if r1 > 128 else r1
                kt2 = (r1 + 127) // 128

                def ev1(j, si, t0, n, ps, pk):
                    mw = min(128, r1 - j * 128)
                    if f1 is None:
                        self.cp(mid[0:mw, j, t0:t0 + n], ps, [pk], [("mid", j, si)], eng="act")
                    else:
                        self.act(mid[0:mw, j, t0:t0 + n], ps, f1, [pk], [("mid", j, si)])
                with ExitStack() as esw:
                    self.proj(esw, xj, xkey, DC, w1.ap(), 0, r1, ev1, tag=tag + "1", wg=2)
                    P.fence()
                with ExitStack() as esw:
                    self.proj(esw, mid, lambda kt, si: ("mid", kt, si), kt2, w2.ap(), 0, D,
                              to_scratch(dst, f2, bias2, sc2), kparts=kp, tag=tag + "2", wg=2)
                    P.fence()

            lora(1, d["rw_w1"], 96, AF.Tanh, d["rw_w2"], AF.Sigmoid, w0c, self.dS, tag="lw")
            lora(4, d["rw_a1"], 96, None, d["rw_a2"], AF.Sigmoid, a0c, self.aS, tag="la")
            lora(5, d["rw_g1"], 256, AF.Sigmoid, d["rw_g2"], None, None, self.g2S, tag="lg")
            P.fence()

    def rw_post(self):
        P = self.P
        d = self.din
        K = "rwc2"
        with ExitStack() as es:
            F = lambda n, shp=[128, NT]: self.sb(es, n, shp, F32)
            kkc, kac, rkc = F("q_kk", [128, DC]), F("q_ka", [128, DC]), F("q_rk", [128, DC])
            for (t, nm) in ((kkc, "rw_k_k"), (kac, "rw_k_a"), (rkc, "rw_r_k")):
                self.load_cols(t[:, :], d[nm].ap().rearrange("o (c p) -> (o c) p", p=128), DC, K)
            nb = 2
            r_, k_, v_, a_, dd = ([F("q_%s%d" % (n, i)) for i in range(nb)] for n in ("r", "k", "v", "a", "d"))
            kk, t1, t2 = F("q_kkn"), F("q_t1"), F("q_t2")
            rb = [self.sb(es, "q_rb%d" % i, [128, NT], BF16) for i in range(nb)]
            av = [self.sb(es, "q_av%d" % i, [128, NT], BF16) for i in range(nb)]
            tok = [self.sb(es, "q_tok%d" % i, [128, 3, 128], BF16) for i in range(4)]
            ntok = 0
            blocks = [(128 * i, 128) for i in range(17)] + [(2176, 16)]
            for c in range(DC):
                u = c % nb
                kr, kk_, kv, ka, kd = (("q", n, u) for n in ("r", "k", "v", "a", "d"))
                P.dma("sp", r_[u][:, :], self.rS.ap()[c], ["rS"], [kr])
                P.dma("sp", k_[u][:, :], self.kS.ap()[c], ["kS"], [kk_])
                P.dma("sp", v_[u][:, :], self.vS.ap()[c], ["vS"], [kv])
                P.dma("sp", a_[u][:, :], self.aS.ap()[c], ["aS"], [ka])
                P.dma("sp", dd[u][:, :], self.dS.ap()[c], ["dS"], [kd])
                self.act(dd[u][:, :], dd[u][:, :], AF.Exp, [kd], [kd], scale=-math.exp(-0.5))
                P.dma("sp", self.dS.ap()[c], dd[u][:, :], [kd], [("dS2", c)])
                self.ts(kk[:, :], k_[u][:, :], kkc[:, c:c + 1], ALU.mult, [kk_, K], ["kk"])
                self.act(t1[:, :], kk[:, :], AF.Square, ["kk"], ["t1"])
                for si, (t0, n) in enumerate(SEGS):
                    b = self.bank()
                    self.mm(self.ps[b][:, 0:n], self.bones[:, :], t1[:, t0:t0 + n], True, True, ["t1", "bones"], [("ps", b)])
                    self.act(t2[:, t0:t0 + n], self.ps[b][:, 0:n], AF.Sqrt, [("ps", b)], ["t2"])
                self.ts(t2[:, :], t2[:, :], 1e-12, ALU.max, ["t2"], ["t2"])
                self.P.op("dve", lambda e: e.reciprocal(out=t2[:, :], in_=t2[:, :]), ["t2"], ["t2"])
                self.tt(kk[:, :], kk[:, :], t2[:, :], ALU.mult, ["kk", "t2"], ["kk"])
                self.ts(av[u][:, :], kk[:, :], -1.0, ALU.mult, ["kk"], [("q", "av", u)])
                P.dma("sp", self.avS.ap()[c], av[u][:, :], [("q", "av", u)], ["avS"])
                self.cp(rb[u][:, :], r_[u][:, :], [kr], [("q", "rb", u)], eng="act")
                P.dma("sp", self.rbS.ap()[c], rb[u][:, :], [("q", "rb", u)], ["rbS"])
                self.tt(kk[:, :], kk[:, :], a_[u][:, :], ALU.mult, ["kk", ka], ["kk"])
                self.ts(t1[:, :], a_[u][:, :], -1.0, ALU.add, [ka, K], ["t1"], s2=kac[:, c:c + 1], op1=ALU.mult)
                self.ts(t1[:, :], t1[:, :], 1.0, ALU.add, ["t1"], ["t1"])
                self.tt(k_[u][:, :], k_[u][:, :], t1[:, :], ALU.mult, [kk_, "t1"], [kk_])
                self.stt(t1[:, :], r_[u][:, :], rkc[:, c:c + 1], k_[u][:, :], ALU.mult, ALU.mult, [kr, kk_, K], ["t1"])
                for si, (t0, n) in enumerate(SEGS):
                    b = self.bank()
                    self.mm(self.ps[b][:, 0:n], self.bones[:, :], t1[:, t0:t0 + n], True, True, ["t1", "bones"], [("ps", b)])
                    self.tt(t2[:, t0:t0 + n], v_[u][:, t0:t0 + n], self.ps[b][:, 0:n], ALU.mult, [kv, ("ps", b)], ["t2"])
                P.dma("sp", self.bonS.ap()[c], t2[:, :], ["t2"], ["bonS"])
                for (t0, n) in blocks:
                    tk = tok[ntok % 4]; tkk = ("q_tok", ntok % 4); ntok += 1
                    for ti, (src, sk) in enumerate(((kk, "kk"), (k_[u], kk_), (v_[u], kv))):
                        b = self.bank()
                        self.tr(self.ps[b][0:n, 0:128], src[:, t0:t0 + n], self.ident[:, :], [sk, "ident"], [("ps", b)])
                        self.cp(tk[0:n, ti, :], self.ps[b][0:n, 0:128], [("ps", b)], [tkk], eng=("act" if ti != 1 else "dve"))
                    for ti, dst in enumerate((self.btok, self.ktok, self.vtok)):
                        P.dma("sp", dst.ap()[t0:t0 + n, c * 128:(c + 1) * 128], tk[0:n, ti, :], [tkk], [dst.name])
            P.fence()


NSL = 16


class RwScanMixin:
    def rw_scan(self, nsplit=4):
        import os
        PE_ = "dve" if os.environ.get("RW_NOPOOL") else "pool"
        P = self.P
        d = self.din
        HH = 16 // nsplit
        with ExitStack() as es:
            F = lambda n, shp: self.sb(es, n, shp, F32)
            Bf = lambda n, shp: self.sb(es, n, shp, BF16)
            ST = [F("w_ST%d" % i, [128, 16, 64]) for i in range(2)]
            Sb = [Bf("w_Sb%d" % i, [128, 16, 64]) for i in range(2)]
            tmp = [F("w_tmp%d" % i, [128, 16, 64]) for i in range(2)]
            ZV = Bf("w_ZV", [6, NSL, 1024])
            BK = Bf("w_BK", [6, NSL, 16, 128])
            AR = [Bf("w_AR%d" % i, [128, 16, 64, 4]) for i in range(2)]
            WD = [F("w_WD%d" % i, [128, 16, 64]) for i in range(2)]
            rblk = [Bf("w_rb%d" % i, [128, 16, 64]) for i in range(2)]
            ablk = [Bf("w_ab%d" % i, [128, 16, 64]) for i in range(2)]
            stin = [F("w_sti%d" % i, [64, 16, 2, 64]) for i in range(2)]
            stout = [F("w_sto%d" % i, [64, 16, 128]) for i in range(2)]
            pzb = [self.ps[h] for h in range(nsplit)]
            pub = [self.ps[4 + h] for h in range(nsplit)]
            pt = self.psbig[3]
            W = HH * 64
            self.memset(BK[:, :, :, :], 0.0, [("BK", s_) for s_ in range(NSL)], eng=PE_)
            self.memset(ZV[:, :, :], 0.0, [("ZV", s_, h) for s_ in range(NSL) for h in range(nsplit)] + [("ZVv", s_) for s_ in range(NSL)])
            hp4 = lambda ap: ap.rearrange("t (hg hp k) -> hp t hg k", hp=2, k=64)
            G = [0]
            blkc = [0]
            nst = [0]

            def chunks(lo, hi, g0):
                out = []
                a = lo
                while a < hi:
                    ga = g0 + a
                    b = min(hi, a + (8 - ga % 8))
                    out.append((a, b, ga % NSL))
                    a = b
                return out

            def run_seq(t0, L, sq, q):
                g0 = G[0]
                if q is None:
                    self.memset(ST[sq][:, :, :], 0.0, [("ST", sq, h) for h in range(nsplit)])
                    self.memset(Sb[sq][:, :, :], 0.0, [("Sb", sq, h) for h in range(nsplit)])
                else:
                    si_ = stin[q % 2]
                    P.dma("sp", si_[:, :, :, :], d["st_rw"].ap()[q].rearrange("(hg hp) v k -> v hg hp k", hp=2), (), [("sti", q % 2)])
                    for hg in range(16):
                        self.tr(pt[:, hg * 64:(hg + 1) * 64], si_[:, hg, :, :].rearrange("v hp k -> v (hp k)"), self.ident[0:64, 0:64],
                                [("sti", q % 2), "ident"], ["pt"])
                    for h in range(nsplit):
                        sl = slice(h * HH, (h + 1) * HH)
                        src = pt[:, h * W:(h + 1) * W].rearrange("p (g v) -> p g v", v=64)
                        self.cp(ST[sq][:, sl, :], src, ["pt"], [("ST", sq, h)])
                        self.cp(Sb[sq][:, sl, :], ST[sq][:, sl, :], [("ST", sq, h)], [("Sb", sq, h)], eng="act")
                import os as _os
                ch = chunks(0, L, g0)
                ch_at = {a: i for i, (a, b, s0) in enumerate(ch)}
                bi_base = blkc[0]
                nblk = (L + 1 + 63) // 64
                blkc[0] += nblk

                def build_tables(Bk):
                    j = 64 * Bk
                    bi = (bi_base + Bk) % 2
                    nb = min(64, L + 1 - j)
                    ark, wdk = ("AR", bi), ("WD", bi)
                    self.memset(AR[bi][:, :, :, :], 0.0, [ark], eng="dve")
                    ga = max(j, 1)
                    nr = j + nb - ga
                    if nr > 0:
                        P.dma("act", rblk[bi][:, :, 0:nr], self.rbS.ap()[:, :, t0 + ga - 1:t0 + ga - 1 + nr].rearrange("c p t -> p c t"), (), [("rb", bi)])
                        for hp in range(2):
                            self.ts(AR[bi][:, :, ga - j:ga - j + nr, hp], rblk[bi][:, :, 0:nr], self.m01[:, hp:hp + 1], ALU.mult,
                                    [("rb", bi), "m01"], [ark], eng="dve")
                    na = min(j + nb, L) - j
                    if na > 0:
                        P.dma("act", ablk[bi][:, :, 0:na], self.avS.ap()[:, :, t0 + j:t0 + j + na].rearrange("c p t -> p c t"), (), [("ab", bi)])
                        for hp in range(2):
                            self.ts(AR[bi][:, :, 0:na, 2 + hp], ablk[bi][:, :, 0:na], self.m01[:, hp:hp + 1], ALU.mult,
                                    [("ab", bi), "m01"], [ark], eng="dve")
                        P.dma("act", WD[bi][:, :, 0:na], self.dS.ap()[:, :, t0 + j:t0 + j + na].rearrange("c p t -> p c t"), (), [wdk])

                def fill(i):
                    (a, b, s0) = ch[i]
                    n = b - a
                    ta, tb = t0 + a, t0 + b
                    wk_ = [("BK", (s0 + x) % NSL) for x in range(n)]
                    P.dma("sp", BK[2:3, s0:s0 + n, :, 0:64], hp4(self.btok.ap()[ta:tb, :])[0:1], (), wk_)
                    P.dma("sp", BK[3:4, s0:s0 + n, :, 64:128], hp4(self.btok.ap()[ta:tb, :])[1:2], (), wk_)
                    P.dma("sp", BK[4:5, s0:s0 + n, :, 0:64], hp4(self.ktok.ap()[ta:tb, :])[0:1], (), wk_)
                    P.dma("sp", BK[5:6, s0:s0 + n, :, 64:128], hp4(self.ktok.ap()[ta:tb, :])[1:2], (), wk_)
                    P.dma("sp", ZV[4:6, s0:s0 + n, :].rearrange("r s (g v) -> r s g v", v=64), hp4(self.vtok.ap()[ta:tb, :]), (),
                          [("ZVv", (s0 + x) % NSL) for x in range(n)])

                for j in range(L + 1 if not _os.environ.get("RW_SKIPGRP") else 0):
                    jj = j % 64
                    if jj == 0:
                        if j == 0:
                            build_tables(0)
                        if j // 64 + 1 < nblk:
                            build_tables(j // 64 + 1)
                    slot = (g0 + j) % NSL
                    if j in ch_at:
                        i = ch_at[j]
                        if i == 0:
                            fill(0)
                        if i + 1 < len(ch):
                            fill(i + 1)
                    bi = (bi_base + j // 64) % 2
                    for h in range(nsplit):
                        for g_ in range(HH):
                            hg = h * HH + g_
                            self.mm(pzb[h][0:4, g_ * 64:(g_ + 1) * 64], AR[bi][:, hg, jj, :], Sb[sq][:, hg, :], True, True,
                                    [("AR", bi), ("Sb", sq, h)], [("pz", h)])
                        cs = slice(h * W, (h + 1) * W)
                        self.cp(ZV[0:4, slot, cs], pzb[h][0:4, 0:W], [("pz", h)], [("ZV", slot, h)], eng="act")
                    if j < L:
                        ub = j % 2
                        for h in range(nsplit):
                            sl = slice(h * HH, (h + 1) * HH)
                            for g_ in range(HH):
                                hg = h * HH + g_
                                self.mm(pub[h][:, g_ * 64:(g_ + 1) * 64], BK[0:6, slot, hg, :], ZV[0:6, slot, hg * 64:(hg + 1) * 64], True, True,
                                        [("BK", slot), ("ZV", slot, h), ("ZVv", slot)], [("pu", h)] + (["pt"] if 4 + h >= 6 else []))
                            wb_ = WD[bi][:, sl, jj:jj + 1].to_broadcast([128, HH, 64])
                            self.tt(tmp[ub][:, sl, :], ST[sq][:, sl, :], wb_, ALU.mult, [("ST", sq, h), ("WD", bi)], [("tmp", ub, h)], eng=(PE_ if h < nsplit - 1 else "dve"))
                            self.tt(ST[sq][:, sl, :], tmp[ub][:, sl, :], pub[h][:, 0:W].rearrange("p (g v) -> p g v", v=64), ALU.add,
                                    [("tmp", ub, h), ("pu", h)], [("ST", sq, h)])
                            self.cp(Sb[sq][:, sl, :], ST[sq][:, sl, :], [("ST", sq, h)], [("Sb", sq, h)], eng="act")
                    if j >= 1 and ((g0 + j) % 8 == 7 or j == L):
                        ga = max(1, j - ((g0 + j) % 8))
                        n = j + 1 - ga
                        s0 = (g0 + ga) % NSL
                        P.dma("sp", hp4(self.ytok.ap()[t0 + ga - 1:t0 + ga - 1 + n, :]),
                              ZV[0:2, s0:s0 + n, :].rearrange("r s (g v) -> r s g v", v=64),
                              [("ZV", (s0 + x) % NSL, h) for x in range(n) for h in range(nsplit)], ["ytok"])
                G[0] = g0 + L + 1
                import os as _os
                if _os.environ.get("RW_SKIPFIN"):
                    return
                if _os.environ.get("RW_SKIPFIN_P") and q is None:
                    return
                if _os.environ.get("RW_SKIPFIN_S") and q is not None:
                    return
                so = stout[nst[0] % 2]; sok = ("sto", nst[0] % 2); nst[0] += 1
                for half in range(2):
                    for g_ in range(8):
                        hg = half * 8 + g_
                        self.tr(pt[0:64, g_ * 128:(g_ + 1) * 128], ST[sq][:, hg, :], self.ident[:, :], [("ST", sq, hg // HH), "ident"], ["pt"])
                    for bb in range(2):
                        self.cp(so[:, half * 8 + bb * 4:half * 8 + bb * 4 + 4, :],
                                pt[0:64, bb * 512:(bb + 1) * 512].rearrange("p (g k) -> p g k", k=128), ["pt"], [sok],
                                eng=("act" if bb else "dve"))
                dst = self.dout["p_rw"].ap() if q is None else self.dout["s_rw"].ap()[q]
                P.dma("sp", dst.rearrange("(hg hp) v k -> v hg hp k", hp=2), so[:, :, :].rearrange("v g (hp k) -> v g hp k", hp=2), [sok],
                      ["rw_out"], is_output=True)
                P.fence()

            import os
            lim = int(os.environ.get("RW_LIMIT", "-1"))
            if lim < 0:
                run_seq(0, TP, 0, None)
                for q in range(NSEQ):
                    run_seq(TP + 8 * q, 8, (q + 1) % 2, q)
            else:
                run_seq(0, lim, 0, None)
                for q in range(int(os.environ.get("RW_NQ", "0"))):
                    run_seq(TP + 8 * q, 8, (q + 1) % 2, q)
            P.fence()

    def rw_out(self):
        P = self.P
        d = self.din
        K = "rwo"
        with ExitStack() as es:
            F = lambda n, shp: self.sb(es, n, shp, F32)
            lnw, lnb = F("o_lnw", [128, DC]), F("o_lnb", [128, DC])
            self.load_cols(lnw[:, :], d["rw_ln_w"].ap().rearrange("o (c p) -> (o c) p", p=128), DC, K)
            self.load_cols(lnb[:, :], d["rw_ln_b"].ap().rearrange("o (c p) -> (o c) p", p=128), DC, K)
            eps2 = F("o_eps", [128, 1])
            self.memset(eps2[:, :], 64e-5, [K])
            bonesb = self.sb(es, "o_bonesb", [128, 128], BF16)
            self.cp(bonesb[:, :], self.bones[:, :], ["bones"], [K])
            xo = self.sb(es, "o_xo", [128, DC, NT], BF16)
            with ExitStack() as es2:
                F2 = lambda n, shp: self.sb(es2, n, shp, F32)
                ytk = [F2("o_ytk%d" % i, [128, D]) for i in range(2)]
                blocks = [(128 * i, 128) for i in range(17)] + [(2176, 16)]
                for bi_, (t0, n) in enumerate(blocks):
                    u = bi_ % 2
                    P.dma("pool", ytk[u][0:n, :], self.ytok.ap()[t0:t0 + n, :], (), [("ytk", u)])
                    for g in range(4):
                        b = self.bank()
                        for c4 in range(4):
                            c = g * 4 + c4
                            self.tr(self.ps[b][:, c4 * 128:c4 * 128 + n], ytk[u][0:n, c * 128:(c + 1) * 128], self.ident[0:n, 0:n],
                                    [("ytk", u), "ident"], [("ps", b)])
                        self.cp(xo[:, g * 4:g * 4 + 4, t0:t0 + n], self.ps[b][:, :].rearrange("p (c t) -> p c t", c=4)[:, :, 0:n],
                                [("ps", b)], [("xo", g * 4 + c4) for c4 in range(4)], eng=("act" if g % 2 else "dve"))
                P.fence()
            with ExitStack() as es2:
                F2 = lambda n, shp: self.sb(es2, n, shp, F32)
                bon = [F2("o_bon%d" % i, [128, NT]) for i in range(2)]
                g2 = [F2("o_g2%d" % i, [128, NT]) for i in range(2)]
                yc, sq, rs = F2("o_yc", [128, NT]), F2("o_sq", [128, NT]), F2("o_rs", [128, NT])
                for c in range(DC):
                    u = c % 2
                    P.dma("sp", bon[u][:, :], self.bonS.ap()[c], (), [("bon", u)])
                    P.dma("act", g2[u][:, :], self.g2S.ap()[c], (), [("g2", u)])
                    for si, (t0, n) in enumerate(SEGS):
                        b = self.bank()
                        self.mm(self.ps[b][:, 0:n], bonesb[:, :], xo[:, c, t0:t0 + n], True, True, [("xo", c), K], [("ps", b)])
                        self.stt(yc[:, t0:t0 + n], self.ps[b][:, 0:n], -1.0 / 64, xo[:, c, t0:t0 + n], ALU.mult, ALU.add, [("ps", b), ("xo", c)], ["oyc"])
                    self.act(sq[:, :], yc[:, :], AF.Square, ["oyc"], ["osq"])
                    for si, (t0, n) in enumerate(SEGS):
                        b = self.bank()
                        self.mm(self.ps[b][:, 0:n], self.bones[:, :], sq[:, t0:t0 + n], True, True, ["osq", "bones"], [("ps", b)])
                        self.act(rs[:, t0:t0 + n], self.ps[b][:, 0:n], AF.Sqrt, [("ps", b), K], ["ors"], bias=eps2[:, 0:1], scale=1.0 / 64)
                    self.P.op("dve", lambda e: e.reciprocal(out=rs[:, :], in_=rs[:, :]), ["ors"], ["ors"])
                    self.tt(yc[:, :], yc[:, :], rs[:, :], ALU.mult, ["oyc", "ors"], ["oyc"])
                    self.ts(yc[:, :], yc[:, :], lnw[:, c:c + 1], ALU.mult, ["oyc", K], ["oyc"], s2=lnb[:, c:c + 1], op1=ALU.add)
                    self.tt(yc[:, :], yc[:, :], bon[u][:, :], ALU.add, ["oyc", ("bon", u)], ["oyc"])
                    self.tt(xo[:, c, :], yc[:, :], g2[u][:, :], ALU.mult, ["oyc", ("g2", u)], [("xo", c)])
                P.fence()
            self.out_proj_res(xo, lambda kt, si: ("xo", kt), DC, d["rw_w_o"].ap(), tag="wo2")

    def final_out(self):
        P = self.P
        lnw = self.c_ln[:, 64:80]
        tiles = [(128 * i, 128) for i in range(16)] + [(2048, 16), (2064, 128)]
        with ExitStack() as es:
            F = lambda n, shp: self.sb(es, n, shp, F32)
            hs = [F("z_hs%d" % i, [128, DC, 128]) for i in range(2)]
            sq = [F("z_sq%d" % i, [128, 128]) for i in range(2)]
            rs = [F("z_rs%d" % i, [128, 128]) for i in range(2)]
            xn = [F("z_xn%d" % i, [128, DC, 128]) for i in range(2)]
            tok = [F("z_tok%d" % i, [128, D]) for i in range(2)]
            for ti, (t0, n) in enumerate(tiles):
                u = ti % 2
                P.dma("sp", hs[u][:, :, 0:n], self.hT.ap()[:, :, t0:t0 + n].rearrange("c p t -> p c t"), (), [("zhs", u)])
                b = self.bank()
                for c in range(DC):
                    self.act(sq[c % 2][:, 0:n], hs[u][:, c, 0:n], AF.Square, [("zhs", u)], [("zsq", c % 2)])
                    self.mm(self.ps[b][:, 0:n], self.ones[:, :], sq[c % 2][:, 0:n], c == 0, c == DC - 1, [("zsq", c % 2), "ones"], [("ps", b)])
                self.act(rs[u][:, 0:n], self.ps[b][:, 0:n], AF.Sqrt, [("ps", b), "epsD"], [("zrs", u)], bias=self.epsD[:, 0:1], scale=1.0 / D)
                self.P.op("dve", lambda e, r_=rs[u], n=n: e.reciprocal(out=r_[:, 0:n], in_=r_[:, 0:n]), [("zrs", u)], [("zrs", u)])
                for c in range(DC):
                    self.stt(xn[u][:, c, 0:n], hs[u][:, c, 0:n], lnw[:, c:c + 1], rs[u][:, 0:n], ALU.mult, ALU.mult, [("zhs", u), ("zrs", u), "cols"], [("zxn", u)])
                for g in range(4):
                    b = self.bank()
                    for c4 in range(4):
                        c = g * 4 + c4
                        self.tr(self.ps[b][0:n, c4 * 128:(c4 + 1) * 128], xn[u][:, c, 0:n], self.ident[:, :], [("zxn", u), "ident"], [("ps", b)])
                    self.cp(tok[u][0:n, g * 512:(g + 1) * 512], self.ps[b][0:n, :], [("ps", b)], [("ztok", u)], eng=("act" if g % 2 else "dve"))
                if t0 == 0:
                    P.dma("sp", self.dout["y_p"].ap()[0:112, :], tok[u][16:128, :], [("ztok", u)], ["y_p"], is_output=True)
                elif t0 < TP:
                    P.dma("sp", self.dout["y_p"].ap()[t0 - 16:t0 - 16 + n, :], tok[u][0:n, :], [("ztok", u)], ["y_p"], is_output=True)
                else:
                    P.dma("sp", self.dout["y_s"].ap()[:, :], tok[u][0:n, :], [("ztok", u)], ["y_s"], is_output=True)
            P.fence()

class Builder(BuilderBase, S5Mixin, HgMixin, FfnMixin, RwMixin, RwScanMixin):
    def build(self):
        if self.upto == "scanonly":
            self.inp("st_rw", [NSEQ, 32, 64, 64])
            self.outp("p_rw", [32, 64, 64]); self.outp("s_rw", [NSEQ, 32, 64, 64])
            self.scr_in = {"rbS", "avS", "dS", "btok", "ktok", "vtok"}
            self.debug = {"ytok"}
            self.consts()
            self.rw_declare()
            self.rw_scan()
            return self.finish()
        self.declare()
        self.consts()
        self.param_cols()
        self.stage_in()
        if self.upto == "in":
            return self.finish()
        es0 = ExitStack()
        xnT = self.sb(es0, "xnT", [128, DC, NT], BF16)
        self.norm(xnT, self.c_ln[:, 0:16])
        self.l0_inproj(xnT)
        es0.close()
        if self.upto == "inproj":
            return self.finish()
        es1 = ExitStack()
        yT = self.sb(es1, "yT", [128, DC, NT], BF16)
        self.s5_setup()
        self.s5_main(yT)
        self.es_s5.close()
        self.s5_glu(yT)
        if self.upto != "s5":
            self.hg_main(yT)
        if self.upto == "hg":
            dbg = self.scratch("dbg_yb", [8, 128, NT], BF16)
            self.P.dma("sp", dbg.ap().rearrange("c p t -> p c t"), yT[:, 8:16, :], [("yT", j, si) for j in range(8, 16) for si in range(5)], ["dbg"], is_output=True)
            return self.finish()
        if self.upto not in ("s5", "hg"):
            self.out_proj_res(yT, lambda kt, si: ("yT", kt, si), DC, self.din["ev_w_out"].ap())
            es1.close()
            es3 = ExitStack()
            xnT = self.sb(es3, "xnT", [128, DC, NT], BF16)
            self.norm(xnT, self.c_ln[:, 32:48])
            self.ffn(0, xnT)
            es3.close()
            self.ffn_down(0)
        if self.upto == "l0":
            return self.finish()
        if self.upto not in ("s5", "hg"):
            self.layer1()
            return self.finish()
        if self.upto == "s5":
            dbg = self.scratch("dbg_ya", [8, 128, NT], BF16)
            self.P.dma("sp", dbg.ap().rearrange("c p t -> p c t"), yT[:, 0:8, :], [("yT", j, si) for j in range(8) for si in range(5)], ["dbg"], is_output=True)
            return self.finish()
        return self.finish()

    def layer1(self):
        self.rw_declare()
        es3 = ExitStack()
        xnT = self.sb(es3, "xnT", [128, DC, NT], BF16)
        es4 = ExitStack()
        xn32 = self.sb(es4, "xn32", [128, DC, 144], F32)
        self.norm(xnT, self.c_ln[:, 16:32], out32=xn32)
        self.rw_shift_out(xn32)
        es4.close()
        self.rw_proj(xnT)
        es3.close()
        if "dbg_xj" in self.debug:
            return
        self.rw_post()
        if self.upto == "rwpost":
            return
        self.rw_scan()
        if self.upto == "rwscan":
            return
        self.rw_out()
        if self.upto == "l1a":
            return
        es3 = ExitStack()
        xnT = self.sb(es3, "xnT", [128, DC, NT], BF16)
        self.norm(xnT, self.c_ln[:, 48:64])
        self.ffn(1, xnT)
        es3.close()
        self.ffn_down(1)
        self.final_out()

    def finish(self):
        cnt = self.P.emit()
        return self.nc, cnt


_CACHE = {}


def kernel(**inputs):
    n_cores = 8
    if "nc" not in _CACHE:
        B = Builder(upto="all")
        nc, _ = B.build()
        _CACHE["nc"] = nc
        _CACHE["names"] = list(B.din)
    nc = _CACHE["nc"]
    names = _CACHE["names"]
    shared = None
    in_maps = []
    for c in range(n_cores):
        m = _prep_inputs(inputs, c)
        in_maps.append({k: m[k] for k in names})
    res = run_bass_kernel_spmd(nc, in_maps, core_ids=list(range(n_cores))).results
    f = np.float32
    st = lambda k, cores: np.stack([np.asarray(res[c][k], dtype=f) for c in cores])
    cat = lambda k: np.concatenate([np.asarray(res[c][k], dtype=f) for c in range(n_cores)], axis=0)
    p = range(4)
    y_prompt = st("y_p", p)
    y_sample = cat("y_s").reshape(128, 8, D)
    p_s5r = st("p_s5r", p)[None]
    p_s5i = st("p_s5i", p)[None]
    p_hg = st("p_hg", p)[None]
    p_rw = st("p_rw", p)[None]
    p_sh = st("p_sh", p).reshape(1, 4, D)
    p_cv = np.stack([np.asarray(res[c]["p_cv"], dtype=f) for c in p], axis=1)
    s_s5r = cat("s_s5r")[None]
    s_s5i = cat("s_s5i")[None]
    s_hg = cat("s_hg")[None]
    s_rw = cat("s_rw")[None]
    s_sh = cat("s_sh")[None]
    s_cv = np.concatenate([np.asarray(res[c]["s_cv"], dtype=f) for c in range(n_cores)], axis=1)
    return (y_prompt, y_sample, p_s5r, p_s5i, p_hg, p_rw, p_sh, p_cv, s_s5r, s_s5i, s_hg, s_rw, s_sh, s_cv)
```

```python
import math
from contextlib import ExitStack
import numpy as np
import concourse.bass as bass
import concourse.mybir as mybir
from concourse.bass_utils import run_bass_kernel_spmd

F32 = mybir.dt.float32
BF16 = mybir.dt.bfloat16
AF = mybir.ActivationFunctionType
ALU = mybir.AluOpType
AX = mybir.AxisListType

D = 2048
DC = 16
TP = 2064
NSEQ = 16
TS = 128
NT = TP + TS
SEGS = [(0, 512), (512, 512), (1024, 512), (1536, 512), (2048, 144)]
EVEN_IN = 5120
D_FF = 5632
FC = 44
EPS = 1e-6

COMPUTE = ("pe", "act", "dve", "pool")
NRING = 8


class Op:
    __slots__ = ("eng", "fn", "deps", "is_dma", "ring", "ring_val", "sig", "sigval", "idx")


class Prog:
    def __init__(self, nc):
        self.nc = nc
        self.ops = []
        self.last_w = {}
        self.readers = {}
        self.ring_pos = {"sp": 0, "act": 0, "pool": 0}
        self.ring_cnt = {}
        self.ring_last = {}
        self.out_dmas = []
        self.last_on = {}

    def _add(self, eng, fn, reads, writes, is_dma, extra_deps=()):
        op = Op()
        op.eng, op.fn, op.is_dma = eng, fn, is_dma
        op.idx = len(self.ops)
        op.sig = False
        op.sigval = None
        deps = set(extra_deps)
        for k in reads:
            w = self.last_w.get(k)
            if w is not None:
                deps.add(w)
        for k in writes:
            w = self.last_w.get(k)
            if w is not None:
                deps.add(w)
            for r in self.readers.get(k, ()):
                deps.add(r)
        if is_dma:
            q = eng
            slot = self.ring_pos[q] % NRING
            self.ring_pos[q] += 1
            key = (q, slot)
            prev = self.ring_last.get(key)
            if prev is not None:
                deps.add(prev)
            self.ring_cnt[key] = self.ring_cnt.get(key, 0) + 16
            op.ring = key
            op.ring_val = self.ring_cnt[key]
            self.ring_last[key] = op.idx
        else:
            op.ring = None
            op.ring_val = None
            if fn is not None:
                self.last_on[eng] = op.idx
        best = {}
        red = set()
        for dd in deps:
            dop = self.ops[dd]
            if dop.is_dma or dop.fn is None:
                red.add(dd)
            else:
                if best.get(dop.eng, -1) < dd:
                    best[dop.eng] = dd
        red.update(best.values())
        op.deps = red
        for k in writes:
            self.last_w[k] = op.idx
            self.readers[k] = []
        for k in reads:
            if k not in writes:
                self.readers.setdefault(k, []).append(op.idx)
        self.ops.append(op)
        return op

    def op(self, eng, fn, reads=(), writes=()):
        return self._add(eng, fn, tuple(reads), tuple(writes), False)

    def dma(self, q, out, in_, reads=(), writes=(), is_output=False, **kw):
        def fn(e):
            return e.dma_start(out=out, in_=in_, **kw)
        o = self._add(q, fn, tuple(reads), tuple(writes), True)
        if is_output:
            self.out_dmas.append(o.idx)
        return o

    def fence(self):
        deps = set(self.last_on.values()) | set(self.ring_last.values())
        for e in ("pe", "act", "dve", "pool", "sp"):
            self._add(e, None, (), (), False, extra_deps=deps)
        self.last_w = {}
        self.readers = {}

    def emit(self):
        nc = self.nc
        ops = self.ops
        for o in ops:
            for d in o.deps:
                dop = ops[d]
                if not dop.is_dma and dop.fn is not None:
                    if dop.eng == "pe" and o.eng == "pe" and not o.is_dma and o.fn is not None:
                        continue
                    dop.sig = True
        cnt = {e: 0 for e in COMPUTE}
        for o in ops:
            if not o.is_dma and o.sig:
                assert o.fn is not None
                cnt[o.eng] += 1
                o.sigval = cnt[o.eng]
        sems = {}
        for e in COMPUTE:
            sems[e] = nc.alloc_semaphore("c_" + e)
        for q in ("sp", "act", "pool"):
            for s in range(NRING):
                sems[(q, s)] = nc.alloc_semaphore("d_%s%d" % (q, s))
        streams = {e: [] for e in ("pe", "act", "dve", "pool", "sp")}
        for o in ops:
            streams[o.eng].append(o)
        final_waits = [(ops[i].ring, ops[i].ring_val) for i in self.out_dmas]

        def run(engname, e):
            waited = {}
            for o in streams[engname]:
                need = {}
                for d in o.deps:
                    dop = ops[d]
                    if dop.fn is None:
                        continue
                    if dop.is_dma:
                        k, v = dop.ring, dop.ring_val
                    else:
                        if dop.eng == "pe" and engname == "pe" and not o.is_dma and o.fn is not None:
                            continue
                        k, v = dop.eng, dop.sigval
                    if need.get(k, 0) < v:
                        need[k] = v
                for k, v in need.items():
                    if waited.get(k, 0) >= v:
                        continue
                    e.wait_ge(sems[k], v)
                    waited[k] = v
                if o.fn is None:
                    continue
                ins = o.fn(e)
                if o.is_dma:
                    ins.then_inc(sems[o.ring], 16)
                elif o.sig:
                    ins.then_inc(sems[o.eng], 1)
            if engname == "sp":
                for k, v in final_waits:
                    if waited.get(k, 0) >= v:
                        continue
                    e.wait_ge(sems[k], v)
                    waited[k] = v

        with nc.Block() as block:
            @block.sync
            def _(e):
                run("sp", e)

            @block.tensor
            def _(e):
                run("pe", e)

            @block.vector
            def _(e):
                run("dve", e)

            @block.scalar
            def _(e):
                run("act", e)

            @block.gpsimd
            def _(e):
                run("pool", e)
        return cnt


class BuilderBase:
    def __init__(self, upto="all", debug=()):
        self.nc = nc = bass.Bass("TRN2", target_bir_lowering=False)
        self.P = Prog(nc)
        self.upto = upto
        self.debug = set(debug)
        self.bank_i = 0
        self.din = {}
        self.dout = {}
        self.psbig = [nc.alloc_psum_tensor("psb%d" % i, [128, 1024], F32) for i in range(4)]
        self.ps = [self.psbig[i // 2][:, (i % 2) * 512:(i % 2) * 512 + 512] for i in range(8)]

    def inp(self, name, shape):
        t = self.nc.dram_tensor(name, list(shape), F32, kind="ExternalInput")
        self.din[name] = t
        return t

    def outp(self, name, shape):
        t = self.nc.dram_tensor(name, list(shape), F32, kind="ExternalOutput")
        self.dout[name] = t
        return t

    def scratch(self, name, shape, dt=F32):
        if name in getattr(self, "scr_in", ()):
            t = self.nc.dram_tensor(name, list(shape), dt, kind="ExternalInput")
            self.din[name] = t
            return t
        kind = "ExternalOutput" if name in self.debug else "Internal"
        t = self.nc.dram_tensor(name, list(shape), dt, kind=kind)
        if name in self.debug:
            self.dout[name] = t
        return t

    def bank(self):
        pool = getattr(self, "bank_pool", None) or list(range(8))
        b = pool[self.bank_i % len(pool)]
        self.bank_i += 1
        return b

    def sb(self, es, name, shape, dt=F32):
        self.uid = getattr(self, "uid", 0) + 1
        return es.enter_context(self.nc.sbuf_tensor("%s_%d" % (name, self.uid), list(shape), dt))

    def mm(self, out, lhsT, rhs, start, stop, r, w):
        self.P.op("pe", lambda e: e.matmul(out, lhsT=lhsT, rhs=rhs, start=start, stop=stop), r, w)

    def tr(self, out, in_, ident, r, w):
        self.P.op("pe", lambda e: e.transpose(out, in_, ident), r, w)

    def act(self, out, in_, func, r, w, bias=None, scale=None):
        kw = {}
        if bias is not None:
            kw["bias"] = bias
        if scale is not None:
            kw["scale"] = scale
        self.P.op("act", lambda e: e.activation(out=out, in_=in_, func=func, **kw), r, w)

    def tt(self, out, a, b, op, r, w, eng="dve"):
        self.P.op(eng, lambda e: e.tensor_tensor(out=out, in0=a, in1=b, op=op), r, w)

    def ts(self, out, a, s1, op0, r, w, s2=None, op1=None, eng="dve"):
        if op1 is None:
            self.P.op(eng, lambda e: e.tensor_scalar(out=out, in0=a, scalar1=s1, scalar2=None, op0=op0), r, w)
        else:
            self.P.op(eng, lambda e: e.tensor_scalar(out=out, in0=a, scalar1=s1, scalar2=s2, op0=op0, op1=op1), r, w)

    def stt(self, out, a, s, b, op0, op1, r, w):
        self.P.op("dve", lambda e: e.scalar_tensor_tensor(out=out, in0=a, scalar=s, in1=b, op0=op0, op1=op1), r, w)

    def cp(self, out, in_, r, w, eng="dve"):
        if eng == "act":
            self.P.op("act", lambda e: e.copy(out=out, in_=in_), r, w)
        else:
            self.P.op(eng, lambda e: e.tensor_copy(out=out, in_=in_), r, w)

    def memset(self, ap, v, w, eng="dve"):
        self.P.op(eng, lambda e: e.memset(ap, v), (), w)

    def scan(self, out, d0, d1, init, r, w):
        self.P.op("dve", lambda e: e.tensor_tensor_scan(out=out, data0=d0, data1=d1, initial=init, op0=ALU.mult, op1=ALU.add), r, w)

    def consts(self):
        nc, P = self.nc, self.P
        a = nc.alloc_sbuf_tensor
        self.ident = a("ident", [128, 128], F32)
        self.identb = a("identb", [128, 128], BF16)
        self.ones = a("ones", [128, 128], F32)
        self.bones = a("bones", [128, 128], F32)
        self.epsD = a("epsD", [128, 1], F32)
        self.m01 = a("m01", [128, 2], F32)
        self.tril = a("tril", [64, 64], F32)
        self.lcst = [a("lcst0", [128, 128], F32), a("lcst1", [128, 128], F32)]
        self.lc_i = 0
        P.op("pool", lambda e: e.memset(self.ident[:], 0.0), (), ["ident"])
        P.op("pool", lambda e: e.affine_select(out=self.ident[:], in_=self.ident[:], pattern=[[-1, 128]],
                                               compare_op=ALU.not_equal, fill=1.0, base=0, channel_multiplier=1),
             ["ident"], ["ident"])
        self.cp(self.identb[:], self.ident[:], ["ident"], ["identb"])
        self.memset(self.ones[:], 1.0, ["ones"])
        self.memset(self.epsD[:], EPS, ["epsD"])
        self.memset(self.bones[:], 0.0, ["bones"])
        self.memset(self.bones[0:64, 0:64], 1.0, ["bones"])
        self.memset(self.bones[64:128, 64:128], 1.0, ["bones"])
        self.memset(self.m01[:], 0.0, ["m01"])
        self.memset(self.m01[0:64, 0:1], 1.0, ["m01"])
        self.memset(self.m01[64:128, 1:2], 1.0, ["m01"])
        P.op("pool", lambda e: e.memset(self.tril[:], 1.0), (), ["tril"])
        P.op("pool", lambda e: e.affine_select(out=self.tril[:], in_=self.tril[:], pattern=[[1, 64]],
                                               compare_op=ALU.is_ge, fill=0.0, base=0, channel_multiplier=-1),
             ["tril"], ["tril"])

    def load_cols(self, dst, dram2d, R, key):
        P = self.P
        r0 = 0
        while r0 < R:
            n = min(128, R - r0)
            i = self.lc_i % 2
            self.lc_i += 1
            st = self.lcst[i]
            P.dma("sp", st[0:n, :], dram2d[r0:r0 + n, :], (), [("lcst", i)])
            b = self.bank()
            self.tr(self.ps[b][:, 0:n], st[0:n, :], self.ident[0:n, 0:n], [("lcst", i), "ident"], [("ps", b)])
            self.cp(dst[:, r0:r0 + n], self.ps[b][:, 0:n], [("ps", b)], [key])
            r0 += n

    def stage_in(self):
        P = self.P
        xp, meta, xs = self.din["xp"], self.din["meta"], self.din["xs"]
        hT = self.hT
        tiles = [(0, 128, [(meta.ap()[0:16, :], 0, 16), (xp.ap()[0:112, :], 16, 112)])]
        for i in range(1, 16):
            tiles.append((128 * i, 128, [(xp.ap()[128 * i - 16:128 * i + 112, :], 0, 128)]))
        tiles.append((2048, 16, [(xp.ap()[2032:2048, :], 0, 16)]))
        tiles.append((2064, 128, [(xs.ap()[:, :], 0, 128)]))
        with ExitStack() as es:
            tok = [self.sb(es, "tok%d" % i, [128, D], F32) for i in range(2)]
            hTt = [self.sb(es, "hTt%d" % i, [128, DC, 128], F32) for i in range(2)]
            for ti, (t0, n, srcs) in enumerate(tiles):
                tk = tok[ti % 2]
                ht = hTt[ti % 2]
                for (src, r0, nr) in srcs:
                    P.dma("sp", tk[r0:r0 + nr, :], src, (), [("tok", ti % 2)])
                for g in range(4):
                    b = self.bank()
                    for c in range(4):
                        cc = g * 4 + c
                        self.tr(self.ps[b][:, c * 128:c * 128 + n], tk[0:n, cc * 128:(cc + 1) * 128],
                                self.ident[0:n, 0:n], [("tok", ti % 2), "ident"], [("ps", b)])
                    src = self.ps[b][:, :].rearrange("p (c t) -> p c t", c=4)[:, :, 0:n]
                    self.cp(ht[:, g * 4:g * 4 + 4, 0:n], src, [("ps", b)], [("hTt", ti % 2)],
                            eng=("act" if g % 2 else "dve"))
                P.dma("sp", hT.ap()[:, :, t0:t0 + n].rearrange("c p t -> p c t"), ht[:, :, 0:n],
                      [("hTt", ti % 2)], ["hT"])
            P.fence()

    def norm(self, xnT, lnw, out32=None):
        P = self.P
        hT = self.hT
        with ExitStack() as es:
            hseg = [self.sb(es, "hseg%d" % i, [128, DC, 512], F32) for i in range(2)]
            sq = [self.sb(es, "sq%d" % i, [128, 512], F32) for i in range(2)]
            rs = [self.sb(es, "rs%d" % i, [128, 512], F32) for i in range(2)]
            for si, (t0, n) in enumerate(SEGS):
                hs = hseg[si % 2]
                hk = ("hseg", si % 2)
                P.dma("sp", hs[:, :, 0:n], hT.ap()[:, :, t0:t0 + n].rearrange("c p t -> p c t"), ["hT"], [hk])
                b = self.bank()
                for c in range(DC):
                    q = sq[c % 2]
                    self.act(q[:, 0:n], hs[:, c, 0:n], AF.Square, [hk], [("sq", c % 2)])
                    self.mm(self.ps[b][:, 0:n], self.ones[:, :], q[:, 0:n], c == 0, c == DC - 1,
                            [("sq", c % 2), "ones"], [("ps", b)])
                r = rs[si % 2]
                rk = ("rs", si % 2)
                self.act(r[:, 0:n], self.ps[b][:, 0:n], AF.Sqrt, [("ps", b), "epsD"], [rk],
                         bias=self.epsD[:, 0:1], scale=1.0 / D)
                self.P.op("dve", lambda e, r=r, n=n: e.reciprocal(out=r[:, 0:n], in_=r[:, 0:n]), [rk], [rk])
                for c in range(DC):
                    self.stt(xnT[:, c, t0:t0 + n], hs[:, c, 0:n], lnw[:, c:c + 1], r[:, 0:n], ALU.mult, ALU.mult,
                             [hk, rk, "cols"], [("xnT", c, si)])
                    if out32 is not None and si == len(SEGS) - 1:
                        self.stt(out32[:, c, 0:n], hs[:, c, 0:n], lnw[:, c:c + 1], r[:, 0:n], ALU.mult, ALU.mult,
                                 [hk, rk, "cols"], ["out32"])
            P.fence()

    def proj(self, es, xT, xkey, KT, w2d, col0, ncols, evac, wg=4, kparts=128, tag="w"):
        P = self.P
        ntile = (ncols + 127) // 128
        nbuf = 2
        wb = [self.sb(es, "%sb%d" % (tag, i), [128, KT, wg * 128], BF16) for i in range(nbuf)]
        gi = 0
        for j0 in range(0, ntile, wg):
            nj = min(wg, ntile - j0)
            w = wb[gi % nbuf]
            wk = (tag, gi % nbuf)
            gi += 1
            c_lo = j0 * 128
            c_hi = min(ncols, (j0 + nj) * 128)
            src = w2d[:, col0 + c_lo:col0 + c_hi].rearrange("(kt p) n -> p kt n", p=kparts)
            P.dma("pool", w[0:kparts, :, 0:c_hi - c_lo], src, (), [wk])
            for jj in range(nj):
                mw = min(128, ncols - (j0 + jj) * 128)
                for si, (t0, n) in enumerate(SEGS):
                    b = self.bank()
                    for kt in range(KT):
                        rhs = xT(kt, t0, n) if callable(xT) else xT[0:kparts, kt, t0:t0 + n]
                        self.mm(self.ps[b][0:mw, 0:n], w[0:kparts, kt, jj * 128:jj * 128 + mw],
                                rhs, kt == 0, kt == KT - 1, [wk, xkey(kt, si)], [("ps", b)])
                    evac(j0 + jj, si, t0, n, self.ps[b][0:mw, 0:n], ("ps", b))

    def declare(self):
        i = self.inp
        i("xp", [2048, D]); i("meta", [16, D]); i("xs", [TS, D])
        i("st_s5r", [NSEQ, 64, 64]); i("st_s5i", [NSEQ, 64, 64])
        i("st_hg", [NSEQ, 8, 128, 128]); i("st_rw", [NSEQ, 32, 64, 64])
        i("st_sh", [NSEQ, D]); i("st_cv", [2, NSEQ, 2, D_FF])
        i("ln_mix", [2, D]); i("ln_ffn", [2, D]); i("ln_final", [1, D])
        i("ev_w_in", [D, EVEN_IN]); i("ev_w_out", [D, D])
        i("s5_lam_re", [64, 64]); i("s5_lam_im", [64, 64]); i("s5_log_step", [1, 64])
        i("s5_b_re", [64, 64, 16]); i("s5_b_im", [64, 64, 16])
        i("s5_c_re", [64, 16, 64]); i("s5_c_im", [64, 16, 64]); i("s5_d", [64, 16])
        i("s5_w_glu", [1024, 1024]); i("hg_lb", [2, 1024]); i("hg_norm_w", [1, 128])
        i("rw_mu", [6, D]); i("rw_w0", [1, D]); i("rw_w1", [D, 96]); i("rw_w2", [96, D])
        i("rw_a0", [1, D]); i("rw_a1", [D, 96]); i("rw_a2", [96, D])
        i("rw_g1", [D, 256]); i("rw_g2", [256, D])
        i("rw_k_k", [1, D]); i("rw_k_a", [1, D]); i("rw_r_k", [1, D])
        i("rw_w_r", [D, D]); i("rw_w_k", [D, D]); i("rw_w_v", [D, D]); i("rw_w_o", [D, D])
        i("rw_ln_w", [1, D]); i("rw_ln_b", [1, D])
        i("ffn_w_in", [2, D, 2 * D_FF]); i("ffn_conv_w", [2, 3, D_FF]); i("ffn_conv_b", [2, D_FF])
        i("ffn_w_down", [2, D_FF, D])
        o = self.outp
        o("y_p", [2048, D]); o("y_s", [TS, D])
        o("p_s5r", [64, 64]); o("p_s5i", [64, 64]); o("p_hg", [8, 128, 128]); o("p_rw", [32, 64, 64])
        o("p_sh", [1, D]); o("p_cv", [2, 2, D_FF])
        o("s_s5r", [NSEQ, 64, 64]); o("s_s5i", [NSEQ, 64, 64]); o("s_hg", [NSEQ, 8, 128, 128])
        o("s_rw", [NSEQ, 32, 64, 64]); o("s_sh", [NSEQ, D]); o("s_cv", [2, NSEQ, 2, D_FF])
        self.hT = self.scratch("hT", [DC, 128, NT])
        self.zT = self.scratch("zT", [40, 128, NT])
        self.gS = self.scratch("gS", [len(SEGS), 128, FC, 512], BF16)

    def param_cols(self):
        a = self.nc.alloc_sbuf_tensor
        d = self.din
        self.c_ln = a("c_ln", [128, 5 * DC], F32)
        self.load_cols(self.c_ln[:, 0:32], d["ln_mix"].ap().rearrange("l (c p) -> (l c) p", p=128), 32, "cols")
        self.load_cols(self.c_ln[:, 32:64], d["ln_ffn"].ap().rearrange("l (c p) -> (l c) p", p=128), 32, "cols")
        self.load_cols(self.c_ln[:, 64:80], d["ln_final"].ap().rearrange("l (c p) -> (l c) p", p=128), 16, "cols")

    def l0_inproj(self, xnT):
        P = self.P
        with ExitStack() as es:
            zrow = [self.sb(es, "zrow%d" % i, [128, NT], F32) for i in range(2)]

            def evac(j, si, t0, n, ps, pk):
                z = zrow[j % 2]
                self.cp(z[:, t0:t0 + n], ps, [pk], [("zrow", j % 2, si)], eng=("act" if si % 2 else "dve"))
                if si == len(SEGS) - 1:
                    P.dma("sp", self.zT.ap()[j], z[:, :], [("zrow", j % 2, s) for s in range(len(SEGS))], ["zT"])

            self.proj(es, xnT, lambda kt, si: ("xnT", kt, si), DC, self.din["ev_w_in"].ap(), 0, EVEN_IN, evac)
            P.fence()


def _prep_inputs(inputs, core):
    b = core % 4
    f = lambda a: np.ascontiguousarray(np.asarray(a, dtype=np.float32))
    sl = slice(NSEQ * core, NSEQ * core + NSEQ)
    m = {
        "xp": f(inputs["x_prompt"][b]), "meta": f(inputs["meta_tokens"]),
        "xs": f(np.asarray(inputs["x_sample"])[sl].reshape(TS, D)),
        "st_s5r": f(np.asarray(inputs["state_s5_re"])[0, sl]), "st_s5i": f(np.asarray(inputs["state_s5_im"])[0, sl]),
        "st_hg": f(np.asarray(inputs["state_hgrn"])[0, sl]), "st_rw": f(np.asarray(inputs["state_rwkv"])[0, sl]),
        "st_sh": f(np.asarray(inputs["state_shift"])[0, sl]), "st_cv": f(np.asarray(inputs["state_conv"])[:, sl]),
        "ln_mix": f(inputs["ln_mix"]), "ln_ffn": f(inputs["ln_ffn"]), "ln_final": f(np.asarray(inputs["ln_final"]).reshape(1, D)),
        "ev_w_in": f(inputs["ev_w_in"][0]), "ev_w_out": f(inputs["ev_w_out"][0]),
        "s5_lam_re": f(inputs["s5_lam_re"][0]), "s5_lam_im": f(inputs["s5_lam_im"][0]),
        "s5_log_step": f(np.asarray(inputs["s5_log_step"]).reshape(1, 64)),
        "s5_b_re": f(inputs["s5_b_re"][0]), "s5_b_im": f(inputs["s5_b_im"][0]),
        "s5_c_re": f(inputs["s5_c_re"][0]), "s5_c_im": f(inputs["s5_c_im"][0]), "s5_d": f(inputs["s5_d"][0]),
        "s5_w_glu": f(inputs["s5_w_glu"][0]), "hg_lb": f(inputs["hg_lb"]), "hg_norm_w": f(inputs["hg_norm_w"]),
        "rw_mu": f(inputs["rw_mu"][0]), "rw_w0": f(inputs["rw_w0"]), "rw_w1": f(inputs["rw_w1"][0]),
        "rw_w2": f(inputs["rw_w2"][0]), "rw_a0": f(inputs["rw_a0"]), "rw_a1": f(inputs["rw_a1"][0]),
        "rw_a2": f(inputs["rw_a2"][0]), "rw_g1": f(inputs["rw_g1"][0]), "rw_g2": f(inputs["rw_g2"][0]),
        "rw_k_k": f(inputs["rw_k_k"]), "rw_k_a": f(inputs["rw_k_a"]),
        "rw_r_k": f(np.asarray(inputs["rw_r_k"]).reshape(1, D)),
        "rw_w_r": f(inputs["rw_w_r"][0]), "rw_w_k": f(inputs["rw_w_k"][0]), "rw_w_v": f(inputs["rw_w_v"][0]),
        "rw_w_o": f(inputs["rw_w_o"][0]), "rw_ln_w": f(inputs["rw_ln_w"]), "rw_ln_b": f(inputs["rw_ln_b"]),
        "ffn_w_in": f(inputs["ffn_w_in"]), "ffn_conv_w": f(inputs["ffn_conv_w"]), "ffn_conv_b": f(inputs["ffn_conv_b"]),
        "ffn_w_down": f(inputs["ffn_w_down"]),
    }
    return m


TWO_PI = 2.0 * math.pi
C1_2PI = 6.28125
C2_2PI = TWO_PI - C1_2PI
MAGIC = 12582912.0


class S5Mixin:
    def s5_setup(self):
        nc, P = self.nc, self.P
        self.es_s5 = ExitStack()
        a = lambda n, shp, dt: self.sb(self.es_s5, n, shp, dt)
        d = self.din
        K = "s5c"
        self.s5_mag = a("s5_mag", [128, 32], F32)
        self.s5_cth = a("s5_cth", [128, 32], F32)
        self.s5_sth = a("s5_sth", [128, 32], F32)
        self.s5_Bre = a("s5_Bre", [128, 32, 128], BF16)
        self.s5_Bim = a("s5_Bim", [128, 32, 128], BF16)
        self.s5_Cre = a("s5_Cre", [128, 32, 128], BF16)
        self.s5_Cim = a("s5_Cim", [128, 32, 128], BF16)
        self.s5_dD = a("s5_dD", [128, 8, 128], BF16)
        self.s5_ahr = a("s5_ahr", [128, NSEQ, 32], F32)
        self.s5_ahi = a("s5_ahi", [128, NSEQ, 32], F32)
        with ExitStack() as es:
            T = lambda n, shp=[128, 32]: self.sb(es, n, shp, F32)
            lr, li, ls, dt, th, k, r, r2, msk = (T(n) for n in ("s_lr", "s_li", "s_ls", "s_dt", "s_th", "s_k", "s_r", "s_r2", "s_msk"))
            ar, ai, am1, den, zr, zi, t1, t2 = (T(n) for n in ("s_ar", "s_ai", "s_am1", "s_den", "s_zr", "s_zi", "s_t1", "s_t2"))
            rowv = lambda t: t.ap().rearrange("(i gl) p -> i (gl p)", gl=2)
            self.load_cols(lr[:, :], rowv(d["s5_lam_re"]), 32, K)
            self.load_cols(li[:, :], rowv(d["s5_lam_im"]), 32, K)
            st = self.lcst[0]
            P.dma("sp", st[0:1, 0:64], d["s5_log_step"].ap(), (), [("lcst", 0)])
            b = self.bank()
            self.mm(self.ps[b][:, 0:64], self.ones[0:1, :], st[0:1, 0:64], True, True, [("lcst", 0), "ones"], [("ps", b)])
            self.ts(t1[:, :], self.ps[b][:, 0:64:2], self.m01[:, 0:1], ALU.mult, [("ps", b), "m01"], [K])
            self.stt(ls[:, :], self.ps[b][:, 1:64:2], self.m01[:, 1:2], t1[:, :], ALU.mult, ALU.add, [("ps", b), "m01", K], [K])
            R_, W_ = [K], [K]
            self.ts(lr[:, :], lr[:, :], -1e-4, ALU.min, R_, W_)
            self.act(dt[:, :], ls[:, :], AF.Exp, R_, W_)
            self.tt(t1[:, :], lr[:, :], dt[:, :], ALU.mult, R_, W_)
            self.act(self.s5_mag[:, :], t1[:, :], AF.Exp, R_, W_)
            self.tt(th[:, :], li[:, :], dt[:, :], ALU.mult, R_, W_)
            self.ts(k[:, :], th[:, :], 1.0 / TWO_PI, ALU.mult, R_, W_)
            self.ts(k[:, :], k[:, :], MAGIC, ALU.add, R_, W_)
            self.ts(k[:, :], k[:, :], -MAGIC, ALU.add, R_, W_)
            self.stt(r[:, :], k[:, :], -C1_2PI, th[:, :], ALU.mult, ALU.add, R_, W_)
            self.stt(r[:, :], k[:, :], -C2_2PI, r[:, :], ALU.mult, ALU.add, R_, W_)
            self.ts(r[:, :], r[:, :], math.pi, ALU.min, R_, W_, s2=-math.pi, op1=ALU.max)
            self.act(self.s5_sth[:, :], r[:, :], AF.Sin, R_, W_)
            self.ts(r2[:, :], r[:, :], math.pi / 2, ALU.add, R_, W_)
            self.ts(msk[:, :], r2[:, :], math.pi, ALU.is_gt, R_, W_)
            self.stt(r2[:, :], msk[:, :], -TWO_PI, r2[:, :], ALU.mult, ALU.add, R_, W_)
            self.ts(r2[:, :], r2[:, :], math.pi, ALU.min, R_, W_, s2=-math.pi, op1=ALU.max)
            self.act(self.s5_cth[:, :], r2[:, :], AF.Sin, R_, W_)
            self.tt(ar[:, :], self.s5_mag[:, :], self.s5_cth[:, :], ALU.mult, R_, W_)
            self.tt(ai[:, :], self.s5_mag[:, :], self.s5_sth[:, :], ALU.mult, R_, W_)
            self.ts(am1[:, :], ar[:, :], -1.0, ALU.add, R_, W_)
            self.tt(den[:, :], lr[:, :], lr[:, :], ALU.mult, R_, W_)
            self.tt(t1[:, :], li[:, :], li[:, :], ALU.mult, R_, W_)
            self.tt(den[:, :], den[:, :], t1[:, :], ALU.add, R_, W_)
            self.P.op("dve", lambda e: e.reciprocal(out=den[:, :], in_=den[:, :]), R_, W_)
            self.tt(t1[:, :], am1[:, :], lr[:, :], ALU.mult, R_, W_)
            self.tt(t2[:, :], ai[:, :], li[:, :], ALU.mult, R_, W_)
            self.tt(t1[:, :], t1[:, :], t2[:, :], ALU.add, R_, W_)
            self.tt(zr[:, :], t1[:, :], den[:, :], ALU.mult, R_, W_)
            self.tt(t1[:, :], ai[:, :], lr[:, :], ALU.mult, R_, W_)
            self.tt(t2[:, :], am1[:, :], li[:, :], ALU.mult, R_, W_)
            self.tt(t1[:, :], t1[:, :], t2[:, :], ALU.subtract, R_, W_)
            self.tt(zi[:, :], t1[:, :], den[:, :], ALU.mult, R_, W_)
            Bre = self.sb(es, "s_Bre", [128, 32, 16], F32)
            Bim = self.sb(es, "s_Bim", [128, 32, 16], F32)
            bbr = self.sb(es, "s_bbr", [128, 32, 16], F32)
            bbi = self.sb(es, "s_bbi", [128, 32, 16], F32)
            tb = self.sb(es, "s_tb", [128, 32, 16], F32)
            for (dst, nm) in ((Bre, "s5_b_re"), (Bim, "s5_b_im")):
                P.dma("sp", dst[:, :, :], d[nm].ap().rearrange("g p c -> (g p) c").rearrange("(i q) c -> q i c", q=128),
                      (), [K])
            zrb = zr[:, :].unsqueeze(2).to_broadcast([128, 32, 16])
            zib = zi[:, :].unsqueeze(2).to_broadcast([128, 32, 16])
            self.tt(bbr[:, :, :], Bre[:, :, :], zrb, ALU.mult, R_, W_)
            self.tt(tb[:, :, :], Bim[:, :, :], zib, ALU.mult, R_, W_)
            self.tt(bbr[:, :, :], bbr[:, :, :], tb[:, :, :], ALU.subtract, R_, W_)
            self.tt(bbi[:, :, :], Bim[:, :, :], zrb, ALU.mult, R_, W_)
            self.tt(tb[:, :, :], Bre[:, :, :], zib, ALU.mult, R_, W_)
            self.tt(bbi[:, :, :], bbi[:, :, :], tb[:, :, :], ALU.add, R_, W_)
            stg = [self.sb(es, "s_stg%d" % i, [128, 128], F32) for i in range(4)]
            n_st = 0
            for i in range(32):
                gp = i % 4
                for (srcb, dstT) in ((bbr, self.s5_Bre), (bbi, self.s5_Bim)):
                    s = stg[n_st % 4]; sk = ("s_stg", n_st % 4); n_st += 1
                    self.memset(s[:, :], 0.0, [sk])
                    self.cp(s[0:64, 32 * gp:32 * gp + 16], srcb[0:64, i, :], [K, sk], [sk])
                    self.cp(s[64:128, 32 * gp + 16:32 * gp + 32], srcb[64:128, i, :], [K, sk], [sk])
                    b = self.bank()
                    self.tr(self.ps[b][:, 0:128], s[:, :], self.ident[:, :], [sk, "ident"], [("ps", b)])
                    self.cp(dstT[:, i, :], self.ps[b][:, 0:128], [("ps", b)], [K], eng="act")
                for (nm, dstT, neg) in (("s5_c_re", self.s5_Cre, False), ("s5_c_im", self.s5_Cim, True)):
                    s = stg[n_st % 4]; sk = ("s_stg", n_st % 4); n_st += 1
                    self.memset(s[:, :], 0.0, [sk])
                    P.dma("sp", s[32 * gp:32 * gp + 16, 0:64], d[nm].ap()[2 * i], [sk], [sk])
                    P.dma("sp", s[32 * gp + 16:32 * gp + 32, 64:128], d[nm].ap()[2 * i + 1], [sk], [sk])
                    b = self.bank()
                    self.tr(self.ps[b][:, 0:128], s[:, :], self.ident[:, :], [sk, "ident"], [("ps", b)])
                    if neg:
                        self.ts(dstT[:, i, :], self.ps[b][:, 0:128], -1.0, ALU.mult, [("ps", b)], [K])
                    else:
                        self.cp(dstT[:, i, :], self.ps[b][:, 0:128], [("ps", b)], [K], eng="act")
            dcol = self.sb(es, "s_dcol", [128, 8], F32)
            self.load_cols(dcol[:, :], d["s5_d"].ap().rearrange("(m g) c -> m (g c)", g=8), 8, K)
            for m in range(8):
                self.ts(self.s5_dD[:, m, :], self.ident[:, :], dcol[:, m:m + 1], ALU.mult, [K, "ident"], [K])
            h0r = self.sb(es, "s_h0r", [128, NSEQ, 32], F32)
            h0i = self.sb(es, "s_h0i", [128, NSEQ, 32], F32)
            th0 = self.sb(es, "s_th0", [128, NSEQ, 32], F32)
            sv = lambda t: t.ap().rearrange("s (i gl) p -> (s i) (gl p)", gl=2)
            self.load_cols(h0r[:, :, :].rearrange("q s i -> q (s i)"), sv(d["st_s5r"]), NSEQ * 32, K)
            self.load_cols(h0i[:, :, :].rearrange("q s i -> q (s i)"), sv(d["st_s5i"]), NSEQ * 32, K)
            arb = ar[:, :].unsqueeze(1).to_broadcast([128, NSEQ, 32])
            aib = ai[:, :].unsqueeze(1).to_broadcast([128, NSEQ, 32])
            self.tt(self.s5_ahr[:, :, :], h0r[:, :, :], arb, ALU.mult, R_, W_)
            self.tt(th0[:, :, :], h0i[:, :, :], aib, ALU.mult, R_, W_)
            self.tt(self.s5_ahr[:, :, :], self.s5_ahr[:, :, :], th0[:, :, :], ALU.subtract, R_, W_)
            self.tt(self.s5_ahi[:, :, :], h0i[:, :, :], arb, ALU.mult, R_, W_)
            self.tt(th0[:, :, :], h0r[:, :, :], aib, ALU.mult, R_, W_)
            self.tt(self.s5_ahi[:, :, :], self.s5_ahi[:, :, :], th0[:, :, :], ALU.add, R_, W_)
            P.fence()

    def s5_main(self, yT):
        P = self.P
        K = "s5c"
        with ExitStack() as es:
            F = lambda n, shp: self.sb(es, n, shp, F32)
            cr = F("m_cr", [128, TP]); si = F("m_si", [128, TP])
            d0 = F("m_d0", [128, NT])
            wr = F("m_wr", [128, NT]); wi = F("m_wi", [128, NT])
            t1 = F("m_t1", [128, NT]); t2 = F("m_t2", [128, NT])
            ub = self.sb(es, "m_ub", [128, NT], BF16)
            xre = self.sb(es, "m_xre", [128, NT], BF16)
            xim = self.sb(es, "m_xim", [128, NT], BF16)
            pfr = F("m_pfr", [128, 32]); pfi = F("m_pfi", [128, 32])
            sfr = F("m_sfr", [128, NSEQ, 32]); sfi = F("m_sfi", [128, NSEQ, 32])
            ft = F("m_ft", [128, NSEQ])
            s3 = lambda ap: ap.rearrange("p (s t) -> p s t", t=8)
            for m in range(8):
                P.dma("pool", ub[:, :], self.zT.ap()[m], (), ["ub"])
                self.bank_pool = [5, 6, 7]
                for gp in range(4):
                    i = 4 * m + gp
                    self.memset(cr[:, 0:1], 1.0, ["cr"])
                    self.memset(si[:, 0:1], 0.0, ["si"])
                    self.cp(cr[:, 1:2], self.s5_cth[:, i:i + 1], [K], ["cr"])
                    self.cp(si[:, 1:2], self.s5_sth[:, i:i + 1], [K], ["si"])
                    L = 1
                    while L + 1 < TP:
                        n = min(L, TP - 1 - L)
                        cL, sL = cr[:, L:L + 1], si[:, L:L + 1]
                        self.ts(t1[:, 0:n], si[:, 1:1 + n], sL, ALU.mult, ["si"], ["t1"])
                        self.ts(t2[:, 0:n], cr[:, 1:1 + n], sL, ALU.mult, ["cr", "si"], ["t2"])
                        self.stt(cr[:, L + 1:L + 1 + n], cr[:, 1:1 + n], cL, t1[:, 0:n], ALU.mult, ALU.subtract, ["cr", "t1"], ["cr"])
                        self.stt(si[:, L + 1:L + 1 + n], si[:, 1:1 + n], cL, t2[:, 0:n], ALU.mult, ALU.add, ["si", "cr", "t2"], ["si"])
                        L += n
                    self.cp(d0[:, :], self.s5_mag[:, i:i + 1].to_broadcast([128, NT]), [K], ["d0"])
                    self.memset(d0[:, 0:1], 0.0, ["d0"])
                    self.memset(d0[:, TP:NT:8], 0.0, ["d0"])
                    crS = cr[:, 0:8].unsqueeze(1).to_broadcast([128, NSEQ, 8])
                    siS = si[:, 0:8].unsqueeze(1).to_broadcast([128, NSEQ, 8])
                    for sidx, (t0, n) in enumerate(SEGS):
                        br, bi = self.bank(), self.bank()
                        self.mm(self.ps[br][:, 0:n], self.s5_Bre[:, i, :], ub[:, t0:t0 + n], True, True, [K, "ub"], [("ps", br)])
                        self.mm(self.ps[bi][:, 0:n], self.s5_Bim[:, i, :], ub[:, t0:t0 + n], True, True, [K, "ub"], [("ps", bi)])
                        parts = [(0, min(n, TP - t0), False)]
                        if t0 + n > TP:
                            parts.append((TP - t0, n - (TP - t0), True))
                        for (o, ln, samp) in parts:
                            if samp:
                                pr, pi_ = s3(self.ps[br][:, o:o + ln]), s3(self.ps[bi][:, o:o + ln])
                                c_, s_ = crS, siS
                                v = lambda t: s3(t[:, t0 + o:t0 + o + ln])
                            else:
                                pr, pi_ = self.ps[br][:, o:o + ln], self.ps[bi][:, o:o + ln]
                                c_, s_ = cr[:, t0 + o:t0 + o + ln], si[:, t0 + o:t0 + o + ln]
                                v = lambda t: t[:, t0 + o:t0 + o + ln]
                            self.tt(v(t1), pr, c_, ALU.mult, [("ps", br), "cr"], ["t1"])
                            self.tt(v(t2), pi_, s_, ALU.mult, [("ps", bi), "si"], ["t2"])
                            self.tt(v(wr), v(t1), v(t2), ALU.add, ["t1", "t2"], ["wr"])
                            self.tt(v(t1), pi_, c_, ALU.mult, [("ps", bi), "cr"], ["t1"])
                            self.tt(v(t2), pr, s_, ALU.mult, [("ps", br), "si"], ["t2"])
                            self.tt(v(wi), v(t1), v(t2), ALU.subtract, ["t1", "t2"], ["wi"])
                    self.tt(wr[:, TP:NT:8], wr[:, TP:NT:8], self.s5_ahr[:, :, i], ALU.add, ["wr", K], ["wr"])
                    self.tt(wi[:, TP:NT:8], wi[:, TP:NT:8], self.s5_ahi[:, :, i], ALU.add, ["wi", K], ["wi"])
                    self.scan(wr[:, :], d0[:, :], wr[:, :], 0.0, ["d0", "wr"], ["wr"])
                    self.scan(wi[:, :], d0[:, :], wi[:, :], 0.0, ["d0", "wi"], ["wi"])
                    for samp in (False, True):
                        if samp:
                            v = lambda t: s3(t[:, TP:NT])
                            c_, s_ = crS, siS
                            vo = lambda t: s3(t[:, TP:NT])
                        else:
                            v = lambda t: t[:, 0:TP]
                            c_, s_ = cr[:, :], si[:, :]
                            vo = lambda t: t[:, 0:TP]
                        self.tt(v(t1), v(wr), c_, ALU.mult, ["wr", "cr"], ["t1"])
                        self.tt(v(t2), v(wi), s_, ALU.mult, ["wi", "si"], ["t2"])
                        self.tt(vo(xre), v(t1), v(t2), ALU.subtract, ["t1", "t2"], ["xre"])
                        self.tt(v(t1), v(wr), s_, ALU.mult, ["wr", "si"], ["t1"])
                        self.tt(v(t2), v(wi), c_, ALU.mult, ["wi", "cr"], ["t2"])
                        self.tt(vo(xim), v(t1), v(t2), ALU.add, ["t1", "t2"], ["xim"])
                    cP, sP = cr[:, TP - 1:TP], si[:, TP - 1:TP]
                    self.ts(ft[:, 0:1], wi[:, TP - 1:TP], sP, ALU.mult, ["wi", "si"], ["ft"])
                    self.stt(pfr[:, i:i + 1], wr[:, TP - 1:TP], cP, ft[:, 0:1], ALU.mult, ALU.subtract, ["wr", "cr", "ft"], ["pf"])
                    self.ts(ft[:, 0:1], wr[:, TP - 1:TP], sP, ALU.mult, ["wr", "si"], ["ft"])
                    self.stt(pfi[:, i:i + 1], wi[:, TP - 1:TP], cP, ft[:, 0:1], ALU.mult, ALU.add, ["wi", "cr", "ft"], ["pf"])
                    c7, s7 = cr[:, 7:8], si[:, 7:8]
                    self.ts(ft[:, :], wi[:, TP + 7:NT:8], s7, ALU.mult, ["wi", "si"], ["ft"])
                    self.stt(sfr[:, :, i], wr[:, TP + 7:NT:8], c7, ft[:, :], ALU.mult, ALU.subtract, ["wr", "cr", "ft"], ["sf"])
                    self.ts(ft[:, :], wr[:, TP + 7:NT:8], s7, ALU.mult, ["wr", "si"], ["ft"])
                    self.stt(sfi[:, :, i], wi[:, TP + 7:NT:8], c7, ft[:, :], ALU.mult, ALU.add, ["wi", "cr", "ft"], ["sf"])
                    for sidx, (t0, n) in enumerate(SEGS):
                        b = sidx
                        self.mm(self.ps[b][:, 0:n], self.s5_Cre[:, i, :], xre[:, t0:t0 + n], gp == 0, False, [K, "xre"], [("ps", b)])
                        self.mm(self.ps[b][:, 0:n], self.s5_Cim[:, i, :], xim[:, t0:t0 + n], False, False, [K, "xim"], [("ps", b)])
                for sidx, (t0, n) in enumerate(SEGS):
                    b = sidx
                    self.mm(self.ps[b][:, 0:n], self.s5_dD[:, m, :], ub[:, t0:t0 + n], False, True, [K, "ub"], [("ps", b)])
                    self.act(yT[:, 8 + m, t0:t0 + n], self.ps[b][:, 0:n], AF.Gelu_apprx_tanh, [("ps", b)], [("ya", m, sidx)])
                self.bank_pool = None
            stq = [self.sb(es, "m_stq%d" % j, [128, 128], F32) for j in range(2)]
            nq = 0
            for (srcT, dst) in ((pfr, "p_s5r"), (pfi, "p_s5i")):
                b = self.bank()
                s = stq[nq % 2]; sk = ("stq", nq % 2); nq += 1
                self.tr(self.ps[b][0:32, 0:128], srcT[:, :], self.ident[:, :], ["pf", "ident"], [("ps", b)])
                self.cp(s[0:32, :], self.ps[b][0:32, 0:128], [("ps", b)], [sk])
                P.dma("sp", self.dout[dst].ap().rearrange("(i gl) p -> i (gl p)", gl=2), s[0:32, :], [sk], [dst], is_output=True)
            for (srcT, dst) in ((sfr, "s_s5r"), (sfi, "s_s5i")):
                flat = srcT[:, :, :].rearrange("q s i -> q (s i)")
                dv = self.dout[dst].ap().rearrange("s (i gl) p -> (s i) (gl p)", gl=2)
                for c in range(4):
                    b = self.bank()
                    s = stq[nq % 2]; sk = ("stq", nq % 2); nq += 1
                    self.tr(self.ps[b][:, 0:128], flat[:, c * 128:(c + 1) * 128], self.ident[:, :], ["sf", "ident"], [("ps", b)])
                    self.cp(s[:, :], self.ps[b][:, 0:128], [("ps", b)], [sk])
                    P.dma("sp", dv[c * 128:(c + 1) * 128, :], s[:, :], [sk], [dst], is_output=True)
            P.fence()

    def s5_glu(self, yT):
        with ExitStack() as es:
            sg = [self.sb(es, "g_sg%d" % i, [128, 512], F32) for i in range(2)]
            cnt = [0]

            def evac(j, si, t0, n, ps, pk):
                s = sg[cnt[0] % 2]; sk = ("g_sg", cnt[0] % 2); cnt[0] += 1
                self.act(s[:, 0:n], ps, AF.Sigmoid, [pk], [sk])
                self.tt(yT[:, j, t0:t0 + n], s[:, 0:n], yT[:, 8 + j, t0:t0 + n], ALU.mult, [sk, ("ya", j, si)], [("yT", j, si)])

            self.proj(es, lambda kt, t0, n: yT[:, 8 + kt, t0:t0 + n], lambda kt, si: ("ya", kt, si), 8,
                      self.din["s5_w_glu"].ap(), 0, 1024, evac, tag="wg")
            self.P.fence()


HG_CHUNKS = [(64 * n, 64, None) for n in range(32)] + [(2048, 16, None)] + [(TP + 8 * q, 8, q) for q in range(NSEQ)]


class HgMixin:
    def hg_main(self, yT):
        P = self.P
        d = self.din
        K = "hgc"
        with ExitStack() as es:
            F = lambda n, shp=[128, NT]: self.sb(es, n, shp, F32)
            lbc = F("h_lbc", [128, 16]); lb = F("h_lb", [128, 8]); oml = F("h_oml", [128, 8]); nw = F("h_nw", [128, 1])
            cmask = F("h_cmask")
            self.load_cols(lbc[:, :], d["hg_lb"].ap().rearrange("l (c p) -> (l c) p", p=128), 16, K)
            self.load_cols(nw[:, :], d["hg_norm_w"].ap(), 1, K)
            self.tt(lb[:, :], lbc[:, 0:8], lbc[:, 8:16], ALU.subtract, [K], [K])
            self.act(lb[:, :], lb[:, :], AF.Sigmoid, [K], [K])
            self.ts(oml[:, :], lb[:, :], -1.0, ALU.mult, [K], [K], s2=1.0, op1=ALU.add)
            self.memset(cmask[:, :], 1.0, [K])
            self.memset(cmask[:, 0:2048:64], 0.0, [K])
            self.memset(cmask[:, 2048:2049], 0.0, [K])
            self.memset(cmask[:, TP:NT:8], 0.0, [K])
            qT, fg, iT, gT, kk, bc, btf, ex, kend, oall = (F(n) for n in ("h_q", "h_f", "h_i", "h_g", "h_kk", "h_bc", "h_btf", "h_ex", "h_kend", "h_o"))
            qin = self.sb(es, "h_qin", [128, NT], BF16)
            kin = self.sb(es, "h_kin", [128, NT], BF16)
            dec = F("h_dec", [128, 49])
            S = [F("h_S%d" % i, [128, 128]) for i in range(2)]
            Sb = [self.sb(es, "h_Sb%d" % i, [128, 128], BF16) for i in range(2)]
            SS = [F("h_SS%d" % i, [128, 128]) for i in range(2)]
            SSb = [self.sb(es, "h_SSb%d" % i, [128, 128], BF16) for i in range(2)]
            attm = [self.sb(es, "h_att%d" % i, [64, 64], BF16) for i in range(2)]
            vt = [self.sb(es, "h_vt%d" % i, [64, 128], BF16) for i in range(2)]
            ket = [self.sb(es, "h_ket%d" % i, [64, 128], BF16) for i in range(2)]
            sq = [F("h_sq%d" % i, [128, 512]) for i in range(2)]
            rs = [F("h_rs%d" % i, [128, 512]) for i in range(2)]
            eps = F("h_eps", [128, 1])
            self.memset(eps[:, :], EPS, [K])
            s_i = 0
            for hh in range(8):
                for (t, j) in ((qT, 8 + hh), (fg, 16 + hh), (iT, 24 + hh), (gT, 32 + hh)):
                    P.dma("sp", t[:, :], self.zT.ap()[j], (), [t.name])
                R = lambda *ts: [t.name if hasattr(t, "name") else t for t in ts]
                self.act(fg[:, :], fg[:, :], AF.Sigmoid, R(fg), R(fg))
                self.ts(fg[:, :], fg[:, :], oml[:, hh:hh + 1], ALU.mult, R(fg, K), R(fg), s2=lb[:, hh:hh + 1], op1=ALU.add)
                self.ts(kk[:, :], fg[:, :], -1.0, ALU.mult, R(fg), R(kk), s2=1.0, op1=ALU.add)
                self.act(bc[:, :], fg[:, :], AF.Ln, R(fg), R(bc))
                self.scan(bc[:, :], cmask[:, :], bc[:, :], 0.0, R(bc, K), R(bc))
                v3 = lambda ap, c: ap.rearrange("p (n c) -> p n c", c=c)
                self.cp(v3(btf[:, 0:2048], 64), v3(bc[:, 0:2048], 64)[:, :, 63:64].to_broadcast([128, 32, 64]), R(bc), R(btf))
                self.cp(btf[:, 2048:TP], bc[:, TP - 1:TP].to_broadcast([128, 16]), R(bc), R(btf))
                self.cp(v3(btf[:, TP:NT], 8), v3(bc[:, TP:NT], 8)[:, :, 7:8].to_broadcast([128, NSEQ, 8]), R(bc), R(btf))
                self.act(dec[:, 0:32], bc[:, 63:2048:64], AF.Exp, R(bc), R(dec))
                self.act(dec[:, 32:33], bc[:, TP - 1:TP], AF.Exp, R(bc), R(dec))
                self.act(dec[:, 33:49], bc[:, TP + 7:NT:8], AF.Exp, R(bc), R(dec))
                self.act(qT[:, :], qT[:, :], AF.Silu, R(qT), R(qT))
                self.act(ex[:, :], bc[:, :], AF.Exp, R(bc), R(ex))
                self.tt(qin[:, :], qT[:, :], ex[:, :], ALU.mult, R(qT, ex), R(qin))
                self.act(ex[:, :], bc[:, :], AF.Exp, R(bc, qin), R(ex), scale=-1.0)
                self.tt(kin[:, :], kk[:, :], ex[:, :], ALU.mult, R(kk, ex), R(kin))
                self.tt(btf[:, :], btf[:, :], bc[:, :], ALU.subtract, R(btf, bc), R(btf))
                self.act(btf[:, :], btf[:, :], AF.Exp, R(btf), R(btf))
                self.tt(kend[:, :], kk[:, :], btf[:, :], ALU.mult, R(kk, btf), R(kend))
                curP = 0
                self.memset(S[0][:, :], 0.0, [("S", 0)])
                self.memset(Sb[0][:, :], 0.0, [("Sb", 0)])
                pr_ = [n for n, c_ in enumerate(HG_CHUNKS) if c_[2] is None]
                sa_ = [n for n, c_ in enumerate(HG_CHUNKS) if c_[2] is not None]
                order = []
                while pr_ or sa_:
                    order += pr_[:2]; pr_ = pr_[2:]
                    order += sa_[:1]; sa_ = sa_[1:]
                for it_, n in enumerate(order):
                    (t0, C, q) = HG_CHUNKS[n]
                    if q is None:
                        Sc, Sbc, kS, kSb = S[curP], Sb[curP], ("S", curP), ("Sb", curP)
                    else:
                        k_ = q % 2
                        Sc, Sbc, kS, kSb = SS[k_], SSb[k_], ("SS", k_), ("SSb", k_)
                        P.dma("sp", Sc[:, :], d["st_hg"].ap()[q, hh], (), [kS])
                        self.cp(Sbc[:, :], Sc[:, :], [kS], [kSb], eng="act")
                    a_i = it_ % 2
                    ba, bv, bk, bo, bs = (self.bank() for _ in range(5))
                    self.mm(self.ps[ba][0:C, 0:C], kin[:, t0:t0 + C], qin[:, t0:t0 + C], True, True, R(kin, qin), [("ps", ba)])
                    self.tt(attm[a_i][0:C, 0:C], self.ps[ba][0:C, 0:C], self.tril[0:C, 0:C], ALU.mult, [("ps", ba), "tril"], [("att", a_i)])
                    self.tr(self.ps[bv][0:C, 0:128], iT[:, t0:t0 + C], self.ident[:, :], R(iT, "ident"), [("ps", bv)])
                    self.cp(vt[a_i][0:C, :], self.ps[bv][0:C, 0:128], [("ps", bv)], [("vt", a_i)], eng="act")
                    self.tr(self.ps[bk][0:C, 0:128], kend[:, t0:t0 + C], self.ident[:, :], R(kend, "ident"), [("ps", bk)])
                    self.cp(ket[a_i][0:C, :], self.ps[bk][0:C, 0:128], [("ps", bk)], [("ket", a_i)], eng="act")
                    self.mm(self.ps[bo][:, 0:C], vt[a_i][0:C, :], attm[a_i][0:C, 0:C], True, False, [("vt", a_i), ("att", a_i)], [("ps", bo)])
                    self.mm(self.ps[bo][:, 0:C], Sbc[:, :], qin[:, t0:t0 + C], False, True, [kSb, "h_qin"], [("ps", bo)])
                    self.cp(oall[:, t0:t0 + C], self.ps[bo][:, 0:C], [("ps", bo)], R(oall), eng="act")
                    self.mm(self.ps[bs][:, 0:128], ket[a_i][0:C, :], vt[a_i][0:C, :], True, True, [("ket", a_i), ("vt", a_i)], [("ps", bs)])
                    if q is None:
                        nxt = 1 - curP
                        self.stt(S[nxt][:, :], Sc[:, :], dec[:, n:n + 1], self.ps[bs][:, 0:128], ALU.mult, ALU.add,
                                 [kS, ("ps", bs)] + R(dec), [("S", nxt)])
                        self.cp(Sb[nxt][:, :], S[nxt][:, :], [("S", nxt)], [("Sb", nxt)], eng="act")
                        curP = nxt
                        if n == 32:
                            P.dma("sp", self.dout["p_hg"].ap()[hh], S[curP][:, :], [("S", curP)], ["p_hg"], is_output=True)
                    else:
                        self.stt(Sc[:, :], Sc[:, :], dec[:, n:n + 1], self.ps[bs][:, 0:128], ALU.mult, ALU.add,
                                 [kS, ("ps", bs)] + R(dec), [kS])
                        P.dma("sp", self.dout["s_hg"].ap()[q, hh], Sc[:, :], [kS], ["s_hg"], is_output=True)
                self.act(gT[:, :], gT[:, :], AF.Silu, R(gT), R(gT))
                for si, (t0, n) in enumerate(SEGS):
                    q_ = sq[si % 2]; r_ = rs[si % 2]
                    b = self.bank()
                    self.act(q_[:, 0:n], oall[:, t0:t0 + n], AF.Square, R(oall), [("hsq", si % 2)])
                    self.mm(self.ps[b][:, 0:n], self.ones[:, :], q_[:, 0:n], True, True, [("hsq", si % 2), "ones"], [("ps", b)])
                    self.act(r_[:, 0:n], self.ps[b][:, 0:n], AF.Sqrt, [("ps", b), K], [("hrs", si % 2)], bias=eps[:, 0:1], scale=1.0 / 128)
                    self.P.op("dve", lambda e, r_=r_, n=n: e.reciprocal(out=r_[:, 0:n], in_=r_[:, 0:n]), [("hrs", si % 2)], [("hrs", si % 2)])
                    self.stt(r_[:, 0:n], oall[:, t0:t0 + n], nw[:, 0:1], r_[:, 0:n], ALU.mult, ALU.mult, R(oall, K) + [("hrs", si % 2)], [("hrs", si % 2)])
                    self.tt(yT[:, 8 + hh, t0:t0 + n], r_[:, 0:n], gT[:, t0:t0 + n], ALU.mult, [("hrs", si % 2)] + R(gT), [("yT", 8 + hh, si)])
            P.fence()


class FfnMixin:
    def out_proj_res(self, xT, xkey, KT, w2d, tag="wo"):
        P = self.P
        with ExitStack() as es:
            hcol = [self.sb(es, "hcol%d" % i, [128, NT], F32) for i in range(3)]

            def evac(j, si, t0, n, ps, pk):
                h = hcol[j % 3]; hk = ("hcol", j % 3)
                if si == 0:
                    P.dma("sp", h[:, :], self.hT.ap()[j], [("hT", j)], [hk])
                self.tt(h[:, t0:t0 + n], h[:, t0:t0 + n], ps, ALU.add, [hk, pk], [hk])
                if si == len(SEGS) - 1:
                    P.dma("sp", self.hT.ap()[j], h[:, :], [hk], [("hT", j)])

            self.proj(es, xT, xkey, KT, w2d, 0, D, evac, tag=tag)
            P.fence()

    def ffn(self, l, xnT):
        P = self.P
        d = self.din
        K = "ffc"
        v3 = lambda ap: ap.rearrange("p (s t) -> p s t", t=8)
        gS = self.gS
        with ExitStack() as es:
            F = lambda n, shp: self.sb(es, n, shp, F32)
            cw = F("f_cw", [128, 3 * FC]); cb = F("f_cb", [128, FC])
            cvin = F("f_cvin", [128, FC, 32]); cvP = F("f_cvP", [128, FC, 2]); cvS = F("f_cvS", [128, FC, 32])
            self.load_cols(cw[:, :], d["ffn_conv_w"].ap()[l].rearrange("r (c p) -> (r c) p", p=128), 3 * FC, K)
            self.load_cols(cb[:, :], d["ffn_conv_b"].ap()[l:l + 1, :].rearrange("o (c p) -> (o c) p", p=128), FC, K)
            with ExitStack() as es2:
                rows = self.sb(es2, "f_rows", [32, D_FF], F32)
                P.dma("sp", rows[:, :], d["st_cv"].ap()[l].rearrange("s r f -> (s r) f"), (), ["f_rows"])
                for c in range(FC):
                    b = self.bank()
                    self.tr(self.ps[b][:, 0:32], rows[0:32, c * 128:(c + 1) * 128], self.ident[0:32, 0:32], ["f_rows", "ident"], [("ps", b)])
                    self.cp(cvin[:, c, :], self.ps[b][:, 0:32], [("ps", b)], [K], eng=("act" if c % 2 else "dve"))
                P.fence()
            wf = [self.sb(es, "f_wf%d" % i, [128, DC, 512], BF16) for i in range(2)]
            aP = [F("f_aP%d" % i, [128, TP + 2]) for i in range(2)]
            aS = [F("f_aS%d" % i, [128, NSEQ, 10]) for i in range(2)]
            cc = [F("f_cc%d" % i, [128, NT]) for i in range(2)]
            go = [self.sb(es, "f_go%d" % i, [128, NT], BF16) for i in range(2)]
            for i in range(2):
                self.memset(aP[i][:, 0:2], 0.0, [("aP", i)])
            w_in = d["ffn_w_in"].ap()[l]
            for g0 in range(0, FC, 2):
                w = wf[(g0 // 2) % 2]; wk = ("f_wf", (g0 // 2) % 2)
                P.dma("pool", w[:, :, 0:256], w_in[:, g0 * 128:(g0 + 2) * 128].rearrange("(kt p) n -> p kt n", p=128), (), [wk])
                P.dma("pool", w[:, :, 256:512], w_in[:, D_FF + g0 * 128:D_FF + (g0 + 2) * 128].rearrange("(kt p) n -> p kt n", p=128), (), [wk])
                for jj in range(2):
                    j = g0 + jj
                    ap_, as_, c_, g_ = aP[j % 2], aS[j % 2], cc[j % 2], go[j % 2]
                    ak, ask, ck, gk = ("aP", j % 2), ("aS", j % 2), ("cc", j % 2), ("go", j % 2)
                    self.cp(as_[:, :, 0:2], cvin[:, j, :].rearrange("p (s r) -> p s r", r=2), [K], [ask])
                    for si, (t0, n) in enumerate(SEGS):
                        b = self.bank()
                        for kt in range(DC):
                            self.mm(self.ps[b][:, 0:n], w[:, kt, jj * 128:(jj + 1) * 128], xnT[:, kt, t0:t0 + n], kt == 0, kt == DC - 1,
                                    [wk, ("xnT", kt, si)], [("ps", b)])
                        npr = min(n, TP - t0)
                        self.cp(ap_[:, 2 + t0:2 + t0 + npr], self.ps[b][:, 0:npr], [("ps", b)], [ak], eng="act")
                        if npr < n:
                            self.cp(as_[:, :, 2:10], v3(self.ps[b][:, npr:n]), [("ps", b)], [ask], eng="act")
                    w0, w1, w2 = (cw[:, r * FC + j:r * FC + j + 1] for r in range(3))
                    self.ts(c_[:, 0:TP], ap_[:, 2:2 + TP], w2, ALU.mult, [ak, K], [ck], s2=cb[:, j:j + 1], op1=ALU.add)
                    self.stt(c_[:, 0:TP], ap_[:, 1:1 + TP], w1, c_[:, 0:TP], ALU.mult, ALU.add, [ak, K, ck], [ck])
                    self.stt(c_[:, 0:TP], ap_[:, 0:TP], w0, c_[:, 0:TP], ALU.mult, ALU.add, [ak, K, ck], [ck])
                    cs = v3(c_[:, TP:NT])
                    self.ts(cs, as_[:, :, 2:10], w2, ALU.mult, [ask, K], [ck], s2=cb[:, j:j + 1], op1=ALU.add)
                    self.stt(cs, as_[:, :, 1:9], w1, cs, ALU.mult, ALU.add, [ask, K, ck], [ck])
                    self.stt(cs, as_[:, :, 0:8], w0, cs, ALU.mult, ALU.add, [ask, K, ck], [ck])
                    self.cp(cvP[:, j, :], ap_[:, TP:TP + 2], [ak], [K])
                    self.cp(cvS[:, j, :].rearrange("p (s r) -> p s r", r=2), as_[:, :, 8:10], [ask], [K])
                    self.act(c_[:, :], c_[:, :], AF.Gelu_apprx_tanh, [ck], [ck])
                    for si, (t0, n) in enumerate(SEGS):
                        b = self.bank()
                        for kt in range(DC):
                            self.mm(self.ps[b][:, 0:n], w[:, kt, 256 + jj * 128:256 + (jj + 1) * 128], xnT[:, kt, t0:t0 + n], kt == 0, kt == DC - 1,
                                    [wk, ("xnT", kt, si)], [("ps", b)])
                        self.tt(g_[:, t0:t0 + n], c_[:, t0:t0 + n], self.ps[b][:, 0:n], ALU.mult, [ck, ("ps", b)], [gk])
                    for si, (t0, n) in enumerate(SEGS):
                        P.dma("sp", gS.ap()[si, :, j, 0:n], g_[:, t0:t0 + n], [gk], [("gS", j)])
            with ExitStack() as es2:
                rows = self.sb(es2, "f_rowo", [32, D_FF], F32)
                rowp = self.sb(es2, "f_rowp", [2, D_FF], F32)
                for c in range(FC):
                    b = self.bank()
                    self.tr(self.ps[b][0:32, 0:128], cvS[:, c, :], self.ident[:, :], [K, "ident"], [("ps", b)])
                    self.cp(rows[0:32, c * 128:(c + 1) * 128], self.ps[b][0:32, 0:128], [("ps", b)], ["f_rowo"], eng=("act" if c % 2 else "dve"))
                    b = self.bank()
                    self.tr(self.ps[b][0:2, 0:128], cvP[:, c, :], self.ident[:, :], [K, "ident"], [("ps", b)])
                    self.cp(rowp[0:2, c * 128:(c + 1) * 128], self.ps[b][0:2, 0:128], [("ps", b)], ["f_rowp"], eng=("dve" if c % 2 else "act"))
                P.dma("sp", self.dout["s_cv"].ap()[l].rearrange("s r f -> (s r) f"), rows[:, :], ["f_rowo"], ["s_cv"], is_output=True)
                P.dma("sp", self.dout["p_cv"].ap()[l], rowp[:, :], ["f_rowp"], ["p_cv"], is_output=True)
                P.fence()
            P.fence()

    def ffn_down(self, l):
        P = self.P
        gS = self.gS
        w_dn = self.din["ffn_w_down"].ap()[l]
        with ExitStack() as es:
            wd = [self.sb(es, "d_wd%d" % i, [128, FC, 256], BF16) for i in range(2)]
            gs = [self.sb(es, "d_gs%d" % i, [128, FC, 512], BF16) for i in range(2)]
            hc = [self.sb(es, "d_hc%d" % i, [128, 2, NT], F32) for i in range(2)]
            gi = 0
            for g0 in range(0, DC, 2):
                w = wd[(g0 // 2) % 2]; wk = ("d_wd", (g0 // 2) % 2)
                h = hc[(g0 // 2) % 2]; hk = ("d_hc", (g0 // 2) % 2)
                P.dma("pool", w[:, :, :], w_dn[:, g0 * 128:(g0 + 2) * 128].rearrange("(kt p) n -> p kt n", p=128), (), [wk])
                P.dma("sp", h[:, :, :], self.hT.ap()[g0:g0 + 2].rearrange("c p t -> p c t"), [("hT", g0), ("hT", g0 + 1)], [hk])
                for si, (t0, n) in enumerate(SEGS):
                    g = gs[gi % 2]; gk = ("d_gs", gi % 2); gi += 1
                    P.dma("act", g[:, :, 0:n], gS.ap()[si, :, :, 0:n], [("gS", j) for j in range(FC)], [gk])
                    for ii in range(2):
                        b = self.bank()
                        for kt in range(FC):
                            self.mm(self.ps[b][:, 0:n], w[:, kt, ii * 128:(ii + 1) * 128], g[:, kt, 0:n], kt == 0, kt == FC - 1, [wk, gk], [("ps", b)])
                        self.tt(h[:, ii, t0:t0 + n], h[:, ii, t0:t0 + n], self.ps[b][:, 0:n], ALU.add, [hk, ("ps", b)], [hk])
                P.dma("sp", self.hT.ap()[g0:g0 + 2].rearrange("c p t -> p c t"), h[:, :, :], [hk], [("hT", g0), ("hT", g0 + 1)])
            P.fence()


class RwMixin:
    def rw_declare(self):
        sc = self.scratch
        fm = lambda n, dt=F32: sc(n, [DC, 128, NT], dt)
        self.rS, self.kS, self.vS, self.aS = fm("rS"), fm("kS"), fm("vS"), fm("aS")
        self.dS, self.g2S, self.bonS = fm("dS"), fm("g2S"), fm("bonS")
        self.rbS, self.avS = fm("rbS", BF16), fm("avS", BF16)
        self.btok = sc("btok", [NT, D], BF16)
        self.ktok = sc("ktok", [NT, D], BF16)
        self.vtok = sc("vtok", [NT, D], BF16)
        self.ytok = sc("ytok", [NT, D], BF16)

    def rw_shift_out(self, xn32):
        P = self.P
        if True:
            with ExitStack() as es2:
                tokA = self.sb(es2, "r_tokA", [16, D], F32)
                tokB = self.sb(es2, "r_tokB", [128, D], F32)
                for c in range(DC):
                    b = self.bank()
                    self.tr(self.ps[b][0:16, 0:128], xn32[:, c, 0:16], self.ident[:, :], ["out32", "ident"], [("ps", b)])
                    self.cp(tokA[0:16, c * 128:(c + 1) * 128], self.ps[b][0:16, 0:128], [("ps", b)], ["tokA"], eng="act")
                    b = self.bank()
                    self.tr(self.ps[b][:, 0:128], xn32[:, c, 16:144], self.ident[:, :], ["out32", "ident"], [("ps", b)])
                    self.cp(tokB[:, c * 128:(c + 1) * 128], self.ps[b][:, 0:128], [("ps", b)], ["tokB"])
                P.dma("sp", self.dout["p_sh"].ap(), tokA[15:16, :], ["tokA"], ["p_sh"], is_output=True)
                for q in range(NSEQ):
                    P.dma("sp", self.dout["s_sh"].ap()[q:q + 1, :], tokB[8 * q + 7:8 * q + 8, :], ["tokB"], ["s_sh"], is_output=True)
                P.fence()

    def rw_proj(self, xnT):
        P = self.P
        d = self.din
        K = "rwc"
        v3 = lambda ap: ap.rearrange("p (s t) -> p s t", t=8)
        with ExitStack() as es:
            F = lambda n, shp: self.sb(es, n, shp, F32)
            mu = F("r_mu", [128, 96]); omm = F("r_omm", [128, 96])
            self.load_cols(mu[:, :], d["rw_mu"].ap().rearrange("j (c p) -> (j c) p", p=128), 96, K)
            self.ts(omm[:, :], mu[:, :], -1.0, ALU.mult, [K], [K], s2=1.0, op1=ALU.add)
            w0c = F("r_w0", [128, DC]); a0c = F("r_a0", [128, DC])
            self.load_cols(w0c[:, :], d["rw_w0"].ap().rearrange("o (c p) -> (o c) p", p=128), DC, K)
            self.load_cols(a0c[:, :], d["rw_a0"].ap().rearrange("o (c p) -> (o c) p", p=128), DC, K)
            shin = F("r_shin", [128, DC, NSEQ])
            for c in range(DC):
                self.load_cols(shin[:, c, :], d["st_sh"].ap()[:, c * 128:(c + 1) * 128], NSEQ, K)
            xj = self.sb(es, "r_xj", [128, DC, NT], BF16)
            t1s = [self.sb(es, "r_t1%d" % i, [128, NT], BF16) for i in range(2)]
            zrow = [F("r_zrow%d" % i, [128, NT]) for i in range(2)]

            def build_mix(j):
                for c in range(DC):
                    m_, o_ = mu[:, j * 16 + c:j * 16 + c + 1], omm[:, j * 16 + c:j * 16 + c + 1]
                    xk = [("xnT", c, si) for si in range(5)]
                    wk = [("xj", c, si) for si in range(5)]
                    t1 = t1s[c % 2]; tk1 = ("r_t1", c % 2)
                    self.act(t1[:, :], xnT[:, c, :], AF.Copy, xk + [K], [tk1], scale=o_)
                    self.stt(xj[:, c, 1:TP], xnT[:, c, 0:TP - 1], m_, t1[:, 1:TP], ALU.mult, ALU.add, xk + [tk1, K], wk)
                    self.cp(xj[:, c, 0:1], t1[:, 0:1], [tk1], wk)
                    self.stt(v3(xj[:, c, TP:NT])[:, :, 1:8], v3(xnT[:, c, TP:NT])[:, :, 0:7], m_, v3(t1[:, TP:NT])[:, :, 1:8],
                             ALU.mult, ALU.add, xk + [tk1, K], wk)
                    self.stt(xj[:, c, TP:NT:8], shin[:, c, :], m_, t1[:, TP:NT:8], ALU.mult, ALU.add, [tk1, K], wk)

            xkey = lambda kt, si: ("xj", kt, si)

            def to_scratch(dst, func=None, bias=None, scale=None):
                def evac(j, si, t0, n, ps, pk):
                    z = zrow[j % 2]
                    if func is None:
                        self.cp(z[:, t0:t0 + n], ps, [pk], [("zrow", j % 2, si)], eng=("act" if si % 2 else "dve"))
                    else:
                        self.act(z[:, t0:t0 + n], ps, func, [pk, K], [("zrow", j % 2, si)],
                                 bias=(bias[:, j:j + 1] if bias is not None else None), scale=scale)
                    if si == len(SEGS) - 1:
                        P.dma("sp", dst.ap()[j], z[:, :], [("zrow", j % 2, s_) for s_ in range(len(SEGS))], [dst.name])
                return evac

            if "dbg_xj" in self.debug:
                build_mix(0)
                dbg = self.scratch("dbg_xj", [DC, 128, NT], BF16)
                dbg2 = self.scratch("dbg_xn", [DC, 128, NT], BF16)
                P.dma("sp", dbg.ap().rearrange("c p t -> p c t"), xj[:, :, :], [("xj", c, si) for c in range(DC) for si in range(5)], ["dbg"], is_output=True)
                P.dma("sp", dbg2.ap().rearrange("c p t -> p c t"), xnT[:, :, :], [("xnT", c, si) for c in range(DC) for si in range(5)], ["dbg2"], is_output=True)
                P.fence()
                return
            with ExitStack() as esw:
                build_mix(0)
                self.proj(esw, xj, xkey, DC, d["rw_w_r"].ap(), 0, D, to_scratch(self.rS), tag="wr", wg=2)
                P.fence()
            with ExitStack() as esw:
                build_mix(2)
                self.proj(esw, xj, xkey, DC, d["rw_w_k"].ap(), 0, D, to_scratch(self.kS), tag="wk", wg=2)
                P.fence()
            with ExitStack() as esw:
                build_mix(3)
                self.proj(esw, xj, xkey, DC, d["rw_w_v"].ap(), 0, D, to_scratch(self.vS), tag="wv", wg=2)
                P.fence()
            mid = self.sb(es, "r_mid", [128, 2, NT], BF16)

            def lora(jmix, w1, r1, f1, w2, f2, bias2, dst, sc2=None, tag="l"):
                build_mix(jmix)
                kp = 128 if r1 > 128 else r1
                kt2 = (r1 + 127) // 128

                def ev1(j, si, t0, n, ps, pk):
                    mw = min(128, r1 - j * 128)
                    if f1 is None:
                        self.cp(mid[0:mw, j, t0:t0 + n], ps, [pk], [("mid", j, si)], eng="act")
                    else:
                        self.act(mid[0:mw, j, t0:t0 + n], ps, f1, [pk], [("mid", j, si)])
                with ExitStack() as esw:
                    self.proj(esw, xj, xkey, DC, w1.ap(), 0, r1, ev1, tag=tag + "1", wg=2)
                    P.fence()
                with ExitStack() as esw:
                    self.proj(esw, mid, lambda kt, si: ("mid", kt, si), kt2, w2.ap(), 0, D,
                              to_scratch(dst, f2, bias2, sc2), kparts=kp, tag=tag + "2", wg=2)
                    P.fence()

            lora(1, d["rw_w1"], 96, AF.Tanh, d["rw_w2"], AF.Sigmoid, w0c, self.dS, tag="lw")
            lora(4, d["rw_a1"], 96, None, d["rw_a2"], AF.Sigmoid, a0c, self.aS, tag="la")
            lora(5, d["rw_g1"], 256, AF.Sigmoid, d["rw_g2"], None, None, self.g2S, tag="lg")
            P.fence()

    def rw_post(self):
        P = self.P
        d = self.din
        K = "rwc2"
        with ExitStack() as es:
            F = lambda n, shp=[128, NT]: self.sb(es, n, shp, F32)
            kkc, kac, rkc = F("q_kk", [128, DC]), F("q_ka", [128, DC]), F("q_rk", [128, DC])
            for (t, nm) in ((kkc, "rw_k_k"), (kac, "rw_k_a"), (rkc, "rw_r_k")):
                self.load_cols(t[:, :], d[nm].ap().rearrange("o (c p) -> (o c) p", p=128), DC, K)
            nb = 2
            r_, k_, v_, a_, dd = ([F("q_%s%d" % (n, i)) for i in range(nb)] for n in ("r", "k", "v", "a", "d"))
            kk, t1, t2 = F("q_kkn"), F("q_t1"), F("q_t2")
            rb = [self.sb(es, "q_rb%d" % i, [128, NT], BF16) for i in range(nb)]
            av = [self.sb(es, "q_av%d" % i, [128, NT], BF16) for i in range(nb)]
            tok = [self.sb(es, "q_tok%d" % i, [128, 3, 128], BF16) for i in range(4)]
            ntok = 0
            blocks = [(128 * i, 128) for i in range(17)] + [(2176, 16)]
            for c in range(DC):
                u = c % nb
                kr, kk_, kv, ka, kd = (("q", n, u) for n in ("r", "k", "v", "a", "d"))
                P.dma("sp", r_[u][:, :], self.rS.ap()[c], ["rS"], [kr])
                P.dma("sp", k_[u][:, :], self.kS.ap()[c], ["kS"], [kk_])
                P.dma("sp", v_[u][:, :], self.vS.ap()[c], ["vS"], [kv])
                P.dma("sp", a_[u][:, :], self.aS.ap()[c], ["aS"], [ka])
                P.dma("sp", dd[u][:, :], self.dS.ap()[c], ["dS"], [kd])
                self.act(dd[u][:, :], dd[u][:, :], AF.Exp, [kd], [kd], scale=-math.exp(-0.5))
                P.dma("sp", self.dS.ap()[c], dd[u][:, :], [kd], [("dS2", c)])
                self.ts(kk[:, :], k_[u][:, :], kkc[:, c:c + 1], ALU.mult, [kk_, K], ["kk"])
                self.act(t1[:, :], kk[:, :], AF.Square, ["kk"], ["t1"])
                for si, (t0, n) in enumerate(SEGS):
                    b = self.bank()
                    self.mm(self.ps[b][:, 0:n], self.bones[:, :], t1[:, t0:t0 + n], True, True, ["t1", "bones"], [("ps", b)])
                    self.act(t2[:, t0:t0 + n], self.ps[b][:, 0:n], AF.Sqrt, [("ps", b)], ["t2"])
                self.ts(t2[:, :], t2[:, :], 1e-12, ALU.max, ["t2"], ["t2"])
                self.P.op("dve", lambda e: e.reciprocal(out=t2[:, :], in_=t2[:, :]), ["t2"], ["t2"])
                self.tt(kk[:, :], kk[:, :], t2[:, :], ALU.mult, ["kk", "t2"], ["kk"])
                self.ts(av[u][:, :], kk[:, :], -1.0, ALU.mult, ["kk"], [("q", "av", u)])
                P.dma("sp", self.avS.ap()[c], av[u][:, :], [("q", "av", u)], ["avS"])
                self.cp(rb[u][:, :], r_[u][:, :], [kr], [("q", "rb", u)], eng="act")
                P.dma("sp", self.rbS.ap()[c], rb[u][:, :], [("q", "rb", u)], ["rbS"])
                self.tt(kk[:, :], kk[:, :], a_[u][:, :], ALU.mult, ["kk", ka], ["kk"])
                self.ts(t1[:, :], a_[u][:, :], -1.0, ALU.add, [ka, K], ["t1"], s2=kac[:, c:c + 1], op1=ALU.mult)
                self.ts(t1[:, :], t1[:, :], 1.0, ALU.add, ["t1"], ["t1"])
                self.tt(k_[u][:, :], k_[u][:, :], t1[:, :], ALU.mult, [kk_, "t1"], [kk_])
                self.stt(t1[:, :], r_[u][:, :], rkc[:, c:c + 1], k_[u][:, :], ALU.mult, ALU.mult, [kr, kk_, K], ["t1"])
                for si, (t0, n) in enumerate(SEGS):
                    b = self.bank()
                    self.mm(self.ps[b][:, 0:n], self.bones[:, :], t1[:, t0:t0 + n], True, True, ["t1", "bones"], [("ps", b)])
                    self.tt(t2[:, t0:t0 + n], v_[u][:, t0:t0 + n], self.ps[b][:, 0:n], ALU.mult, [kv, ("ps", b)], ["t2"])
                P.dma("sp", self.bonS.ap()[c], t2[:, :], ["t2"], ["bonS"])
                for (t0, n) in blocks:
                    tk = tok[ntok % 4]; tkk = ("q_tok", ntok % 4); ntok += 1
                    for ti, (src, sk) in enumerate(((kk, "kk"), (k_[u], kk_), (v_[u], kv))):
                        b = self.bank()
                        self.tr(self.ps[b][0:n, 0:128], src[:, t0:t0 + n], self.ident[:, :], [sk, "ident"], [("ps", b)])
                        self.cp(tk[0:n, ti, :], self.ps[b][0:n, 0:128], [("ps", b)], [tkk], eng=("act" if ti != 1 else "dve"))
                    for ti, dst in enumerate((self.btok, self.ktok, self.vtok)):
                        P.dma("sp", dst.ap()[t0:t0 + n, c * 128:(c + 1) * 128], tk[0:n, ti, :], [tkk], [dst.name])
            P.fence()


NSL = 16


class RwScanMixin:
    def rw_scan(self, nsplit=4):
        import os
        PE_ = "dve" if os.environ.get("RW_NOPOOL") else "pool"
        P = self.P
        d = self.din
        HH = 16 // nsplit
        with ExitStack() as es:
            F = lambda n, shp: self.sb(es, n, shp, F32)
            Bf = lambda n, shp: self.sb(es, n, shp, BF16)
            ST = [F("w_ST%d" % i, [128, 16, 64]) for i in range(2)]
            Sb = [Bf("w_Sb%d" % i, [128, 16, 64]) for i in range(2)]
            tmp = [F("w_tmp%d" % i, [128, 16, 64]) for i in range(2)]
            ZV = Bf("w_ZV", [6, NSL, 1024])
            BK = Bf("w_BK", [6, NSL, 16, 128])
            AR = [Bf("w_AR%d" % i, [128, 16, 64, 4]) for i in range(2)]
            WD = [F("w_WD%d" % i, [128, 16, 64]) for i in range(2)]
            rblk = [Bf("w_rb%d" % i, [128, 16, 64]) for i in range(2)]
            ablk = [Bf("w_ab%d" % i, [128, 16, 64]) for i in range(2)]
            stin = [F("w_sti%d" % i, [64, 16, 2, 64]) for i in range(2)]
            stout = [F("w_sto%d" % i, [64, 16, 128]) for i in range(2)]
            pzb = [self.ps[h] for h in range(nsplit)]
            pub = [self.ps[4 + h] for h in range(nsplit)]
            pt = self.psbig[3]
            W = HH * 64
            self.memset(BK[:, :, :, :], 0.0, [("BK", s_) for s_ in range(NSL)], eng=PE_)
            self.memset(ZV[:, :, :], 0.0, [("ZV", s_, h) for s_ in range(NSL) for h in range(nsplit)] + [("ZVv", s_) for s_ in range(NSL)])
            hp4 = lambda ap: ap.rearrange("t (hg hp k) -> hp t hg k", hp=2, k=64)
            G = [0]
            blkc = [0]
            nst = [0]

            def chunks(lo, hi, g0):
                out = []
                a = lo
                while a < hi:
                    ga = g0 + a
                    b = min(hi, a + (8 - ga % 8))
                    out.append((a, b, ga % NSL))
                    a = b
                return out

            def run_seq(t0, L, sq, q):
                g0 = G[0]
                if q is None:
                    self.memset(ST[sq][:, :, :], 0.0, [("ST", sq, h) for h in range(nsplit)])
                    self.memset(Sb[sq][:, :, :], 0.0, [("Sb", sq, h) for h in range(nsplit)])
                else:
                    si_ = stin[q % 2]
                    P.dma("sp", si_[:, :, :, :], d["st_rw"].ap()[q].rearrange("(hg hp) v k -> v hg hp k", hp=2), (), [("sti", q % 2)])
                    for hg in range(16):
                        self.tr(pt[:, hg * 64:(hg + 1) * 64], si_[:, hg, :, :].rearrange("v hp k -> v (hp k)"), self.ident[0:64, 0:64],
                                [("sti", q % 2), "ident"], ["pt"])
                    for h in range(nsplit):
                        sl = slice(h * HH, (h + 1) * HH)
                        src = pt[:, h * W:(h + 1) * W].rearrange("p (g v) -> p g v", v=64)
                        self.cp(ST[sq][:, sl, :], src, ["pt"], [("ST", sq, h)])
                        self.cp(Sb[sq][:, sl, :], ST[sq][:, sl, :], [("ST", sq, h)], [("Sb", sq, h)], eng="act")
                import os as _os
                ch = chunks(0, L, g0)
                ch_at = {a: i for i, (a, b, s0) in enumerate(ch)}
                bi_base = blkc[0]
                nblk = (L + 1 + 63) // 64
                blkc[0] += nblk

                def build_tables(Bk):
                    j = 64 * Bk
                    bi = (bi_base + Bk) % 2
                    nb = min(64, L + 1 - j)
                    ark, wdk = ("AR", bi), ("WD", bi)
                    if Bk == 0 or Bk == nblk - 1:
                        self.memset(AR[bi][:, :, :, :], 0.0, [ark], eng="dve")
                    ga = max(j, 1)
                    nr = j + nb - ga
                    if nr > 0:
                        P.dma("act", rblk[bi][:, :, 0:nr], self.rbS.ap()[:, :, t0 + ga - 1:t0 + ga - 1 + nr].rearrange("c p t -> p c t"), (), [("rb", bi)])
                        for hp in range(2):
                            self.ts(AR[bi][:, :, ga - j:ga - j + nr, hp], rblk[bi][:, :, 0:nr], self.m01[:, hp:hp + 1], ALU.mult,
                                    [("rb", bi), "m01"], [ark], eng="dve")
                    na = min(j + nb, L) - j
                    if na > 0:
                        P.dma("act", ablk[bi][:, :, 0:na], self.avS.ap()[:, :, t0 + j:t0 + j + na].rearrange("c p t -> p c t"), (), [("ab", bi)])
                        for hp in range(2):
                            self.ts(AR[bi][:, :, 0:na, 2 + hp], ablk[bi][:, :, 0:na], self.m01[:, hp:hp + 1], ALU.mult,
                                    [("ab", bi), "m01"], [ark], eng="dve")
                        P.dma("act", WD[bi][:, :, 0:na], self.dS.ap()[:, :, t0 + j:t0 + j + na].rearrange("c p t -> p c t"), (), [wdk])

                def fill(i):
                    (a, b, s0) = ch[i]
                    n = b - a
                    ta, tb = t0 + a, t0 + b
                    wk_ = [("BK", (s0 + x) % NSL) for x in range(n)]
                    P.dma("sp", BK[2:3, s0:s0 + n, :, 0:64], hp4(self.btok.ap()[ta:tb, :])[0:1], (), wk_)
                    P.dma("sp", BK[3:4, s0:s0 + n, :, 64:128], hp4(self.btok.ap()[ta:tb, :])[1:2], (), wk_)
                    P.dma("sp", BK[4:5, s0:s0 + n, :, 0:64], hp4(self.ktok.ap()[ta:tb, :])[0:1], (), wk_)
                    P.dma("sp", BK[5:6, s0:s0 + n, :, 64:128], hp4(self.ktok.ap()[ta:tb, :])[1:2], (), wk_)
                    P.dma("sp", ZV[4:6, s0:s0 + n, :].rearrange("r s (g v) -> r s g v", v=64), hp4(self.vtok.ap()[ta:tb, :]), (),
                          [("ZVv", (s0 + x) % NSL) for x in range(n)])

                for j in range(L + 1 if not _os.environ.get("RW_SKIPGRP") else 0):
                    jj = j % 64
                    if jj == 0:
                        if j == 0:
                            build_tables(0)
                        if j // 64 + 1 < nblk:
                            build_tables(j // 64 + 1)
                    slot = (g0 + j) % NSL
                    if j in ch_at:
                        i = ch_at[j]
                        if i == 0:
                            fill(0)
                        if i + 1 < len(ch):
                            fill(i + 1)
                    bi = (bi_base + j // 64) % 2
                    for h in range(nsplit):
                        for g_ in range(HH):
                            hg = h * HH + g_
                            self.mm(pzb[h][0:4, g_ * 64:(g_ + 1) * 64], AR[bi][:, hg, jj, :], Sb[sq][:, hg, :], True, True,
                                    [("AR", bi), ("Sb", sq, h)], [("pz", h)])
                        cs = slice(h * W, (h + 1) * W)
                        self.cp(ZV[0:4, slot, cs], pzb[h][0:4, 0:W], [("pz", h)], [("ZV", slot, h)], eng="act")
                    if j < L:
                        ub = j % 2
                        for h in range(nsplit):
                            sl = slice(h * HH, (h + 1) * HH)
                            for g_ in range(HH):
                                hg = h * HH + g_
                                self.mm(pub[h][:, g_ * 64:(g_ + 1) * 64], BK[0:6, slot, hg, :], ZV[0:6, slot, hg * 64:(hg + 1) * 64], True, True,
                                        [("BK", slot), ("ZV", slot, h), ("ZVv", slot)], [("pu", h)] + (["pt"] if 4 + h >= 6 else []))
                            wb_ = WD[bi][:, sl, jj:jj + 1].to_broadcast([128, HH, 64])
                            self.tt(tmp[ub][:, sl, :], ST[sq][:, sl, :], wb_, ALU.mult, [("ST", sq, h), ("WD", bi)], [("tmp", ub, h)], eng=(PE_ if h < nsplit - 1 else "dve"))
                            self.tt(ST[sq][:, sl, :], tmp[ub][:, sl, :], pub[h][:, 0:W].rearrange("p (g v) -> p g v", v=64), ALU.add,
                                    [("tmp", ub, h), ("pu", h)], [("ST", sq, h)])
                            self.cp(Sb[sq][:, sl, :], ST[sq][:, sl, :], [("ST", sq, h)], [("Sb", sq, h)], eng="act")
                    if j >= 1 and ((g0 + j) % 8 == 7 or j == L):
                        ga = max(1, j - ((g0 + j) % 8))
                        n = j + 1 - ga
                        s0 = (g0 + ga) % NSL
                        P.dma("sp", hp4(self.ytok.ap()[t0 + ga - 1:t0 + ga - 1 + n, :]),
                              ZV[0:2, s0:s0 + n, :].rearrange("r s (g v) -> r s g v", v=64),
                              [("ZV", (s0 + x) % NSL, h) for x in range(n) for h in range(nsplit)], ["ytok"])
                G[0] = g0 + L + 1
                import os as _os
                if _os.environ.get("RW_SKIPFIN"):
                    return
                if _os.environ.get("RW_SKIPFIN_P") and q is None:
                    return
                if _os.environ.get("RW_SKIPFIN_S") and q is not None:
                    return
                so = stout[nst[0] % 2]; sok = ("sto", nst[0] % 2); nst[0] += 1
                for half in range(2):
                    for g_ in range(8):
                        hg = half * 8 + g_
                        self.tr(pt[0:64, g_ * 128:(g_ + 1) * 128], ST[sq][:, hg, :], self.ident[:, :], [("ST", sq, hg // HH), "ident"], ["pt"])
                    for bb in range(2):
                        self.cp(so[:, half * 8 + bb * 4:half * 8 + bb * 4 + 4, :],
                                pt[0:64, bb * 512:(bb + 1) * 512].rearrange("p (g k) -> p g k", k=128), ["pt"], [sok],
                                eng=("act" if bb else "dve"))
                dst = self.dout["p_rw"].ap() if q is None else self.dout["s_rw"].ap()[q]
                P.dma("sp", dst.rearrange("(hg hp) v k -> v hg hp k", hp=2), so[:, :, :].rearrange("v g (hp k) -> v g hp k", hp=2), [sok],
                      ["rw_out"], is_output=True)
                P.fence()

            import os
            lim = int(os.environ.get("RW_LIMIT", "-1"))
            if lim < 0:
                run_seq(0, TP, 0, None)
                for q in range(NSEQ):
                    run_seq(TP + 8 * q, 8, (q + 1) % 2, q)
            else:
                run_seq(0, lim, 0, None)
                for q in range(int(os.environ.get("RW_NQ", "0"))):
                    run_seq(TP + 8 * q, 8, (q + 1) % 2, q)
            P.fence()

    def rw_out(self):
        P = self.P
        d = self.din
        K = "rwo"
        with ExitStack() as es:
            F = lambda n, shp: self.sb(es, n, shp, F32)
            lnw, lnb = F("o_lnw", [128, DC]), F("o_lnb", [128, DC])
            self.load_cols(lnw[:, :], d["rw_ln_w"].ap().rearrange("o (c p) -> (o c) p", p=128), DC, K)
            self.load_cols(lnb[:, :], d["rw_ln_b"].ap().rearrange("o (c p) -> (o c) p", p=128), DC, K)
            eps2 = F("o_eps", [128, 1])
            self.memset(eps2[:, :], 64e-5, [K])
            bonesb = self.sb(es, "o_bonesb", [128, 128], BF16)
            self.cp(bonesb[:, :], self.bones[:, :], ["bones"], [K])
            xo = self.sb(es, "o_xo", [128, DC, NT], BF16)
            with ExitStack() as es2:
                F2 = lambda n, shp: self.sb(es2, n, shp, F32)
                ytk = [F2("o_ytk%d" % i, [128, D]) for i in range(2)]
                blocks = [(128 * i, 128) for i in range(17)] + [(2176, 16)]
                for bi_, (t0, n) in enumerate(blocks):
                    u = bi_ % 2
                    P.dma("pool", ytk[u][0:n, :], self.ytok.ap()[t0:t0 + n, :], (), [("ytk", u)])
                    for g in range(4):
                        b = self.bank()
                        for c4 in range(4):
                            c = g * 4 + c4
                            self.tr(self.ps[b][:, c4 * 128:c4 * 128 + n], ytk[u][0:n, c * 128:(c + 1) * 128], self.ident[0:n, 0:n],
                                    [("ytk", u), "ident"], [("ps", b)])
                        self.cp(xo[:, g * 4:g * 4 + 4, t0:t0 + n], self.ps[b][:, :].rearrange("p (c t) -> p c t", c=4)[:, :, 0:n],
                                [("ps", b)], [("xo", g * 4 + c4) for c4 in range(4)], eng=("act" if g % 2 else "dve"))
                P.fence()
            with ExitStack() as es2:
                F2 = lambda n, shp: self.sb(es2, n, shp, F32)
                bon = [F2("o_bon%d" % i, [128, NT]) for i in range(2)]
                g2 = [F2("o_g2%d" % i, [128, NT]) for i in range(2)]
                ycs = [F2("o_yc%d" % i, [128, NT]) for i in range(2)]
                sqs = [F2("o_sq%d" % i, [128, NT]) for i in range(2)]
                rss = [F2("o_rs%d" % i, [128, NT]) for i in range(2)]
                for c in range(DC):
                    u = c % 2
                    yc, sq, rs = ycs[u], sqs[u], rss[u]
                    P.dma("sp", bon[u][:, :], self.bonS.ap()[c], (), [("bon", u)])
                    P.dma("act", g2[u][:, :], self.g2S.ap()[c], (), [("g2", u)])
                    for si, (t0, n) in enumerate(SEGS):
                        b = self.bank()
                        self.mm(self.ps[b][:, 0:n], bonesb[:, :], xo[:, c, t0:t0 + n], True, True, [("xo", c), K], [("ps", b)])
                        self.stt(yc[:, t0:t0 + n], self.ps[b][:, 0:n], -1.0 / 64, xo[:, c, t0:t0 + n], ALU.mult, ALU.add, [("ps", b), ("xo", c)], [("oyc", u)])
                    self.act(sq[:, :], yc[:, :], AF.Square, [("oyc", u)], [("osq", u)])
                    for si, (t0, n) in enumerate(SEGS):
                        b = self.bank()
                        self.mm(self.ps[b][:, 0:n], self.bones[:, :], sq[:, t0:t0 + n], True, True, [("osq", u), "bones"], [("ps", b)])
                        self.act(rs[:, t0:t0 + n], self.ps[b][:, 0:n], AF.Sqrt, [("ps", b), K], [("ors", u)], bias=eps2[:, 0:1], scale=1.0 / 64)
                    self.P.op("dve", lambda e, rs=rs: e.reciprocal(out=rs[:, :], in_=rs[:, :]), [("ors", u)], [("ors", u)])
                    self.tt(yc[:, :], yc[:, :], rs[:, :], ALU.mult, [("oyc", u), ("ors", u)], [("oyc", u)])
                    self.ts(yc[:, :], yc[:, :], lnw[:, c:c + 1], ALU.mult, [("oyc", u), K], [("oyc", u)], s2=lnb[:, c:c + 1], op1=ALU.add)
                    self.tt(yc[:, :], yc[:, :], bon[u][:, :], ALU.add, [("oyc", u), ("bon", u)], [("oyc", u)])
                    self.tt(xo[:, c, :], yc[:, :], g2[u][:, :], ALU.mult, [("oyc", u), ("g2", u)], [("xo", c)])
                P.fence()
            self.out_proj_res(xo, lambda kt, si: ("xo", kt), DC, d["rw_w_o"].ap(), tag="wo2")

    def final_out(self):
        P = self.P
        lnw = self.c_ln[:, 64:80]
        tiles = [(128 * i, 128) for i in range(16)] + [(2048, 16), (2064, 128)]
        with ExitStack() as es:
            F = lambda n, shp: self.sb(es, n, shp, F32)
            hs = [F("z_hs%d" % i, [128, DC, 128]) for i in range(2)]
            sq = [F("z_sq%d" % i, [128, 128]) for i in range(2)]
            rs = [F("z_rs%d" % i, [128, 128]) for i in range(2)]
            xn = [F("z_xn%d" % i, [128, DC, 128]) for i in range(2)]
            tok = [F("z_tok%d" % i, [128, D]) for i in range(2)]
            for ti, (t0, n) in enumerate(tiles):
                u = ti % 2
                P.dma("sp", hs[u][:, :, 0:n], self.hT.ap()[:, :, t0:t0 + n].rearrange("c p t -> p c t"), (), [("zhs", u)])
                b = self.bank()
                for c in range(DC):
                    self.act(sq[c % 2][:, 0:n], hs[u][:, c, 0:n], AF.Square, [("zhs", u)], [("zsq", c % 2)])
                    self.mm(self.ps[b][:, 0:n], self.ones[:, :], sq[c % 2][:, 0:n], c == 0, c == DC - 1, [("zsq", c % 2), "ones"], [("ps", b)])
                self.act(rs[u][:, 0:n], self.ps[b][:, 0:n], AF.Sqrt, [("ps", b), "epsD"], [("zrs", u)], bias=self.epsD[:, 0:1], scale=1.0 / D)
                self.P.op("dve", lambda e, r_=rs[u], n=n: e.reciprocal(out=r_[:, 0:n], in_=r_[:, 0:n]), [("zrs", u)], [("zrs", u)])
                for c in range(DC):
                    self.stt(xn[u][:, c, 0:n], hs[u][:, c, 0:n], lnw[:, c:c + 1], rs[u][:, 0:n], ALU.mult, ALU.mult, [("zhs", u), ("zrs", u), "cols"], [("zxn", u)])
                for g in range(4):
                    b = self.bank()
                    for c4 in range(4):
                        c = g * 4 + c4
                        self.tr(self.ps[b][0:n, c4 * 128:(c4 + 1) * 128], xn[u][:, c, 0:n], self.ident[:, :], [("zxn", u), "ident"], [("ps", b)])
                    self.cp(tok[u][0:n, g * 512:(g + 1) * 512], self.ps[b][0:n, :], [("ps", b)], [("ztok", u)], eng=("act" if g % 2 else "dve"))
                if t0 == 0:
                    P.dma("sp", self.dout["y_p"].ap()[0:112, :], tok[u][16:128, :], [("ztok", u)], ["y_p"], is_output=True)
                elif t0 < TP:
                    P.dma("sp", self.dout["y_p"].ap()[t0 - 16:t0 - 16 + n, :], tok[u][0:n, :], [("ztok", u)], ["y_p"], is_output=True)
                else:
                    P.dma("sp", self.dout["y_s"].ap()[:, :], tok[u][0:n, :], [("ztok", u)], ["y_s"], is_output=True)
            P.fence()

class Builder(BuilderBase, S5Mixin, HgMixin, FfnMixin, RwMixin, RwScanMixin):
    def build(self):
        if self.upto == "scanonly":
            self.inp("st_rw", [NSEQ, 32, 64, 64])
            self.outp("p_rw", [32, 64, 64]); self.outp("s_rw", [NSEQ, 32, 64, 64])
            self.scr_in = {"rbS", "avS", "dS", "btok", "ktok", "vtok"}
            self.debug = {"ytok"}
            self.consts()
            self.rw_declare()
            self.rw_scan()
            return self.finish()
        self.declare()
        self.consts()
        self.param_cols()
        self.stage_in()
        if self.upto == "in":
            return self.finish()
        es0 = ExitStack()
        xnT = self.sb(es0, "xnT", [128, DC, NT], BF16)
        self.norm(xnT, self.c_ln[:, 0:16])
        self.l0_inproj(xnT)
        es0.close()
        if self.upto == "inproj":
            return self.finish()
        es1 = ExitStack()
        yT = self.sb(es1, "yT", [128, DC, NT], BF16)
        self.s5_setup()
        self.s5_main(yT)
        self.es_s5.close()
        self.s5_glu(yT)
        if self.upto != "s5":
            self.hg_main(yT)
        if self.upto == "hg":
            dbg = self.scratch("dbg_yb", [8, 128, NT], BF16)
            self.P.dma("sp", dbg.ap().rearrange("c p t -> p c t"), yT[:, 8:16, :], [("yT", j, si) for j in range(8, 16) for si in range(5)], ["dbg"], is_output=True)
            return self.finish()
        if self.upto not in ("s5", "hg"):
            self.out_proj_res(yT, lambda kt, si: ("yT", kt, si), DC, self.din["ev_w_out"].ap())
            es1.close()
            es3 = ExitStack()
            xnT = self.sb(es3, "xnT", [128, DC, NT], BF16)
            self.norm(xnT, self.c_ln[:, 32:48])
            self.ffn(0, xnT)
            es3.close()
            self.ffn_down(0)
        if self.upto == "l0":
            return self.finish()
        if self.upto not in ("s5", "hg"):
            self.layer1()
            return self.finish()
        if self.upto == "s5":
            dbg = self.scratch("dbg_ya", [8, 128, NT], BF16)
            self.P.dma("sp", dbg.ap().rearrange("c p t -> p c t"), yT[:, 0:8, :], [("yT", j, si) for j in range(8) for si in range(5)], ["dbg"], is_output=True)
            return self.finish()
        return self.finish()

    def layer1(self):
        self.rw_declare()
        es3 = ExitStack()
        xnT = self.sb(es3, "xnT", [128, DC, NT], BF16)
        es4 = ExitStack()
        xn32 = self.sb(es4, "xn32", [128, DC, 144], F32)
        self.norm(xnT, self.c_ln[:, 16:32], out32=xn32)
        self.rw_shift_out(xn32)
        es4.close()
        self.rw_proj(xnT)
        es3.close()
        if "dbg_xj" in self.debug:
            return
        self.rw_post()
        if self.upto == "rwpost":
            return
        self.rw_scan()
        if self.upto == "rwscan":
            return
        self.rw_out()
        if self.upto == "l1a":
            return
        es3 = ExitStack()
        xnT = self.sb(es3, "xnT", [128, DC, NT], BF16)
        self.norm(xnT, self.c_ln[:, 48:64])
        self.ffn(1, xnT)
        es3.close()
        self.ffn_down(1)
        self.final_out()

    def finish(self):
        cnt = self.P.emit()
        return self.nc, cnt


_CACHE = {}


def kernel(**inputs):
    n_cores = 8
    if "nc" not in _CACHE:
        B = Builder(upto="all")
        nc, _ = B.build()
        _CACHE["nc"] = nc
        _CACHE["names"] = list(B.din)
    nc = _CACHE["nc"]
    names = _CACHE["names"]
    shared = None
    in_maps = []
    for c in range(n_cores):
        m = _prep_inputs(inputs, c)
        in_maps.append({k: m[k] for k in names})
    res = run_bass_kernel_spmd(nc, in_maps, core_ids=list(range(n_cores))).results
    f = np.float32
    st = lambda k, cores: np.stack([np.asarray(res[c][k], dtype=f) for c in cores])
    cat = lambda k: np.concatenate([np.asarray(res[c][k], dtype=f) for c in range(n_cores)], axis=0)
    p = range(4)
    y_prompt = st("y_p", p)
    y_sample = cat("y_s").reshape(128, 8, D)
    p_s5r = st("p_s5r", p)[None]
    p_s5i = st("p_s5i", p)[None]
    p_hg = st("p_hg", p)[None]
    p_rw = st("p_rw", p)[None]
    p_sh = st("p_sh", p).reshape(1, 4, D)
    p_cv = np.stack([np.asarray(res[c]["p_cv"], dtype=f) for c in p], axis=1)
    s_s5r = cat("s_s5r")[None]
    s_s5i = cat("s_s5i")[None]
    s_hg = cat("s_hg")[None]
    s_rw = cat("s_rw")[None]
    s_sh = cat("s_sh")[None]
    s_cv = np.concatenate([np.asarray(res[c]["s_cv"], dtype=f) for c in range(n_cores)], axis=1)
    return (y_prompt, y_sample, p_s5r, p_s5i, p_hg, p_rw, p_sh, p_cv, s_s5r, s_s5i, s_hg, s_rw, s_sh, s_cv)
```
